# Optimizing a Trainium2 kernel written in Bass

```python
import jax, jax.numpy as jnp
from jax import lax
import numpy as np

D_MODEL = 1024
BATCH = 8
SEQ = 2048
DEPTH = 1

GRID_W = 64
CTX_LEN = 256
HEAD_DIM = 128
N_Q_HEADS = D_MODEL // HEAD_DIM
N_KV_HEADS = N_Q_HEADS // 4
GQA_GROUP = N_Q_HEADS // N_KV_HEADS
Q_WIDTH = N_Q_HEADS * HEAD_DIM
KV_WIDTH = N_KV_HEADS * HEAD_DIM
FOURIER_GROUP = 128
N_FOURIER_GROUPS = 4
FOURIER_WIDTH = N_FOURIER_GROUPS * FOURIER_GROUP
N_BRANCHES = 2
IN_WIDTH = Q_WIDTH + 2 * KV_WIDTH + FOURIER_WIDTH + N_BRANCHES * D_MODEL
D_FF = 2816
Q_BLOCK = 128
AXIS_ROPE_DIM = HEAD_DIM // 2
ROPE_THETA = 10000.0
EPS = 1e-6
N_MOD = 9
ATTN_SCALE = HEAD_DIM ** -0.5

kernel_name = 'hybrid_gqa_fourier_macaron_dit_layer'


def _rmsnorm(t, g):
    t32 = t.astype(jnp.float32)
    r = t32 * lax.rsqrt(jnp.mean(t32 * t32, axis=-1, keepdims=True) + EPS)
    return (r * g.astype(jnp.float32)).astype(t.dtype)


def _modulate(h, shift, scale):
    return h * (1.0 + scale) + shift


def _adaln(cond, w, b):
    return jax.nn.silu(cond) @ w + b


def _swiglu(h, w_in, w_out):
    gate, up = jnp.split(h @ w_in, 2, axis=-1)
    return (jax.nn.silu(gate) * up) @ w_out


def _axial_rope_tables(n_tokens, dtype):
    rows = n_tokens // GRID_W
    row_ids = jnp.repeat(jnp.arange(rows, dtype=jnp.float32), GRID_W)
    col_ids = jnp.tile(jnp.arange(GRID_W, dtype=jnp.float32), rows)
    inv_freq = ROPE_THETA ** (-jnp.arange(0, AXIS_ROPE_DIM, 2, dtype=jnp.float32) / AXIS_ROPE_DIM)
    ang = jnp.concatenate([row_ids[:, None] * inv_freq, col_ids[:, None] * inv_freq], axis=-1)
    return jnp.cos(ang).astype(dtype), jnp.sin(ang).astype(dtype)


def _apply_rope(t, cos, sin):
    t1, t2 = t[..., :AXIS_ROPE_DIM], t[..., AXIS_ROPE_DIM:]
    c, s = cos[None, :, None, :], sin[None, :, None, :]
    return jnp.concatenate([t1 * c - t2 * s, t2 * c + t1 * s], axis=-1)


def _heads_norm(t, n_heads, g):
    b, n = t.shape[:2]
    return _rmsnorm(t.reshape(b, n, n_heads, HEAD_DIM), g)


def _split_in(p):
    q, k, v, f, ga, gf = jnp.split(
        p,
        [Q_WIDTH, Q_WIDTH + KV_WIDTH, Q_WIDTH + 2 * KV_WIDTH,
         Q_WIDTH + 2 * KV_WIDTH + FOURIER_WIDTH,
         Q_WIDTH + 2 * KV_WIDTH + FOURIER_WIDTH + D_MODEL],
        axis=-1)
    return q, k, v, f, ga, gf


def _attend(q, k, v):
    s = jnp.einsum('bqkgd,bskd->bkgqs', q, k).astype(jnp.float32) * ATTN_SCALE
    p = jax.nn.softmax(s, axis=-1).astype(v.dtype)
    return jnp.einsum('bkgqs,bskd->bqkgd', p, v)


def _latent_attention(q, k, v, k_c, v_c):
    b, n = q.shape[:2]
    k_all = jnp.concatenate([k_c, k], axis=1)
    v_all = jnp.concatenate([v_c, v], axis=1)
    n_blk = n // Q_BLOCK
    qb = q.reshape(b, n_blk, Q_BLOCK, N_KV_HEADS, GQA_GROUP, HEAD_DIM)
    qb = jnp.moveaxis(qb, 1, 0)
    out = lax.map(lambda qblk: _attend(qblk, k_all, v_all), qb)
    return jnp.moveaxis(out, 0, 1).reshape(b, n, Q_WIDTH)


def _context_attention(q_c, k_c, v_c):
    b, n = q_c.shape[:2]
    qg = q_c.reshape(b, n, N_KV_HEADS, GQA_GROUP, HEAD_DIM)
    return _attend(qg, k_c, v_c).reshape(b, n, Q_WIDTH)


def _fourier_mix(f):
    b, n = f.shape[:2]
    fg = f.reshape(b, n, N_FOURIER_GROUPS, FOURIER_GROUP).astype(jnp.float32)
    y = jnp.fft.fft2(fg, axes=(1, 3), norm='ortho').real
    return y.reshape(b, n, FOURIER_WIDTH).astype(f.dtype)


def _merge(y_attn, y_four, ga, gf, w_ab, w_fb, w_o):
    merged = jax.nn.sigmoid(ga) * (y_attn @ w_ab) + jax.nn.sigmoid(gf) * (y_four @ w_fb)
    return merged @ w_o


def setup_inputs(seed: int = 0) -> dict:
    key = jax.random.key(seed)
    ks = jax.random.split(key, 20)
    L = DEPTH

    def w(k, shape, fan_in):
        return jax.random.normal(k, shape, jnp.float32) * fan_in ** -0.5

    def gain(k, shape):
        return 1.0 + 0.05 * jax.random.normal(k, shape, jnp.float32)

    return {
        'x': jax.random.normal(ks[0], (BATCH, SEQ, D_MODEL), jnp.float32),
        'c': jax.random.normal(ks[1], (BATCH, D_MODEL), jnp.float32),
        'ctx': jax.random.normal(ks[2], (BATCH, CTX_LEN, D_MODEL), jnp.float32),
        'c_ctx': jax.random.normal(ks[3], (D_MODEL,), jnp.float32),
        'w_ada': w(ks[4], (L, D_MODEL, N_MOD * D_MODEL), D_MODEL),
        'b_ada': 0.02 * jax.random.normal(ks[5], (L, N_MOD * D_MODEL), jnp.float32),
        'norm_ffn1': gain(ks[6], (L, D_MODEL)),
        'w_ffn1_in': w(ks[7], (L, D_MODEL, 2 * D_FF), D_MODEL),
        'w_ffn1_out': w(ks[8], (L, D_FF, D_MODEL), D_FF),
        'norm_mix': gain(ks[9], (L, D_MODEL)),
        'w_in': w(ks[10], (L, D_MODEL, IN_WIDTH), D_MODEL),
        'q_norm': gain(ks[11], (L, HEAD_DIM)),
        'k_norm': gain(ks[12], (L, HEAD_DIM)),
        'w_attn_branch': w(ks[13], (L, Q_WIDTH, D_MODEL), Q_WIDTH),
        'w_fourier_branch': w(ks[14], (L, FOURIER_WIDTH, D_MODEL), FOURIER_WIDTH),
        'w_out': w(ks[15], (L, D_MODEL, D_MODEL), D_MODEL),
        'norm_ffn2': gain(ks[16], (L, D_MODEL)),
        'w_ffn2_in': w(ks[17], (L, D_MODEL, 2 * D_FF), D_MODEL),
        'w_ffn2_out': w(ks[18], (L, D_FF, D_MODEL), D_FF),
    }


def reference(x, c, ctx, c_ctx, w_ada, b_ada, norm_ffn1, w_ffn1_in, w_ffn1_out, norm_mix,
              w_in, q_norm, k_norm, w_attn_branch, w_fourier_branch, w_out, norm_ffn2,
              w_ffn2_in, w_ffn2_out):
    n_lat = x.shape[1]
    cos, sin = _axial_rope_tables(n_lat, x.dtype)
    for i in range(DEPTH):
        last = i == DEPTH - 1
        mod = _adaln(c, w_ada[i], b_ada[i])[:, None, :]
        mod_c = _adaln(c_ctx, w_ada[i], b_ada[i])
        sh1, sc1, g1, sh2, sc2, g2, sh3, sc3, g3 = jnp.split(mod, N_MOD, axis=-1)
        csh1, csc1, cg1, csh2, csc2, cg2, csh3, csc3, cg3 = jnp.split(mod_c, N_MOD, axis=-1)

        x = x + 0.5 * g1 * _swiglu(_modulate(_rmsnorm(x, norm_ffn1[i]), sh1, sc1),
                                   w_ffn1_in[i], w_ffn1_out[i])
        ctx = ctx + 0.5 * cg1 * _swiglu(_modulate(_rmsnorm(ctx, norm_ffn1[i]), csh1, csc1),
                                        w_ffn1_in[i], w_ffn1_out[i])

        h = _modulate(_rmsnorm(x, norm_mix[i]), sh2, sc2)
        hc = _modulate(_rmsnorm(ctx, norm_mix[i]), csh2, csc2)
        q, k, v, f, ga, gf = _split_in(h @ w_in[i])
        q = _apply_rope(_heads_norm(q, N_Q_HEADS, q_norm[i]), cos, sin)
        k = _apply_rope(_heads_norm(k, N_KV_HEADS, k_norm[i]), cos, sin)
        v = v.reshape(v.shape[0], n_lat, N_KV_HEADS, HEAD_DIM)
        if last:
            k_c, v_c = jnp.split(hc @ w_in[i][:, Q_WIDTH:Q_WIDTH + 2 * KV_WIDTH], 2, axis=-1)
        else:
            q_c, k_c, v_c, f_c, ga_c, gf_c = _split_in(hc @ w_in[i])
        k_c = _heads_norm(k_c, N_KV_HEADS, k_norm[i])
        v_c = v_c.reshape(v_c.shape[0], v_c.shape[1], N_KV_HEADS, HEAD_DIM)

        y_attn = _latent_attention(q, k, v, k_c, v_c)
        mix = _merge(y_attn, _fourier_mix(f), ga, gf,
                     w_attn_branch[i], w_fourier_branch[i], w_out[i])
        if not last:
            q_c = _heads_norm(q_c, N_Q_HEADS, q_norm[i])
            mix_c = _merge(_context_attention(q_c, k_c, v_c), _fourier_mix(f_c), ga_c, gf_c,
                           w_attn_branch[i], w_fourier_branch[i], w_out[i])
            ctx = ctx + cg2 * mix_c
            ctx = ctx + 0.5 * cg3 * _swiglu(_modulate(_rmsnorm(ctx, norm_ffn2[i]), csh3, csc3),
                                            w_ffn2_in[i], w_ffn2_out[i])
        x = x + g2 * mix

        x = x + 0.5 * g3 * _swiglu(_modulate(_rmsnorm(x, norm_ffn2[i]), sh3, sc3),
                                   w_ffn2_in[i], w_ffn2_out[i])
    return x
```

```python
import os
import numpy as np
import ml_dtypes
from contextlib import ExitStack
import concourse.bass as bass
import concourse.mybir as mybir
from concourse.bass_utils import run_bass_kernel_spmd

F32 = mybir.dt.float32
BF16 = mybir.dt.bfloat16
AF = mybir.ActivationFunctionType
ALU = mybir.AluOpType

D = 1024
SEQ = 2048
CTX = 256
DFF = 2816
NCH = 8
EPS = 1e-6
ATTN_SCALE = 128 ** -0.5
NV = 128
USE_APPROX_RECIP = False
PRIO_TILE = 7
V_C, V_CC, V_BADA, V_NF1, V_NMIX, V_NF2, V_QN, V_KN = 0, 8, 16, 88, 96, 104, 112, 113
C_ONES, C_RPERM, C_CSC, C_COS, C_SIN, C_END = 0, 128, 256, 512, 2560, 4608


class Buf:
    __slots__ = ("name", "w", "r", "dsem", "dcnt")

    def __init__(self, name=""):
        self.name = name
        self.w = None
        self.r = []
        self.dsem = None
        self.dcnt = 0


class Prog:
    ENG = ("pe", "act", "dve", "pool", "sp")

    def __init__(self, nc):
        self.nc = nc
        self.q = {k: [] for k in self.ENG}
        self.cnt = {k: 0 for k in self.ENG}
        self.waited = {k: {} for k in self.ENG}
        self.semkeys = list(self.ENG)
        self.sems = {}
        self.n_dsem = 0
        self.dbufs = []

    def _collect(self, eng, reads, writes, extra):
        need = {}

        def add(tok):
            if tok is None:
                return
            k, v = tok
            if need.get(k, 0) < v:
                need[k] = v
        for b in reads:
            add(b.w)
        for b in writes:
            add(b.w)
            for t in b.r:
                add(t)
        for t in extra:
            add(t)
        out = []
        wd = self.waited[eng]
        for k, v in need.items():
            if k == eng and eng == "pe":
                continue
            if wd.get(k, 0) >= v:
                continue
            wd[k] = v
            out.append((k, v))
        return out

    def _commit(self, tok, reads, writes):
        for b in writes:
            b.w = tok
            b.r = []
        for b in reads:
            b.r.append(tok)

    def op(self, eng, fn, reads=(), writes=(), extra=()):
        waits = self._collect(eng, reads, writes, extra)
        self.cnt[eng] += 1
        tok = (eng, self.cnt[eng])
        sems = self.sems

        def run(e, waits=waits, fn=fn, eng=eng):
            for k, v in waits:
                e.wait_ge(sems[k], v)
            ins = fn(e)
            ins.then_inc(sems[eng], 1)
        self.q[eng].append(run)
        self._commit(tok, reads, writes)
        return tok

    def dma(self, eng, fn, buf, reads=(), writes=None, n=1):
        if writes is None:
            writes = (buf,)
        waits = self._collect(eng, reads, writes, ())
        if buf.dsem is None:
            buf.dsem = ("d", self.n_dsem)
            self.n_dsem += 1
            self.semkeys.append(buf.dsem)
            self.dbufs.append(buf)
        buf.dcnt += 16 * n
        tok = (buf.dsem, buf.dcnt)
        sems = self.sems

        def run(e, waits=waits, fn=fn, key=buf.dsem):
            for k, v in waits:
                e.wait_ge(sems[k], v)
            for ins in fn(e):
                ins.then_inc(sems[key], 16)
        self.q[eng].append(run)
        self._commit(tok, reads, writes)
        return tok

    def wait(self, eng, toks):
        waits = self._collect(eng, (), (), toks)
        if not waits:
            return
        sems = self.sems

        def run(e, waits=waits):
            for k, v in waits:
                e.wait_ge(sems[k], v)
        self.q[eng].append(run)

    def barrier(self):
        toks = [(k, self.cnt[k]) for k in self.ENG if self.cnt[k] > 0]
        toks += [(b.dsem, b.dcnt) for b in self.dbufs]
        for eng in self.ENG:
            self.wait(eng, toks)

    def build(self, stack):
        nc = self.nc
        for k in self.semkeys:
            nm = k if isinstance(k, str) else "d%d" % k[1]
            self.sems[k] = stack.enter_context(nc.semaphore("s_" + nm))
        block = stack.enter_context(nc.Block())
        q = self.q

        @block.tensor
        def _(e):
            for c in q["pe"]:
                c(e)

        @block.scalar
        def _(e):
            for c in q["act"]:
                c(e)

        @block.vector
        def _(e):
            for c in q["dve"]:
                c(e)

        @block.gpsimd
        def _(e):
            for c in q["pool"]:
                c(e)

        @block.sync
        def _(e):
            for c in q["sp"]:
                c(e)


class WStream:
    def __init__(self, P, slots):
        self.P = P
        self.free = list(slots)
        self.pending = []

    def request(self, src, nel=2048):
        h = {"src": src, "nel": nel, "slot": None}
        self.pending.append(h)
        self._pump()
        return h

    def _pump(self):
        while self.pending and self.free:
            h = self.pending.pop(0)
            slot = self.free.pop(0)
            h["slot"] = slot
            ap, buf = slot
            view = ap[:, 0:h["nel"]]
            h["view"] = view
            src = h["src"]
            h["tok"] = self.P.dma("pool", lambda e, view=view, src=src: [e.dma_start(out=view, in_=src, max_dma_last_dim=8192)], buf)

    def use(self, h):
        assert h["slot"] is not None, "weight tile not issued (ring too small for access order)"
        return h["view"], h["slot"][1]

    def done(self, h):
        self.free.append(h["slot"])
        self._pump()


def kv(ap, c):
    return ap.rearrange("p (k c) -> p k c", c=c)


def build_program(stage=9):
    nc = bass.Bass("TRN2", target_bir_lowering=False)
    dt = nc.dram_tensor
    x_d = dt("x", [SEQ, D], F32, kind="ExternalInput").ap()
    ctx_d = dt("ctx", [CTX, D], F32, kind="ExternalInput").ap()
    vecs_d = dt("vecs", [128, NV], F32, kind="ExternalInput").ap()
    ident_d = dt("ident", [128, 128], F32, kind="ExternalInput").ap()
    cbf_d = dt("cbf", [128, C_END], BF16, kind="ExternalInput").ap()
    tabs_d = dt("tabs", [32, 128, 1024], BF16, kind="ExternalInput").ap()
    w_ada_d = dt("w_ada", [36, 128, 2048], F32, kind="ExternalInput").ap()
    w1i_d = dt("w_ffn1_in", [22, 128, 2048], F32, kind="ExternalInput").ap()
    w1o_d = dt("w_ffn1_out", [11, 128, 2048], F32, kind="ExternalInput").ap()
    win_d = dt("w_in", [16, 128, 2048], F32, kind="ExternalInput").ap()
    wab_d = dt("w_ab", [4, 128, 2048], F32, kind="ExternalInput").ap()
    wfb_d = dt("w_fb", [4, 128, 1024], F32, kind="ExternalInput").ap()
    wo_d = dt("w_o", [4, 128, 2048], F32, kind="ExternalInput").ap()
    w2i_d = dt("w_ffn2_in", [22, 128, 2048], F32, kind="ExternalInput").ap()
    w2o_d = dt("w_ffn2_out", [11, 128, 2048], F32, kind="ExternalInput").ap()
    out_d = dt("out", [SEQ, D], F32, kind="ExternalOutput").ap()

    with ExitStack() as st:
        P = Prog(nc)
        sb = lambda name, shape, dtype: st.enter_context(nc.sbuf_tensor(name, shape, dtype))
        xT = sb("xT", [128, NCH, SEQ], F32)
        cxT = sb("cxT", [128, NCH, CTX], F32)
        ring = sb("ring", [128, 8, 2048], BF16)
        arena = sb("arena", [128, 34816], BF16)
        tmpf = sb("tmpf", [128, 8, 512], F32)
        tbf = sb("tbf", [128, 4, 512], BF16)
        cbf = sb("cbf_sb", [128, C_END], BF16)
        ident = sb("ident_sb", [128, 128], F32)
        vecs = sb("vecs_sb", [128, NV], F32)
        mods = sb("mods", [128, 2, 72], F32)
        cols = sb("cols", [128, 16, 8], F32)
        csb = sb("csb", [128, 8, 2], BF16)
        csf = sb("csf", [128, 16], F32)
        epsc = sb("epsc", [128, 1], F32)
        pt2 = sb("pt2", [128, 2, 512], BF16)
        ones_f = sb("ones_f", [128, 128], F32)
        ps = [st.enter_context(nc.psum_tensor("ps%d" % i, [128, 512], F32)) for i in range(8)]

        ps_b = [Buf("ps%d" % i) for i in range(8)]
        tmpf_b = [Buf("tmpf%d" % i) for i in range(8)]
        tbf_b = [Buf("tbf%d" % i) for i in range(4)]
        xT_b = [[Buf("xT%d_%d" % (m, t)) for t in range(4)] for m in range(NCH)]
        cxT_b = [Buf("cxT%d" % m) for m in range(NCH)]
        ring_slots = [(ring[:, i, :], Buf("ring%d" % i)) for i in range(8)]
        cbf_b, ident_b, vecs_b, csb_b, eps_b = (Buf("cbf"), Buf("ident"), Buf("vecs"), Buf("csb"), Buf("eps"))
        mods_b = [Buf("mods%d" % i) for i in range(3)]
        cols_b = [Buf("cols%d" % i) for i in range(16)]
        ones_bf = cbf[:, C_ONES:C_ONES + 128]
        rperm_bf = cbf[:, C_RPERM:C_RPERM + 128]
        csc_bf = cbf[:, C_CSC:C_CSC + 256]
        cos_bf = cbf[:, C_COS:C_COS + 2048]
        sin_bf = cbf[:, C_SIN:C_SIN + 2048]

        W = WStream(P, ring_slots)
        xring = [(arena[:, 24576 + i * 2048:24576 + (i + 1) * 2048], Buf("xring%d" % i)) for i in range(4)]
        if stage >= 2:
            W.free.extend(xring)
        lstg = [(tmpf[:, 6, :], tmpf_b[6]), (tmpf[:, 7, :], tmpf_b[7])]
        for i, off in enumerate((22528, 23552, 32768, 33792)):
            lstg.append((arena[:, off:off + 1024].bitcast(F32), Buf("lstg%d" % i)))
        lstg_n = [0]

        P.dma("sp", lambda e: [e.dma_start(out=vecs[:], in_=vecs_d)], vecs_b)
        P.dma("sp", lambda e: [e.dma_start(out=ident[:], in_=ident_d)], ident_b)
        P.dma("sp", lambda e: [e.dma_start(out=cbf[:], in_=cbf_d)], cbf_b)
        P.op("dve", lambda e: e.memset(epsc[:], EPS), writes=[eps_b])
        onesf_b = Buf("onesf")
        P.op("dve", lambda e: e.memset(ones_f[:], 1.0), writes=[onesf_b])

        ada_req = {}

        def ada_request(t):
            ada_req[t] = W.request(w_ada_d[t])

        for t in range(8):
            ada_request(t)

        def load_tile_thunks(src_rows, dst_fn, dst_bufs, banks):
            th = []
            for half in range(2):
                pb = banks[half]
                sap, sbuf_ = lstg[lstg_n[0] % len(lstg)]
                lstg_n[0] += 1

                def Dm(half=half, sap=sap, sbuf_=sbuf_):
                    P.dma("sp", lambda e: [e.dma_start(out=sap, in_=src_rows[:, half * 512:(half + 1) * 512])], sbuf_)

                def Tr(half=half, pb=pb, sap=sap, sbuf_=sbuf_):
                    def tr(e):
                        for j in range(4):
                            ins = e.transpose(ps[pb][:, j * 128:(j + 1) * 128], sap[:, j * 128:(j + 1) * 128], ident[:])
                        return ins
                    P.op("pe", tr, reads=[sbuf_, ident_b], writes=[ps_b[pb]])
                    dst = dst_fn(half)
                    if half == 0 or pb == 6:
                        P.op("act", lambda e: e.copy(out=dst, in_=ps[pb][:].rearrange("p (a b) -> p a b", b=128)),
                             reads=[ps_b[pb]], writes=dst_bufs[half * 4:half * 4 + 4])
                    else:
                        P.op("dve", lambda e: e.tensor_copy(out=dst, in_=ps[pb][:].rearrange("p (a b) -> p a b", b=128)),
                             reads=[ps_b[pb]], writes=dst_bufs[half * 4:half * 4 + 4])
                th.append((Dm, Tr))
            return th

        def load_pairs(bi, banks):
            pairs = []
            if bi == 4:
                for i in range(2):
                    pairs += load_tile_thunks(ctx_d[i * 128:(i + 1) * 128, :],
                                              lambda half, i=i: cxT[:, half * 4:half * 4 + 4, i * 128:(i + 1) * 128], cxT_b, banks)
            else:
                for i in range(bi * 4, bi * 4 + 4):
                    pairs += load_tile_thunks(x_d[i * 128:(i + 1) * 128, :],
                                              lambda half, i=i: xT[:, half * 4:half * 4 + 4, i * 128:(i + 1) * 128],
                                              [xT_b[m][bi] for m in range(NCH)], banks)
            return pairs

        LA = len(lstg)

        all_pairs = {0: load_pairs(0, (0, 1))}
        if stage >= 2:
            for bi in range(1, 5):
                all_pairs[bi] = load_pairs(bi, (6, 6))
        else:
            for bi in range(1, 5):
                all_pairs[bi] = load_pairs(bi, (0, 1))

        def load_block_thunks(bi, first_issued):
            pairs = all_pairs[bi]
            la = min(LA, len(pairs))
            th = []
            if not first_issued:
                th.append(lambda: [pairs[k][0]() for k in range(la)])
            for k in range(len(pairs)):
                if k + la < len(pairs):
                    th.append(lambda k=k: (pairs[k][1](), pairs[k + la][0]()))
                else:
                    th.append(pairs[k][1])
            if bi + 1 in all_pairs:
                nxt = all_pairs[bi + 1]

                def issue_next():
                    if bi == 0 and stage >= 2:
                        P.wait("sp", [ada_req[PRIO_TILE]["tok"]])
                    for k in range(min(LA, len(nxt))):
                        nxt[k][0]()
                th.append(issue_next)
            return th

        for th in load_block_thunks(0, False):
            th()
        if stage < 2:
            for bi in range(1, 5):
                for th in load_block_thunks(bi, True):
                    th()

        P.op("act", lambda e: e.activation(out=csf[:], in_=vecs[:, V_C:V_C + 16], func=AF.Silu), reads=[vecs_b], writes=[csb_b])
        P.op("dve", lambda e: e.tensor_copy(out=csb[:, :, 0], in_=csf[:, 0:8]), reads=[csb_b], writes=[csb_b])
        P.op("dve", lambda e: e.tensor_copy(out=csb[:, :, 1], in_=csf[:, 8:16]), reads=[csb_b], writes=[csb_b])

        def mcol(r, i):
            return mods[:, r, i * 8:(i + 1) * 8]

        def derive(dst, r, isc, vnorm, mb):
            P.op("dve", lambda e: e.scalar_tensor_tensor(out=cols[:, dst, :], in0=mcol(r, isc), scalar=1.0,
                                                          in1=vecs[:, vnorm:vnorm + 8], op0=ALU.add, op1=ALU.mult),
                 reads=[mb, vecs_b], writes=[cols_b[dst]])

        def cpy(dst, r, i, mul, mb):
            P.op("dve", lambda e: e.tensor_scalar(out=cols[:, dst, :], in0=mcol(r, i), scalar1=float(mul), scalar2=None,
                                                   op0=ALU.mult), reads=[mb], writes=[cols_b[dst]])

        def ada_tile(t):
            wv, wb = W.use(ada_req[t])
            wv3 = kv(wv, 256)

            def mm(e, t=t, wv3=wv3):
                for j in range(2):
                    ch = t * 2 + j
                    for k in range(8):
                        ins = e.matmul(ps[7][:, 2 * ch:2 * ch + 2], wv3[:, k, j * 128:(j + 1) * 128], csb[:, k, :],
                                       start=(k == 0), stop=(k == 7))
                return ins
            P.op("pe", mm, reads=[wb, csb_b], writes=[ps_b[7]])
            W.done(ada_req[t])

        ADA_PARTS = {"0a": (0, 16), "0b": (16, 24), "1": (24, 48), "2": (48, 72)}
        mods_pb = {k: Buf("mods" + k) for k in ADA_PARTS}

        def ada_finish(part):
            c0, c1 = ADA_PARTS[part]
            mb = mods_pb[part]
            for r in range(2):
                P.op("dve", lambda e, r=r: e.tensor_tensor(out=mods[:, r, c0:c1],
                                                            in0=ps[7][:, 2 * c0:2 * c1].rearrange("p (c r) -> p c r", r=2)[:, :, r],
                                                            in1=vecs[:, V_BADA + c0:V_BADA + c1], op=ALU.add),
                     reads=[ps_b[7], vecs_b], writes=[mb])
            if part == "0a":
                derive(0, 0, 1, V_NF1, mb); cpy(1, 0, 0, 1.0, mb)
                derive(9, 1, 1, V_NF1, mb); cpy(10, 1, 0, 1.0, mb)
            elif part == "0b":
                cpy(2, 0, 2, 0.5, mb); cpy(11, 1, 2, 0.5, mb)
            elif part == "1":
                derive(3, 0, 4, V_NMIX, mb); cpy(4, 0, 3, 1.0, mb); cpy(5, 0, 5, 1.0, mb)
                derive(12, 1, 4, V_NMIX, mb); cpy(13, 1, 3, 1.0, mb)
            else:
                derive(6, 0, 7, V_NF2, mb); cpy(7, 0, 6, 1.0, mb); cpy(8, 0, 8, 0.5, mb)

        def ada_consume(t):
            ada_tile(t)
            if t == 7:
                ada_finish("0a")
            if t == 11:
                ada_finish("0b")
            if t == 23:
                ada_finish("1")
            if t == 35:
                ada_finish("2")

        for t in range(8):
            ada_consume(t)
        ada_early = list(range(8, 12))
        ada_pending = list(range(12, 36))

        def mh_thunks(xsl, xbufs, ntok, ia, ish, hsl, hbufs, statbank, sq_eng="act", aff_eng="pool"):
            def A(m):
                sq = tbf[:, m % 3, 0:ntok]
                if sq_eng == "act":
                    P.op("act", lambda e: e.activation(out=sq, in_=xsl(m), func=AF.Square), reads=[xbufs[m]],
                         writes=[tbf_b[m % 3]])
                else:
                    P.op("dve", lambda e: e.tensor_tensor(out=sq, in0=xsl(m), in1=xsl(m), op=ALU.mult), reads=[xbufs[m]],
                         writes=[tbf_b[m % 3]])

            def B(m):
                sq = tbf[:, m % 3, 0:ntok]
                P.op("pe", lambda e: e.matmul(ps[statbank][:, 0:ntok], ones_bf, sq, start=(m == 0), stop=(m == 7)),
                     reads=[tbf_b[m % 3], cbf_b], writes=[ps_b[statbank]])

            def lnexp():
                P.op("act", lambda e: e.activation(out=tmpf[:, 0, 0:ntok], in_=ps[statbank][:, 0:ntok], func=AF.Ln,
                                                   bias=epsc[:, 0:1], scale=1.0 / D),
                     reads=[ps_b[statbank], eps_b], writes=[tmpf_b[0]])
                P.op("act", lambda e: e.activation(out=tmpf[:, 1, 0:ntok], in_=tmpf[:, 0, 0:ntok], func=AF.Exp, scale=-0.5),
                     reads=[tmpf_b[0]], writes=[tmpf_b[1]])

            def pair(m):
                tt = 2 + m % 2
                P.op("dve", lambda e: e.tensor_tensor(out=tmpf[:, tt, 0:ntok], in0=xsl(m), in1=tmpf[:, 1, 0:ntok], op=ALU.mult),
                     reads=[xbufs[m], tmpf_b[1]], writes=[tmpf_b[tt]])
                if aff_eng == "act":
                    P.op("act", lambda e: e.activation(out=hsl(m), in_=tmpf[:, tt, 0:ntok], func=AF.Identity,
                                                       bias=cols[:, ish, m:m + 1], scale=cols[:, ia, m:m + 1]),
                         reads=[tmpf_b[tt], cols_b[ia], cols_b[ish]], writes=[hbufs[m]])
                else:
                    P.op(aff_eng, lambda e: e.tensor_scalar(out=hsl(m), in0=tmpf[:, tt, 0:ntok], scalar1=cols[:, ia, m:m + 1],
                                                            scalar2=cols[:, ish, m:m + 1], op0=ALU.mult, op1=ALU.add),
                         reads=[tmpf_b[tt], cols_b[ia], cols_b[ish]], writes=[hbufs[m]])
            th = [lambda: A(0), lambda: A(1), lambda: A(2)]
            for m in range(NCH):
                if m + 3 < NCH:
                    th.append(lambda m=m: (B(m), A(m + 3)))
                else:
                    th.append(lambda m=m: B(m))
            th.append(lnexp)
            for m in range(NCH):
                th.append(lambda m=m: pair(m))
            return th

        def make_hT(*args):
            for th in mh_thunks(*args):
                th()

        hp_cnt = [0]

        def head_post(psrc, psrc_b, ntok, vgain, dst, dst_b, rope_off, bss, brot):
            par = hp_cnt[0] % 2
            hp_cnt[0] += 1
            qg, qg_b = tbf[:, 2 * par, 0:ntok], tbf_b[2 * par]
            sq, sq_b = tbf[:, 2 * par + 1, 0:ntok], tbf_b[2 * par + 1]
            P.op("act", lambda e: e.activation(out=qg, in_=psrc, func=AF.Identity, scale=vecs[:, vgain:vgain + 1]),
                 reads=[psrc_b, vecs_b], writes=[qg_b])
            P.op("act", lambda e: e.activation(out=sq, in_=psrc, func=AF.Square), reads=[psrc_b], writes=[sq_b])

            def pe_part():
                P.op("pe", lambda e: e.matmul(ps[bss][:, 0:ntok], ones_bf, sq, start=True, stop=True),
                     reads=[sq_b, cbf_b], writes=[ps_b[bss]])
                if rope_off is not None:
                    P.op("pe", lambda e: e.matmul(ps[brot][:, 0:ntok], rperm_bf, qg, start=True, stop=True),
                         reads=[qg_b, cbf_b], writes=[ps_b[brot]])

            def post_part():
                P.op("act", lambda e: e.activation(out=tmpf[:, 0, 0:ntok], in_=ps[bss][:, 0:ntok], func=AF.Ln,
                                                   bias=epsc[:, 0:1], scale=1.0 / 128),
                     reads=[ps_b[bss], eps_b], writes=[tmpf_b[0]])
                P.op("act", lambda e: e.activation(out=tmpf[:, 1, 0:ntok], in_=tmpf[:, 0, 0:ntok], func=AF.Exp, scale=-0.5),
                     reads=[tmpf_b[0]], writes=[tmpf_b[1]])
                if rope_off is not None:
                    P.op("pool", lambda e: e.tensor_tensor(out=tmpf[:, 2, 0:ntok], in0=qg, in1=cos_bf[:, rope_off:rope_off + ntok],
                                                            op=ALU.mult), reads=[qg_b, cbf_b], writes=[tmpf_b[2]])
                    P.op("dve", lambda e: e.tensor_tensor(out=tmpf[:, 3, 0:ntok], in0=ps[brot][:, 0:ntok],
                                                           in1=sin_bf[:, rope_off:rope_off + ntok], op=ALU.mult),
                         reads=[ps_b[brot], cbf_b], writes=[tmpf_b[3]])
                    P.op("dve", lambda e: e.tensor_tensor(out=tmpf[:, 2, 0:ntok], in0=tmpf[:, 2, 0:ntok], in1=tmpf[:, 3, 0:ntok],
                                                           op=ALU.add), reads=[tmpf_b[2], tmpf_b[3]], writes=[tmpf_b[2]])
                    P.op("dve", lambda e: e.tensor_tensor(out=dst, in0=tmpf[:, 2, 0:ntok], in1=tmpf[:, 1, 0:ntok], op=ALU.mult),
                         reads=[tmpf_b[2], tmpf_b[1]], writes=[dst_b])
                else:
                    P.op("dve", lambda e: e.tensor_tensor(out=dst, in0=qg, in1=tmpf[:, 1, 0:ntok], op=ALU.mult),
                         reads=[qg_b, tmpf_b[1]], writes=[dst_b])
            return pe_part, post_part

        def lat_block(t):
            return dict(xsl=lambda m, t=t: xT[:, m, t * 512:(t + 1) * 512], xb=[xT_b[m][t] for m in range(NCH)], ntok=512)
        ctx_block = dict(xsl=lambda m: cxT[:, m, :], xb=cxT_b, ntok=256)

        GROUPS = [(0, 2), (2, 2), (10, 1), (4, 2), (6, 2), (8, 2)]

        def ffn_requests(wi_d, wo_d, gi):
            t0, nt = GROUPS[gi]
            return ([W.request(wi_d[t0 + i]) for i in range(nt)], [W.request(wi_d[11 + t0 + i]) for i in range(nt)],
                    [W.request(wo_d[t0 + i]) for i in range(nt)])

        hT_all_v = arena[:, 0:18432].rearrange("p (k n) -> p k n", n=2304)

        def ffn_prepare(blocks):
            if "hoff" in blocks[0]:
                return
            hoff = 0
            for bi, b in enumerate(blocks):
                b["hoff"] = hoff
                b["hb"] = [Buf("hT%d_%d" % (bi, m)) for m in range(NCH)]
                hoff += b["ntok"]

        def ffn_mh_thunks(blocks, bi):
            b = blocks[bi]
            return mh_thunks(b["xsl"], b["xb"], b["ntok"], b["ia"], b["ish"],
                             lambda m, hoff=b["hoff"], n=b["ntok"]: hT_all_v[:, m, hoff:hoff + n], b["hb"], 6)

        def ffn(wi_d, wo_d, blocks, pre, with_ada=False, after_out=None, pre_block=None):
            hT_all = arena[:, 0:18432].rearrange("p (k n) -> p k n", n=2304)
            act = [arena[:, 18432 + i * 2048:18432 + (i + 1) * 2048].rearrange("p (k n) -> p k n", n=512) for i in range(2)]
            act_b = [[Buf("act%d_%d" % (i, j)) for j in range(4)] for i in range(2)]
            extra = xring
            if not with_ada:
                W.free.extend(extra)
            tiles = {0: pre}

            def ada_req4():
                got = []
                if with_ada:
                    for _ in range(4):
                        if ada_pending:
                            t = ada_pending.pop(0)
                            ada_request(t)
                            got.append(t)
                return got
            if with_ada:
                for t in ada_early:
                    ada_request(t)
            ada_now = ada_req4()
            tiles[1] = ffn_requests(wi_d, wo_d, 1)
            ffn_prepare(blocks)

            stage_b = {}

            def mh_a(bi):
                if bi >= len(blocks) or blocks[bi].get("built"):
                    stage_b[bi] = []
                    return []
                pre = pre_block(bi) if (pre_block is not None and bi > 0) else []
                th = ffn_mh_thunks(blocks, bi)
                stage_b[bi] = th[12:]
                return pre + th[:12]

            def mh_b(bi):
                return stage_b.pop(bi, [])
            side = []
            side_k = [1]

            def pop_side():
                for _ in range(side_k[0]):
                    if side:
                        side.pop(0)()
            cnt = [0]
            for gi, (t0, nt) in enumerate(GROUPS):
                hg, hu, ho = tiles[gi]
                cg = 2 * nt
                wg = [W.use(h) for h in hg]
                wu = [W.use(h) for h in hu]
                wo = [W.use(h) for h in ho]
                def ffn_in(b, par, wg=wg, wu=wu, cg=cg):
                    n, hoff = b["ntok"], b["hoff"]
                    for jj in range(cg):
                        c = cnt[0]
                        cnt[0] += 1
                        pg, pu = c % 2, 2 + c % 2
                        wgv, wg_b = wg[jj // 2]
                        wuv, wu_b = wu[jj // 2]
                        wg3, wu3 = kv(wgv, 256), kv(wuv, 256)
                        co = (jj % 2) * 128

                        def mmg(e, pg=pg, wg3=wg3, co=co):
                            for k in range(8):
                                ins = e.matmul(ps[pg][:, 0:n], wg3[:, k, co:co + 128], hT_all[:, k, hoff:hoff + n],
                                               start=(k == 0), stop=(k == 7))
                            return ins

                        def mmu(e, pu=pu, wu3=wu3, co=co):
                            for k in range(8):
                                ins = e.matmul(ps[pu][:, 0:n], wu3[:, k, co:co + 128], hT_all[:, k, hoff:hoff + n],
                                               start=(k == 0), stop=(k == 7))
                            return ins
                        P.op("pe", mmg, reads=[wg_b] + b["hb"], writes=[ps_b[pg]])
                        pop_side()
                        P.op("pe", mmu, reads=[wu_b] + b["hb"], writes=[ps_b[pu]])
                        pop_side()
                        sg = 4 + c % 2
                        P.op("act", lambda e, pg=pg, sg=sg: e.activation(out=tmpf[:, sg, 0:n], in_=ps[pg][:, 0:n], func=AF.Silu),
                             reads=[ps_b[pg]], writes=[tmpf_b[sg]])
                        P.op("dve", lambda e, pu=pu, sg=sg, jj=jj: e.tensor_tensor(out=act[par][:, jj, 0:n], in0=ps[pu][:, 0:n],
                                                                                     in1=tmpf[:, sg, 0:n], op=ALU.mult),
                             reads=[ps_b[pu], tmpf_b[sg]], writes=[act_b[par][jj]])

                def ffn_out(b, par, wo=wo, cg=cg, after_m=None):
                    n = b["ntok"]
                    for m in range(NCH):
                        po = 4 + m % 2

                        def mmo(e, m=m, po=po):
                            for jj in range(cg):
                                wo3 = kv(wo[jj // 2][0], 1024)
                                ins = e.matmul(ps[po][:, 0:n], wo3[:, jj % 2, m * 128:(m + 1) * 128], act[par][:, jj, 0:n],
                                               start=(jj == 0), stop=(jj == cg - 1))
                            return ins
                        P.op("pe", mmo, reads=[w[1] for w in wo] + act_b[par][0:cg], writes=[ps_b[po]])
                        pop_side()
                        P.op("dve", lambda e, m=m, po=po: e.scalar_tensor_tensor(out=b["xsl"](m), in0=ps[po][:, 0:n],
                                                                                  scalar=cols[:, b["ig"], m:m + 1],
                                                                                  in1=b["xsl"](m), op0=ALU.mult, op1=ALU.add),
                             reads=[ps_b[po], cols_b[b["ig"]], b["xb"][m]], writes=[b["xb"][m]])
                        if after_m is not None:
                            after_m(m)
                last = (gi == len(GROUPS) - 1)
                for bi, b in enumerate(blocks):
                    if gi == 0:
                        if bi == 0:
                            for th in mh_a(0) + mh_b(0):
                                th()
                            side.extend(mh_a(1) + mh_b(1) + mh_a(2))
                        else:
                            side.extend(mh_b(bi + 1))
                        side_k[0] = max(1, (len(side) + 6) // 7)
                    ffn_in(b, bi % 2)
                    while side:
                        side.pop(0)()
                    if gi == 0 and bi > 0:
                        side.extend(mh_a(bi + 2))
                        side_k[0] = max(1, (len(side) + 6) // 7)
                    if with_ada and gi == 0 and bi == 0:
                        while ada_early:
                            ada_consume(ada_early.pop(0))
                    if bi == min(2, len(blocks) - 1):
                        for t in ada_now:
                            ada_consume(t)
                        ada_now = []
                    if bi > 0:
                        ffn_out(blocks[bi - 1], (bi - 1) % 2)
                        while side:
                            side.pop(0)()
                        if last and after_out is not None:
                            side.extend(after_out(bi - 1))
                            side_k[0] = 1
                if last and after_out is not None:
                    fin = after_out(len(blocks) - 1)
                    fin_a, fin_b = fin[0::2], fin[1::2]

                    def fin_hook(m):
                        if m >= 3:
                            while side:
                                side.pop(0)()
                            if fin_a:
                                fin_a.pop(0)()
                    side_k[0] = 2
                    ffn_out(blocks[-1], (len(blocks) - 1) % 2, after_m=fin_hook)
                    for th in fin_a + fin_b:
                        th()
                else:
                    ffn_out(blocks[-1], (len(blocks) - 1) % 2)
                for h in hg + hu + ho:
                    W.done(h)
                ada_now = ada_req4()
                if gi + 2 < len(GROUPS):
                    tiles[gi + 2] = ffn_requests(wi_d, wo_d, gi + 2)
            for s in extra:
                W.free.remove(s)

        if stage >= 2:
            P.wait("pool", [ada_req[PRIO_TILE]["tok"]])
            pre = ffn_requests(w1i_d, w1o_d, 0)
            blocks = []
            for t in range(4):
                b = lat_block(t); b.update(ia=0, ish=1, ig=2); blocks.append(b)
            b = dict(ctx_block); b.update(ia=9, ish=10, ig=11); blocks.append(b)
            ffn(w1i_d, w1o_d, blocks, pre, with_ada=True, pre_block=lambda bi: load_block_thunks(bi, True))
        for t in ada_early + ada_pending:
            ada_request(t)
            ada_consume(t)

        kT = arena[:, 0:4608].rearrange("p (h n) -> p h n", n=2304)
        Vt = arena[:, 4608:9216].rearrange("p (s n) -> p s n", n=256)
        yfour = arena[:, 9216:17408].rearrange("p (g n) -> p g n", n=2048)
        UW = arena[:, 17408:33792].rearrange("p (g t n) -> p g t n", t=16, n=256)
        kT_b = [[Buf("kT%d_%d" % (h, t)) for t in range(5)] for h in range(2)]
        V_b = [Buf("V%d" % s) for s in range(18)]
        UW_b = [[Buf("UW%d_%d" % (g, i)) for i in range(16)] for g in range(4)]
        yf_b = [[Buf("yf%d_%d" % (g, t)) for t in range(4)] for g in range(4)]

        ffn2_pre = []
        ffn2_blocks = []
        for t in range(4):
            b = lat_block(t); b.update(ia=6, ish=7, ig=8); ffn2_blocks.append(b)

        if stage >= 3:
            p1_blocks = []
            b = dict(ctx_block); b.update(ia=12, ish=13, key0=0, kb=0, lat=None); p1_blocks.append(b)
            for t in range(4):
                b = lat_block(t); b.update(ia=3, ish=4, key0=256 + t * 512, kb=t + 1, lat=t); p1_blocks.append(b)

            def p1_req(b):
                r = [W.request(win_d[4]), W.request(win_d[5])]
                if b["lat"] is not None:
                    r += [W.request(win_d[6]), W.request(win_d[7])]
                return r
            reqs = {0: p1_req(p1_blocks[0]), 1: p1_req(p1_blocks[1])}
            P.barrier()
            h2 = [arena[:, 9216 + i * 4096:9216 + (i + 1) * 4096].rearrange("p (k n) -> p k n", n=512) for i in range(2)]
            h2_b = [[Buf("h2_%d_%d" % (i, m)) for m in range(NCH)] for i in range(2)]
            fT = [arena[:, 33792 + i * 512:33792 + (i + 1) * 512] for i in range(2)]
            fT_b = [Buf("fT0"), Buf("fT1")]
            vcnt = 0
            ucnt = 0
            def p1_mh(bi):
                b = p1_blocks[bi]
                make_hT(b["xsl"], b["xb"], b["ntok"], b["ia"], b["ish"],
                        lambda m, hh=h2[bi % 2], n=b["ntok"]: hh[:, m, 0:n], h2_b[bi % 2], 5)
            p1_mh(0)
            for bi, b in enumerate(p1_blocks):
                n = b["ntok"]
                hh, hh_b = h2[bi % 2], h2_b[bi % 2]
                if bi + 1 < len(p1_blocks):
                    p1_mh(bi + 1)
                rq = reqs[bi]
                (wk, wk_b), (wv, wv_b) = W.use(rq[0]), W.use(rq[1])
                wk3, wv3 = kv(wk, 256), kv(wv, 256)
                for kh in range(2):
                    def mmk(e, kh=kh, hh=hh, n=n, wk3=wk3):
                        for k in range(8):
                            ins = e.matmul(ps[kh][:, 0:n], wk3[:, k, kh * 128:(kh + 1) * 128], hh[:, k, 0:n],
                                           start=(k == 0), stop=(k == 7))
                        return ins
                    P.op("pe", mmk, reads=[wk_b] + hh_b, writes=[ps_b[kh]])
                rope = (b["lat"] * 512) if b["lat"] is not None else None
                posts = [head_post(ps[kh][:, 0:n], ps_b[kh], n, V_KN, kT[:, kh, b["key0"]:b["key0"] + n], kT_b[kh][b["kb"]],
                                   rope, (6, 2)[kh], (7, 3)[kh]) for kh in range(2)]
                for i in range(n // 128):
                    s = b["key0"] // 128 + i
                    pv = 2 + vcnt % 2
                    vcnt += 1

                    def mmv(e, i=i, hh=hh, wv3=wv3, pv=pv):
                        for k in range(8):
                            ins = e.matmul(ps[pv][:, 0:256], hh[:, k, i * 128:(i + 1) * 128], wv3[:, k, :],
                                           start=(k == 0), stop=(k == 7))
                        return ins
                    P.op("pe", mmv, reads=[wv_b] + hh_b, writes=[ps_b[pv]])
                    P.op("act", lambda e, s=s, pv=pv: e.copy(out=Vt[:, s, :], in_=ps[pv][:, 0:256]), reads=[ps_b[pv]],
                         writes=[V_b[s]])
                W.done(rq[0]); W.done(rq[1])
                posts[0][0](); posts[0][1]()
                if b["lat"] is None:
                    posts[1][0](); posts[1][1]()
                else:
                    t = b["lat"]
                    wf = [W.use(rq[2]), W.use(rq[3])]

                    def mmf_op(g):
                        pf = 4 + g % 2
                        wf3 = kv(wf[g // 2][0], 256)
                        co = (g % 2) * 128

                        def mmf(e, pf=pf, hh=hh, wf3=wf3, co=co):
                            for k in range(8):
                                ins = e.matmul(ps[pf][:, 0:512], wf3[:, k, co:co + 128], hh[:, k, 0:512],
                                               start=(k == 0), stop=(k == 7))
                            return ins
                        P.op("pe", mmf, reads=[wf[g // 2][1]] + hh_b, writes=[ps_b[pf]])
                    def evac_f(g):
                        pf = 4 + g % 2
                        fs = g % 2
                        P.op("act", lambda e, pf=pf, fs=fs: e.copy(out=fT[fs], in_=ps[pf][:, 0:512]), reads=[ps_b[pf]],
                             writes=[fT_b[fs]])
                    mmf_op(0)
                    mmf_op(1)
                    evac_f(0)
                    posts[1][0](); posts[1][1]()
                    mmf_op(2)
                    for g in range(4):
                        if g > 0:
                            evac_f(g)
                        if g == 2:
                            mmf_op(3)
                        fs = g % 2
                        for i2 in range(2):
                            pu = 6 + ucnt % 2
                            ucnt += 1
                            ti = i2 * 8 + t * 2

                            def mmu2(e, fs=fs, i2=i2, pu=pu):
                                fpar = fT[fs].rearrange("p (j two) -> p two j", two=2)
                                for a in range(2):
                                    ins = e.matmul(ps[pu][:, a * 256:(a + 1) * 256], fpar[:, i2, a * 128:(a + 1) * 128], csc_bf,
                                                   start=True, stop=True)
                                return ins
                            P.op("pe", mmu2, reads=[fT_b[fs], cbf_b], writes=[ps_b[pu]])
                            P.op("act", lambda e, g=g, ti=ti, pu=pu: e.copy(
                                out=UW[:, g, ti:ti + 2, :], in_=ps[pu][:].rearrange("p (a b) -> p a b", b=256)),
                                reads=[ps_b[pu]], writes=[UW_b[g][ti], UW_b[g][ti + 1]])
                    W.done(rq[2]); W.done(rq[3])
                if bi + 2 < len(p1_blocks):
                    reqs[bi + 2] = p1_req(p1_blocks[bi + 2])

            P.barrier()
            tslots = [(ring[:, i // 2, (i % 2) * 1024:(i % 2 + 1) * 1024], Buf("tab%d" % i)) for i in range(8)]
            tab_ring = [sl for sl in W.free if any(sl is r for r in ring_slots[0:4])]
            assert len(tab_ring) == 4 and len(W.free) == 8 and not W.pending
            for sl in tab_ring:
                W.free.remove(sl)
            p2req = [dict(q=None, mid=[], o=[]) for _ in range(4)]
            p2req[0]["q"] = [W.request(win_d[i]) for i in range(4)]
            for t in range(4):
                r = p2req[t]
                if t + 1 < 4:
                    p2req[t + 1]["q"] = []
                for i in range(4):
                    if t + 1 < 4:
                        p2req[t + 1]["q"].append(W.request(win_d[i]))
                    r["mid"].append((W.request(wab_d[i]), W.request(win_d[8 + i]), W.request(wfb_d[i], 1024),
                                     W.request(win_d[12 + i])))
                r["o"] = [W.request(wo_d[i]) for i in range(4)]

            for mb in range(2):
                for ti in range(16):
                    idx = mb * 16 + ti
                    tap, tb = tslots[idx % 8]
                    P.dma("sp", lambda e, tap=tap, idx=idx: [e.dma_start(out=tap, in_=tabs_d[idx])], tb)
                    bk0 = 0 if ti < 8 else 4

                    def mmy(e, ti=ti, tap=tap, bk0=bk0):
                        for g in range(4):
                            e.matmul(ps[bk0 + g][:, 0:512], UW[:, g, ti, 0:128], tap[:, 0:512], start=(ti % 8 == 0), stop=False)
                            ins = e.matmul(ps[bk0 + g][:, 0:512], UW[:, g, ti, 128:256], tap[:, 512:1024], start=False,
                                           stop=(ti % 8 == 7))
                        return ins
                    P.op("pe", mmy, reads=[tb] + [UW_b[g][ti] for g in range(4)], writes=[ps_b[bk0 + g] for g in range(4)])
                for g in range(4):
                    ta = 4 + g % 2
                    P.op("act", lambda e, g=g, ta=ta: e.activation(out=tmpf[:, ta, :], in_=ps[g][:, 0:512], func=AF.Copy,
                                                                    scale=1.0 / 512.0),
                         reads=[ps_b[g]], writes=[tmpf_b[ta]])
                    P.op("dve", lambda e, g=g, ta=ta, mb=mb: e.scalar_tensor_tensor(
                        out=yfour[:, g, mb * 512:(mb + 1) * 512], in0=ps[4 + g][:, 0:512], scalar=1.0 / 512.0, in1=tmpf[:, ta, :],
                        op0=ALU.mult, op1=ALU.add), reads=[ps_b[4 + g], tmpf_b[ta]], writes=[yf_b[g][mb]])
                    P.op("dve", lambda e, g=g, ta=ta, mb=mb: e.scalar_tensor_tensor(
                        out=yfour[:, g, 1024 + mb * 512:1024 + (mb + 1) * 512], in0=ps[4 + g][:, 0:512], scalar=-1.0 / 512.0,
                        in1=tmpf[:, ta, :], op0=ALU.mult, op1=ALU.add), reads=[ps_b[4 + g], tmpf_b[ta]], writes=[yf_b[g][2 + mb]])

            P.barrier()
            W.free.extend(tab_ring)
            W._pump()
            base = 17408
            hq = [arena[:, base:base + 4096].rearrange("p (k n) -> p k n", n=512),
                  arena[:, 30720:34816].rearrange("p (k n) -> p k n", n=512)]
            qt = [arena[:, base + 4096:base + 8192].rearrange("p (k n) -> p k n", n=512),
                  cxT[:].rearrange("p k n -> p (k n)").bitcast(BF16).rearrange("p (k n) -> p k n", n=512)]
            mg = arena[:, base + 8192:base + 12288].rearrange("p (k n) -> p k n", n=512)
            PT = [arena[:, base + 12288 + i * 512:base + 12288 + (i + 1) * 512] for i in range(2)] + [pt2[:, 0, :], pt2[:, 1, :]]
            hq_b = [[Buf("hq%d_%d" % (i, m)) for m in range(NCH)] for i in range(2)]
            qt_b = [[Buf("qt%d_%d" % (i, m)) for m in range(NCH)] for i in range(2)]
            mg_b = [Buf("mg%d" % m) for m in range(NCH)]
            PT_b = [Buf("PT%d" % i) for i in range(4)]
            dacc, dacc_b = tmpf[:, 7, :], tmpf_b[7]
            daccB, daccB_b = tmpf[:, 4, :], tmpf_b[4]
            DEN_ROLE = "PPPDPPPDPPPDPPPDPP"

            def mm8(e, pb, w3, c0, rhs, nk):
                for k in range(nk):
                    ins = e.matmul(ps[pb][:, 0:512], w3[:, k, c0:c0 + 128], rhs(k), start=(k == 0), stop=(k == nk - 1))
                return ins

            def mh_steps(t):
                b = lat_block(t)
                hh, hh_b = hq[t % 2], hq_b[t % 2]
                return mh_thunks(b["xsl"], b["xb"], 512, 3, 4, lambda m: hh[:, m, :], hh_b, 7, sq_eng="dve", aff_eng="pool")

            def q_mm(t, h):
                r = p2req[t]
                wq, wq_b = W.use(r["q"][h // 2])
                wq3 = kv(wq, 256)
                pq = 4 + h % 2
                co = (h % 2) * 128
                hh, hh_b = hq[t % 2], hq_b[t % 2]

                def mmq(e):
                    for k in range(8):
                        ins = e.matmul(ps[pq][:, 0:512], wq3[:, k, co:co + 128], hh[:, k, :], start=(k == 0), stop=(k == 7))
                    return ins
                P.op("pe", mmq, reads=[wq_b] + hh_b, writes=[ps_b[pq]])
                if h % 2 == 1:
                    W.done(r["q"][h // 2])
                return head_post(ps[pq][:, 0:512], ps_b[pq], 512, V_QN, qt[t % 2][:, h, :], qt_b[t % 2][h], t * 512, 3, 6)

            def attention(t, side):
                Q, Q_b = qt[t % 2], qt_b[t % 2]
                for h in range(8):
                    kvh = h // 4
                    po = h % 2
                    SB = [2, 4, 5]

                    def s_op(s, h=h, kvh=kvh):
                        pb = SB[s % 3]
                        kb = kT_b[kvh][0] if s < 2 else kT_b[kvh][1 + (s - 2) // 4]
                        P.op("pe", lambda e: e.matmul(ps[pb][:, 0:512], kT[:, kvh, s * 128:(s + 1) * 128], Q[:, h, :],
                                                      start=True, stop=True),
                             reads=[kb, Q_b[h]], writes=[ps_b[pb]])

                    def e_op(s):
                        pb = SB[s % 3]
                        pi = s % 4
                        P.op("act", lambda e: e.activation(out=PT[pi], in_=ps[pb][:, 0:512], func=AF.Exp, scale=ATTN_SCALE),
                             reads=[ps_b[pb]], writes=[PT_b[pi]])

                    pd = 3 if h % 2 == 0 else 6

                    def pv_op(s, kvh=kvh, po=po, pd=pd):
                        pi = s % 4
                        role = DEN_ROLE[s]
                        if role == "P":
                            def f(e):
                                e.matmul(ps[po][:, 0:512], Vt[:, s, kvh * 128:(kvh + 1) * 128], PT[pi], start=(s == 0), stop=(s == 17))
                                return e.matmul(ps[pd][:, 0:512], ones_bf, PT[pi], start=(s == 0), stop=False)
                            P.op("pe", f, reads=[V_b[s], PT_b[pi], cbf_b], writes=[ps_b[po], ps_b[pd]])
                            return
                        P.op("pe", lambda e: e.matmul(ps[po][:, 0:512], Vt[:, s, kvh * 128:(kvh + 1) * 128], PT[pi],
                                                      start=(s == 0), stop=(s == 17)),
                             reads=[V_b[s], PT_b[pi]], writes=[ps_b[po]])
                        if role == "D":
                            if s == DEN_ROLE.index("D"):
                                P.op("dve", lambda e: e.tensor_copy(out=dacc, in_=PT[pi]), reads=[PT_b[pi]], writes=[dacc_b])
                            else:
                                P.op("dve", lambda e: e.tensor_tensor(out=dacc, in0=dacc, in1=PT[pi], op=ALU.add),
                                     reads=[PT_b[pi], dacc_b], writes=[dacc_b])
                        elif s == 1:
                            pass
                        elif s == 2:
                            P.op("pool", lambda e: e.tensor_tensor(out=daccB, in0=PT[1], in1=PT[2], op=ALU.add),
                                 reads=[PT_b[1], PT_b[2]], writes=[daccB_b])
                        else:
                            P.op("pool", lambda e: e.tensor_tensor(out=daccB, in0=daccB, in1=PT[pi], op=ALU.add),
                                 reads=[PT_b[pi], daccB_b], writes=[daccB_b])
                    s_op(0); s_op(1)
                    for s in range(18):
                        e_op(s)
                        if s + 2 < 18:
                            s_op(s + 2)
                        pv_op(s)
                        if side:
                            side.pop(0)()
                    if "G" in DEN_ROLE:
                        P.op("dve", lambda e: e.tensor_tensor(out=dacc, in0=dacc, in1=daccB, op=ALU.add),
                             reads=[dacc_b, daccB_b], writes=[dacc_b])
                    P.op("pe", lambda e, pd=pd: e.matmul(ps[pd][:, 0:512], ones_f[:], dacc, start=False, stop=True),
                         reads=[dacc_b, onesf_b], writes=[ps_b[pd]])
                    if USE_APPROX_RECIP:
                        P.op("dve", lambda e, pd=pd: e.reciprocal_approx_accurate(tmpf[:, 6, :], ps[pd][:, 0:512], tmpf[:, 5, :]),
                             reads=[ps_b[pd]], writes=[tmpf_b[6], tmpf_b[5]])
                    else:
                        P.op("dve", lambda e, pd=pd: e.reciprocal(out=tmpf[:, 6, :], in_=ps[pd][:, 0:512]),
                             reads=[ps_b[pd]], writes=[tmpf_b[6]])
                    P.op("dve", lambda e, h=h, po=po: e.tensor_tensor(out=Q[:, h, :], in0=ps[po][:, 0:512], in1=tmpf[:, 6, :],
                                                                       op=ALU.mult),
                         reads=[ps_b[po], tmpf_b[6]], writes=[Q_b[h]])
                while side:
                    side.pop(0)()

            def merge_step(t, m):
                r = p2req[t]
                Y, Y_b = qt[t % 2], qt_b[t % 2]
                hh, hh_b = hq[t % 2], hq_b[t % 2]
                hab, hga, hfb, hgf = r["mid"][m // 2]
                (wab, wab_b), (wga, wga_b), (wfb, wfb_b), (wgf, wgf_b) = W.use(hab), W.use(hga), W.use(hfb), W.use(hgf)
                wab3, wga3, wfb3, wgf3 = kv(wab, 256), kv(wga, 256), kv(wfb, 256), kv(wgf, 256)
                c0 = (m % 2) * 128
                P.op("pe", lambda e: mm8(e, 0, wab3, c0, lambda k: Y[:, k, :], 8), reads=[wab_b] + Y_b, writes=[ps_b[0]])
                P.op("pe", lambda e: mm8(e, 1, wga3, c0, lambda k: hh[:, k, :], 8), reads=[wga_b] + hh_b, writes=[ps_b[1]])
                P.op("pe", lambda e: mm8(e, 2, wfb3, c0, lambda k: yfour[:, k, t * 512:(t + 1) * 512], 4),
                     reads=[wfb_b] + [yf_b[g][t] for g in range(4)], writes=[ps_b[2]])
                P.op("pe", lambda e: mm8(e, 7, wgf3, c0, lambda k: hh[:, k, :], 8), reads=[wgf_b] + hh_b, writes=[ps_b[7]])
                P.op("act", lambda e: e.activation(out=tmpf[:, 4, :], in_=ps[1][:, 0:512], func=AF.Sigmoid),
                     reads=[ps_b[1]], writes=[tmpf_b[4]])
                P.op("act", lambda e: e.activation(out=tmpf[:, 5, :], in_=ps[7][:, 0:512], func=AF.Sigmoid),
                     reads=[ps_b[7]], writes=[tmpf_b[5]])
                P.op("dve", lambda e: e.tensor_tensor(out=tmpf[:, 4, :], in0=ps[0][:, 0:512], in1=tmpf[:, 4, :], op=ALU.mult),
                     reads=[ps_b[0], tmpf_b[4]], writes=[tmpf_b[4]])
                P.op("dve", lambda e: e.tensor_tensor(out=tmpf[:, 5, :], in0=ps[2][:, 0:512], in1=tmpf[:, 5, :], op=ALU.mult),
                     reads=[ps_b[2], tmpf_b[5]], writes=[tmpf_b[5]])
                P.op("dve", lambda e: e.tensor_tensor(out=mg[:, m, :], in0=tmpf[:, 4, :], in1=tmpf[:, 5, :], op=ALU.add),
                     reads=[tmpf_b[4], tmpf_b[5]], writes=[mg_b[m]])
                if m % 2 == 1:
                    for hnd in r["mid"][m // 2]:
                        W.done(hnd)

            def outproj(t, early=()):
                r = p2req[t]
                b = lat_block(t)
                early = list(early)
                ek = (len(early) + 7) // 8
                for m in range(NCH):
                    for _ in range(ek):
                        if early:
                            early.pop(0)()
                    wo, wo_b = W.use(r["o"][m // 2])
                    wo3 = kv(wo, 256)
                    c0 = (m % 2) * 128
                    po = m % 2

                    def mmo(e, wo3=wo3, c0=c0, po=po):
                        for k in range(8):
                            ins = e.matmul(ps[po][:, 0:512], wo3[:, k, c0:c0 + 128], mg[:, k, :], start=(k == 0), stop=(k == 7))
                        return ins
                    P.op("pe", mmo, reads=[wo_b] + mg_b, writes=[ps_b[po]])
                    P.op("dve", lambda e, m=m, po=po: e.scalar_tensor_tensor(out=b["xsl"](m), in0=ps[po][:, 0:512],
                                                                              scalar=cols[:, 5, m:m + 1], in1=b["xsl"](m),
                                                                              op0=ALU.mult, op1=ALU.add),
                         reads=[ps_b[po], cols_b[5], b["xb"][m]], writes=[b["xb"][m]])
                    if m % 2 == 1:
                        W.done(r["o"][m // 2])

            for th in mh_steps(0):
                th()
            prev = None
            for h in range(8):
                parts = q_mm(0, h)
                if prev is not None:
                    prev[0](); prev[1]()
                prev = parts
            prev[0](); prev[1]()
            for t in range(4):
                nxt = t + 1 < 4
                attention(t, mh_steps(t + 1) if nxt else [])
                prev = None
                for m in range(NCH):
                    if nxt:
                        parts = q_mm(t + 1, m)
                    merge_step(t, m)
                    if nxt:
                        if prev is not None:
                            prev[0](); prev[1]()
                        prev = parts
                if nxt:
                    prev[0](); prev[1]()
                early = []
                if t == 3 and stage >= 4:
                    ffn_prepare(ffn2_blocks)
                    ffn2_pre.append(ffn_requests(w2i_d, w2o_d, 0))
                    P.wait("pool", [("pe", P.cnt["pe"])])
                    for bi in (0, 1):
                        early += ffn_mh_thunks(ffn2_blocks, bi)
                        ffn2_blocks[bi]["built"] = True
                outproj(t, early)

        st_b = [Buf("store%d" % i) for i in range(8)]

        def emit_output(t):
            stiles = [6, 7, 0, 1, 2, 3] if (t < 3 and stage >= 4) else [6, 7, 0, 1, 2, 3, 4, 5]
            ths = []
            k = 0
            for i in range(t * 4, t * 4 + 4):
                for half in range(2):
                    pb = 6 + half
                    sti = stiles[k % len(stiles)]
                    k += 1

                    def th(i=i, half=half, pb=pb, sti=sti):
                        def tr(e):
                            for j in range(4):
                                m = half * 4 + j
                                ins = e.transpose(ps[pb][:, j * 128:(j + 1) * 128], xT[:, m, i * 128:(i + 1) * 128], ident[:])
                            return ins
                        P.op("pe", tr, reads=[xT_b[m][t] for m in range(half * 4, half * 4 + 4)] + [ident_b], writes=[ps_b[pb]])
                        if half == 0:
                            P.op("act", lambda e: e.copy(out=tmpf[:, sti, :], in_=ps[pb][:]), reads=[ps_b[pb]],
                                 writes=[tmpf_b[sti]])
                        else:
                            P.op("dve", lambda e: e.tensor_copy(out=tmpf[:, sti, :], in_=ps[pb][:]), reads=[ps_b[pb]],
                                 writes=[tmpf_b[sti]])
                        P.dma("sp", lambda e: [e.dma_start(
                            out=out_d[i * 128:(i + 1) * 128, half * 512:(half + 1) * 512], in_=tmpf[:, sti, :])],
                              st_b[sti], reads=[tmpf_b[sti]], writes=[])
                    ths.append(th)
            return ths

        if stage >= 4:
            pre = ffn2_pre[0] if ffn2_pre else ffn_requests(w2i_d, w2o_d, 0)
            P.barrier()
            ffn(w2i_d, w2o_d, ffn2_blocks, pre, after_out=emit_output)
        else:
            P.barrier()
            for t in range(4):
                for th in emit_output(t):
                    th()
        P.wait("sp", [(b.dsem, b.dcnt) for b in st_b if b.dsem is not None])
        P.build(st)
    return nc


_CACHE = {}


def _consts():
    if "c" in _CACHE:
        return _CACHE["c"]
    bf = ml_dtypes.bfloat16
    ident = np.eye(128, dtype=np.float32)
    cb = np.zeros((128, C_END), dtype=np.float32)
    cb[:, C_ONES:C_ONES + 128] = 1.0
    R = np.zeros((128, 128), dtype=np.float32)
    for m in range(64):
        R[m + 64, m] = -1.0
        R[m, m + 64] = 1.0
    cb[:, C_RPERM:C_RPERM + 128] = R
    cc = np.arange(128, dtype=np.float64)
    ang = 2.0 * np.pi * np.outer(cc, cc) / 128.0
    cb[:, C_CSC:C_CSC + 128] = np.cos(ang)
    cb[:, C_CSC + 128:C_CSC + 256] = np.sin(ang)
    rows = SEQ // 64
    row_ids = np.repeat(np.arange(rows, dtype=np.float32), 64)
    col_ids = np.tile(np.arange(64, dtype=np.float32), rows)
    inv_freq = (np.float32(10000.0) ** (-np.arange(0, 64, 2, dtype=np.float32) / np.float32(64))).astype(np.float32)
    angr = np.concatenate([row_ids[:, None] * inv_freq, col_ids[:, None] * inv_freq], axis=-1).astype(np.float32)
    cosT = np.cos(angr).T
    sinT = np.sin(angr).T
    cb[0:64, C_COS:C_COS + 2048] = cosT
    cb[64:128, C_COS:C_COS + 2048] = cosT
    cb[0:64, C_SIN:C_SIN + 2048] = sinT
    cb[64:128, C_SIN:C_SIN + 2048] = sinT
    cbf = cb.astype(bf)
    tabs = np.zeros((32, 128, 1024), dtype=np.float32)
    j = np.arange(128, dtype=np.int64)
    for mb in range(2):
        m = np.arange(mb * 512, (mb + 1) * 512, dtype=np.int64)
        for ti in range(16):
            n = 2 * (128 * (ti % 8) + j) + (1 if ti >= 8 else 0)
            ang = (np.outer(n, m) % SEQ).astype(np.float64) * (2.0 * np.pi / SEQ)
            tabs[mb * 16 + ti, :, 0:512] = np.cos(ang)
            tabs[mb * 16 + ti, :, 512:1024] = -np.sin(ang)
    _CACHE["c"] = (ident, cbf, tabs.astype(bf))
    return _CACHE["c"]


def _colz(v):
    v = np.asarray(v, dtype=np.float32).reshape(-1, 128)
    return np.ascontiguousarray(v.T)


def _pack_k(w, ncols=256):
    K, N = w.shape
    kc = K // 128
    t = w.reshape(kc, 128, N // ncols, ncols)
    return np.ascontiguousarray(t.transpose(2, 1, 0, 3)).reshape(N // ncols, 128, kc * ncols)


def _pack_o(w):
    t = w.reshape(11, 2, 128, 1024)
    return np.ascontiguousarray(t.transpose(0, 2, 1, 3)).reshape(11, 128, 2048)


def kernel(x, c, ctx, c_ctx, w_ada, b_ada, norm_ffn1, w_ffn1_in, w_ffn1_out, norm_mix, w_in, q_norm, k_norm,
           w_attn_branch, w_fourier_branch, w_out, norm_ffn2, w_ffn2_in, w_ffn2_out, _stage=9, _ncores=8):
    f = lambda a: np.ascontiguousarray(np.asarray(a, dtype=np.float32))
    ident, cbf, tabs = _consts()
    key = ("nc", _stage)
    if key not in _CACHE:
        _CACHE[key] = build_program(_stage)
    nc = _CACHE[key]
    x = f(x); ctx = f(ctx); c = f(c)
    shared = dict(ident=ident, cbf=cbf, tabs=tabs, w_ada=_pack_k(f(w_ada)[0]), w_ffn1_in=_pack_k(f(w_ffn1_in)[0]),
                  w_ffn1_out=_pack_o(f(w_ffn1_out)[0]), w_in=_pack_k(f(w_in)[0]), w_ab=_pack_k(f(w_attn_branch)[0]),
                  w_fb=_pack_k(f(w_fourier_branch)[0]), w_o=_pack_k(f(w_out)[0]),
                  w_ffn2_in=_pack_k(f(w_ffn2_in)[0]), w_ffn2_out=_pack_o(f(w_ffn2_out)[0]))
    in_maps = []
    for b in range(_ncores):
        vec = np.zeros((128, NV), dtype=np.float32)
        vec[:, V_C:V_C + 8] = _colz(c[b])
        vec[:, V_CC:V_CC + 8] = _colz(f(c_ctx))
        vec[:, V_BADA:V_BADA + 72] = _colz(f(b_ada)[0])
        vec[:, V_NF1:V_NF1 + 8] = _colz(f(norm_ffn1)[0])
        vec[:, V_NMIX:V_NMIX + 8] = _colz(f(norm_mix)[0])
        vec[:, V_NF2:V_NF2 + 8] = _colz(f(norm_ffn2)[0])
        vec[:, V_QN] = f(q_norm)[0]
        vec[:, V_KN] = f(k_norm)[0]
        m = dict(shared)
        m.update(x=x[b], ctx=ctx[b], vecs=vec)
        in_maps.append(m)
    res = run_bass_kernel_spmd(nc, in_maps, core_ids=list(range(_ncores)))
    return np.stack([np.asarray(r["out"], dtype=np.float32) for r in res.results], axis=0)
```

```python
import os
import numpy as np
import ml_dtypes
from contextlib import ExitStack
import concourse.bass as bass
import concourse.mybir as mybir
from concourse.bass_utils import run_bass_kernel_spmd

F32 = mybir.dt.float32
BF16 = mybir.dt.bfloat16
AF = mybir.ActivationFunctionType
ALU = mybir.AluOpType

D = 1024
SEQ = 2048
CTX = 256
DFF = 2816
NCH = 8
EPS = 1e-6
ATTN_SCALE = 128 ** -0.5
NV = 128
USE_APPROX_RECIP = False
PRIO_TILE = 7
V_C, V_CC, V_BADA, V_NF1, V_NMIX, V_NF2, V_QN, V_KN = 0, 8, 16, 88, 96, 104, 112, 113
C_ONES, C_RPERM, C_CSC, C_COS, C_SIN, C_END = 0, 128, 256, 512, 2560, 4608


class Buf:
    __slots__ = ("name", "w", "r", "dsem", "dcnt")

    def __init__(self, name=""):
        self.name = name
        self.w = None
        self.r = []
        self.dsem = None
        self.dcnt = 0


class Prog:
    ENG = ("pe", "act", "dve", "pool", "sp")

    def __init__(self, nc):
        self.nc = nc
        self.q = {k: [] for k in self.ENG}
        self.cnt = {k: 0 for k in self.ENG}
        self.waited = {k: {} for k in self.ENG}
        self.semkeys = list(self.ENG)
        self.sems = {}
        self.n_dsem = 0
        self.dbufs = []

    def _collect(self, eng, reads, writes, extra):
        need = {}

        def add(tok):
            if tok is None:
                return
            k, v = tok
            if need.get(k, 0) < v:
                need[k] = v
        for b in reads:
            add(b.w)
        for b in writes:
            add(b.w)
            for t in b.r:
                add(t)
        for t in extra:
            add(t)
        out = []
        wd = self.waited[eng]
        for k, v in need.items():
            if k == eng and eng == "pe":
                continue
            if wd.get(k, 0) >= v:
                continue
            wd[k] = v
            out.append((k, v))
        return out

    def _commit(self, tok, reads, writes):
        for b in writes:
            b.w = tok
            b.r = []
        for b in reads:
            b.r.append(tok)

    def op(self, eng, fn, reads=(), writes=(), extra=()):
        waits = self._collect(eng, reads, writes, extra)
        self.cnt[eng] += 1
        tok = (eng, self.cnt[eng])
        sems = self.sems

        def run(e, waits=waits, fn=fn, eng=eng):
            for k, v in waits:
                e.wait_ge(sems[k], v)
            ins = fn(e)
            ins.then_inc(sems[eng], 1)
        self.q[eng].append(run)
        self._commit(tok, reads, writes)
        return tok

    def dma(self, eng, fn, buf, reads=(), writes=None, n=1):
        if writes is None:
            writes = (buf,)
        waits = self._collect(eng, reads, writes, ())
        if buf.dsem is None:
            buf.dsem = ("d", self.n_dsem)
            self.n_dsem += 1
            self.semkeys.append(buf.dsem)
            self.dbufs.append(buf)
        buf.dcnt += 16 * n
        tok = (buf.dsem, buf.dcnt)
        sems = self.sems

        def run(e, waits=waits, fn=fn, key=buf.dsem):
            for k, v in waits:
                e.wait_ge(sems[k], v)
            for ins in fn(e):
                ins.then_inc(sems[key], 16)
        self.q[eng].append(run)
        self._commit(tok, reads, writes)
        return tok

    def wait(self, eng, toks):
        waits = self._collect(eng, (), (), toks)
        if not waits:
            return
        sems = self.sems

        def run(e, waits=waits):
            for k, v in waits:
                e.wait_ge(sems[k], v)
        self.q[eng].append(run)

    def barrier(self):
        toks = [(k, self.cnt[k]) for k in self.ENG if self.cnt[k] > 0]
        toks += [(b.dsem, b.dcnt) for b in self.dbufs]
        for eng in self.ENG:
            self.wait(eng, toks)

    def build(self, stack):
        nc = self.nc
        for k in self.semkeys:
            nm = k if isinstance(k, str) else "d%d" % k[1]
            self.sems[k] = stack.enter_context(nc.semaphore("s_" + nm))
        block = stack.enter_context(nc.Block())
        q = self.q

        @block.tensor
        def _(e):
            for c in q["pe"]:
                c(e)

        @block.scalar
        def _(e):
            for c in q["act"]:
                c(e)

        @block.vector
        def _(e):
            for c in q["dve"]:
                c(e)

        @block.gpsimd
        def _(e):
            for c in q["pool"]:
                c(e)

        @block.sync
        def _(e):
            for c in q["sp"]:
                c(e)


class WStream:
    def __init__(self, P, slots):
        self.P = P
        self.free = list(slots)
        self.pending = []

    def request(self, src, nel=2048):
        h = {"src": src, "nel": nel, "slot": None}
        self.pending.append(h)
        self._pump()
        return h

    def _pump(self):
        while self.pending and self.free:
            h = self.pending.pop(0)
            slot = self.free.pop(0)
            h["slot"] = slot
            ap, buf = slot
            view = ap[:, 0:h["nel"]]
            h["view"] = view
            src = h["src"]
            h["tok"] = self.P.dma("pool", lambda e, view=view, src=src: [e.dma_start(out=view, in_=src, max_dma_last_dim=8192)], buf)

    def use(self, h):
        assert h["slot"] is not None, "weight tile not issued (ring too small for access order)"
        return h["view"], h["slot"][1]

    def done(self, h):
        self.free.append(h["slot"])
        self._pump()


def kv(ap, c):
    return ap.rearrange("p (k c) -> p k c", c=c)


def build_program(stage=9):
    nc = bass.Bass("TRN2", target_bir_lowering=False)
    dt = nc.dram_tensor
    x_d = dt("x", [SEQ, D], F32, kind="ExternalInput").ap()
    ctx_d = dt("ctx", [CTX, D], F32, kind="ExternalInput").ap()
    vecs_d = dt("vecs", [128, NV], F32, kind="ExternalInput").ap()
    ident_d = dt("ident", [128, 128], F32, kind="ExternalInput").ap()
    cbf_d = dt("cbf", [128, C_END], BF16, kind="ExternalInput").ap()
    tabs_d = dt("tabs", [32, 128, 1024], BF16, kind="ExternalInput").ap()
    w_ada_d = dt("w_ada", [36, 128, 2048], F32, kind="ExternalInput").ap()
    w1i_d = dt("w_ffn1_in", [22, 128, 2048], F32, kind="ExternalInput").ap()
    w1o_d = dt("w_ffn1_out", [11, 128, 2048], F32, kind="ExternalInput").ap()
    win_d = dt("w_in", [16, 128, 2048], F32, kind="ExternalInput").ap()
    wab_d = dt("w_ab", [4, 128, 2048], F32, kind="ExternalInput").ap()
    wfb_d = dt("w_fb", [4, 128, 1024], F32, kind="ExternalInput").ap()
    wo_d = dt("w_o", [4, 128, 2048], F32, kind="ExternalInput").ap()
    w2i_d = dt("w_ffn2_in", [22, 128, 2048], F32, kind="ExternalInput").ap()
    w2o_d = dt("w_ffn2_out", [11, 128, 2048], F32, kind="ExternalInput").ap()
    out_d = dt("out", [SEQ, D], F32, kind="ExternalOutput").ap()

    with ExitStack() as st:
        P = Prog(nc)
        sb = lambda name, shape, dtype: st.enter_context(nc.sbuf_tensor(name, shape, dtype))
        xT = sb("xT", [128, NCH, SEQ], F32)
        cxT = sb("cxT", [128, NCH, CTX], F32)
        ring = sb("ring", [128, 8, 2048], BF16)
        arena = sb("arena", [128, 34816], BF16)
        tmpf = sb("tmpf", [128, 8, 512], F32)
        tbf = sb("tbf", [128, 4, 512], BF16)
        cbf = sb("cbf_sb", [128, C_END], BF16)
        ident = sb("ident_sb", [128, 128], F32)
        vecs = sb("vecs_sb", [128, NV], F32)
        mods = sb("mods", [128, 2, 72], F32)
        cols = sb("cols", [128, 16, 8], F32)
        csb = sb("csb", [128, 8, 2], BF16)
        csf = sb("csf", [128, 16], F32)
        epsc = sb("epsc", [128, 1], F32)
        pt2 = sb("pt2", [128, 2, 512], BF16)
        ones_f = sb("ones_f", [128, 128], F32)
        ps = [st.enter_context(nc.psum_tensor("ps%d" % i, [128, 512], F32)) for i in range(8)]

        ps_b = [Buf("ps%d" % i) for i in range(8)]
        tmpf_b = [Buf("tmpf%d" % i) for i in range(8)]
        tbf_b = [Buf("tbf%d" % i) for i in range(4)]
        xT_b = [[Buf("xT%d_%d" % (m, t)) for t in range(4)] for m in range(NCH)]
        cxT_b = [Buf("cxT%d" % m) for m in range(NCH)]
        ring_slots = [(ring[:, i, :], Buf("ring%d" % i)) for i in range(8)]
        cbf_b, ident_b, vecs_b, csb_b, eps_b = (Buf("cbf"), Buf("ident"), Buf("vecs"), Buf("csb"), Buf("eps"))
        mods_b = [Buf("mods%d" % i) for i in range(3)]
        cols_b = [Buf("cols%d" % i) for i in range(16)]
        ones_bf = cbf[:, C_ONES:C_ONES + 128]
        rperm_bf = cbf[:, C_RPERM:C_RPERM + 128]
        csc_bf = cbf[:, C_CSC:C_CSC + 256]
        cos_bf = cbf[:, C_COS:C_COS + 2048]
        sin_bf = cbf[:, C_SIN:C_SIN + 2048]

        W = WStream(P, ring_slots)
        xring = [(arena[:, 24576 + i * 2048:24576 + (i + 1) * 2048], Buf("xring%d" % i)) for i in range(4)]
        if stage >= 2:
            W.free.extend(xring)
        lstg = [(tmpf[:, 6, :], tmpf_b[6]), (tmpf[:, 7, :], tmpf_b[7])]
        for i, off in enumerate((22528, 23552, 32768, 33792)):
            lstg.append((arena[:, off:off + 1024].bitcast(F32), Buf("lstg%d" % i)))
        lstg_n = [0]

        P.dma("sp", lambda e: [e.dma_start(out=vecs[:], in_=vecs_d)], vecs_b)
        P.dma("sp", lambda e: [e.dma_start(out=ident[:], in_=ident_d)], ident_b)
        P.dma("sp", lambda e: [e.dma_start(out=cbf[:], in_=cbf_d)], cbf_b)
        P.op("dve", lambda e: e.memset(epsc[:], EPS), writes=[eps_b])
        onesf_b = Buf("onesf")
        P.op("dve", lambda e: e.memset(ones_f[:], 1.0), writes=[onesf_b])

        ada_req = {}

        def ada_request(t):
            ada_req[t] = W.request(w_ada_d[t])

        for t in range(8):
            ada_request(t)

        def load_tile_thunks(src_rows, dst_fn, dst_bufs, banks):
            th = []
            for half in range(2):
                pb = banks[half]
                sap, sbuf_ = lstg[lstg_n[0] % len(lstg)]
                lstg_n[0] += 1

                def Dm(half=half, sap=sap, sbuf_=sbuf_):
                    P.dma("sp", lambda e: [e.dma_start(out=sap, in_=src_rows[:, half * 512:(half + 1) * 512])], sbuf_)

                def Tr(half=half, pb=pb, sap=sap, sbuf_=sbuf_):
                    def tr(e):
                        for j in range(4):
                            ins = e.transpose(ps[pb][:, j * 128:(j + 1) * 128], sap[:, j * 128:(j + 1) * 128], ident[:])
                        return ins
                    P.op("pe", tr, reads=[sbuf_, ident_b], writes=[ps_b[pb]])
                    dst = dst_fn(half)
                    if half == 0 or pb == 6:
                        P.op("act", lambda e: e.copy(out=dst, in_=ps[pb][:].rearrange("p (a b) -> p a b", b=128)),
                             reads=[ps_b[pb]], writes=dst_bufs[half * 4:half * 4 + 4])
                    else:
                        P.op("dve", lambda e: e.tensor_copy(out=dst, in_=ps[pb][:].rearrange("p (a b) -> p a b", b=128)),
                             reads=[ps_b[pb]], writes=dst_bufs[half * 4:half * 4 + 4])
                th.append((Dm, Tr))
            return th

        def load_pairs(bi, banks):
            pairs = []
            if bi == 4:
                for i in range(2):
                    pairs += load_tile_thunks(ctx_d[i * 128:(i + 1) * 128, :],
                                              lambda half, i=i: cxT[:, half * 4:half * 4 + 4, i * 128:(i + 1) * 128], cxT_b, banks)
            else:
                for i in range(bi * 4, bi * 4 + 4):
                    pairs += load_tile_thunks(x_d[i * 128:(i + 1) * 128, :],
                                              lambda half, i=i: xT[:, half * 4:half * 4 + 4, i * 128:(i + 1) * 128],
                                              [xT_b[m][bi] for m in range(NCH)], banks)
            return pairs

        LA = len(lstg)

        all_pairs = {0: load_pairs(0, (0, 1))}
        if stage >= 2:
            for bi in range(1, 5):
                all_pairs[bi] = load_pairs(bi, (6, 6))
        else:
            for bi in range(1, 5):
                all_pairs[bi] = load_pairs(bi, (0, 1))

        def load_block_thunks(bi, first_issued):
            pairs = all_pairs[bi]
            la = min(LA, len(pairs))
            th = []
            if not first_issued:
                th.append(lambda: [pairs[k][0]() for k in range(la)])
            for k in range(len(pairs)):
                if k + la < len(pairs):
                    th.append(lambda k=k: (pairs[k][1](), pairs[k + la][0]()))
                else:
                    th.append(pairs[k][1])
            if bi + 1 in all_pairs:
                nxt = all_pairs[bi + 1]

                def issue_next():
                    if bi == 0 and stage >= 2:
                        P.wait("sp", [ada_req[PRIO_TILE]["tok"]])
                    for k in range(min(LA, len(nxt))):
                        nxt[k][0]()
                th.append(issue_next)
            return th

        for th in load_block_thunks(0, False):
            th()
        if stage < 2:
            for bi in range(1, 5):
                for th in load_block_thunks(bi, True):
                    th()

        P.op("act", lambda e: e.activation(out=csf[:], in_=vecs[:, V_C:V_C + 16], func=AF.Silu), reads=[vecs_b], writes=[csb_b])
        P.op("dve", lambda e: e.tensor_copy(out=csb[:, :, 0], in_=csf[:, 0:8]), reads=[csb_b], writes=[csb_b])
        P.op("dve", lambda e: e.tensor_copy(out=csb[:, :, 1], in_=csf[:, 8:16]), reads=[csb_b], writes=[csb_b])

        def mcol(r, i):
            return mods[:, r, i * 8:(i + 1) * 8]

        def derive(dst, r, isc, vnorm, mb):
            P.op("dve", lambda e: e.scalar_tensor_tensor(out=cols[:, dst, :], in0=mcol(r, isc), scalar=1.0,
                                                          in1=vecs[:, vnorm:vnorm + 8], op0=ALU.add, op1=ALU.mult),
                 reads=[mb, vecs_b], writes=[cols_b[dst]])

        def cpy(dst, r, i, mul, mb):
            P.op("dve", lambda e: e.tensor_scalar(out=cols[:, dst, :], in0=mcol(r, i), scalar1=float(mul), scalar2=None,
                                                   op0=ALU.mult), reads=[mb], writes=[cols_b[dst]])

        def ada_tile(t):
            wv, wb = W.use(ada_req[t])
            wv3 = kv(wv, 256)

            def mm(e, t=t, wv3=wv3):
                for j in range(2):
                    ch = t * 2 + j
                    for k in range(8):
                        ins = e.matmul(ps[7][:, 2 * ch:2 * ch + 2], wv3[:, k, j * 128:(j + 1) * 128], csb[:, k, :],
                                       start=(k == 0), stop=(k == 7))
                return ins
            P.op("pe", mm, reads=[wb, csb_b], writes=[ps_b[7]])
            W.done(ada_req[t])

        ADA_PARTS = {"0a": (0, 16), "0b": (16, 24), "1": (24, 48), "2": (48, 72)}
        mods_pb = {k: Buf("mods" + k) for k in ADA_PARTS}

        def ada_finish(part):
            c0, c1 = ADA_PARTS[part]
            mb = mods_pb[part]
            for r in range(2):
                P.op("dve", lambda e, r=r: e.tensor_tensor(out=mods[:, r, c0:c1],
                                                            in0=ps[7][:, 2 * c0:2 * c1].rearrange("p (c r) -> p c r", r=2)[:, :, r],
                                                            in1=vecs[:, V_BADA + c0:V_BADA + c1], op=ALU.add),
                     reads=[ps_b[7], vecs_b], writes=[mb])
            if part == "0a":
                derive(0, 0, 1, V_NF1, mb); cpy(1, 0, 0, 1.0, mb)
                derive(9, 1, 1, V_NF1, mb); cpy(10, 1, 0, 1.0, mb)
            elif part == "0b":
                cpy(2, 0, 2, 0.5, mb); cpy(11, 1, 2, 0.5, mb)
            elif part == "1":
                derive(3, 0, 4, V_NMIX, mb); cpy(4, 0, 3, 1.0, mb); cpy(5, 0, 5, 1.0, mb)
                derive(12, 1, 4, V_NMIX, mb); cpy(13, 1, 3, 1.0, mb)
            else:
                derive(6, 0, 7, V_NF2, mb); cpy(7, 0, 6, 1.0, mb); cpy(8, 0, 8, 0.5, mb)

        def ada_consume(t):
            ada_tile(t)
            if t == 7:
                ada_finish("0a")
            if t == 11:
                ada_finish("0b")
            if t == 23:
                ada_finish("1")
            if t == 35:
                ada_finish("2")

        for t in range(8):
            ada_consume(t)
        ada_early = list(range(8, 12))
        ada_pending = list(range(12, 36))

        def mh_thunks(xsl, xbufs, ntok, ia, ish, hsl, hbufs, statbank, sq_eng="act", aff_eng="pool"):
            def A(m):
                sq = tbf[:, m % 3, 0:ntok]
                if sq_eng == "act":
                    P.op("act", lambda e: e.activation(out=sq, in_=xsl(m), func=AF.Square), reads=[xbufs[m]],
                         writes=[tbf_b[m % 3]])
                else:
                    P.op("dve", lambda e: e.tensor_tensor(out=sq, in0=xsl(m), in1=xsl(m), op=ALU.mult), reads=[xbufs[m]],
                         writes=[tbf_b[m % 3]])

            def B(m):
                sq = tbf[:, m % 3, 0:ntok]
                P.op("pe", lambda e: e.matmul(ps[statbank][:, 0:ntok], ones_bf, sq, start=(m == 0), stop=(m == 7)),
                     reads=[tbf_b[m % 3], cbf_b], writes=[ps_b[statbank]])

            def lnexp():
                P.op("act", lambda e: e.activation(out=tmpf[:, 0, 0:ntok], in_=ps[statbank][:, 0:ntok], func=AF.Ln,
                                                   bias=epsc[:, 0:1], scale=1.0 / D),
                     reads=[ps_b[statbank], eps_b], writes=[tmpf_b[0]])
                P.op("act", lambda e: e.activation(out=tmpf[:, 1, 0:ntok], in_=tmpf[:, 0, 0:ntok], func=AF.Exp, scale=-0.5),
                     reads=[tmpf_b[0]], writes=[tmpf_b[1]])

            def pair(m):
                tt = 2 + m % 2
                P.op("dve", lambda e: e.tensor_tensor(out=tmpf[:, tt, 0:ntok], in0=xsl(m), in1=tmpf[:, 1, 0:ntok], op=ALU.mult),
                     reads=[xbufs[m], tmpf_b[1]], writes=[tmpf_b[tt]])
                if aff_eng == "act":
                    P.op("act", lambda e: e.activation(out=hsl(m), in_=tmpf[:, tt, 0:ntok], func=AF.Identity,
                                                       bias=cols[:, ish, m:m + 1], scale=cols[:, ia, m:m + 1]),
                         reads=[tmpf_b[tt], cols_b[ia], cols_b[ish]], writes=[hbufs[m]])
                else:
                    P.op(aff_eng, lambda e: e.tensor_scalar(out=hsl(m), in0=tmpf[:, tt, 0:ntok], scalar1=cols[:, ia, m:m + 1],
                                                            scalar2=cols[:, ish, m:m + 1], op0=ALU.mult, op1=ALU.add),
                         reads=[tmpf_b[tt], cols_b[ia], cols_b[ish]], writes=[hbufs[m]])
            th = [lambda: A(0), lambda: A(1), lambda: A(2)]
            for m in range(NCH):
                if m + 3 < NCH:
                    th.append(lambda m=m: (B(m), A(m + 3)))
                else:
                    th.append(lambda m=m: B(m))
            th.append(lnexp)
            for m in range(NCH):
                th.append(lambda m=m: pair(m))
            return th

        def make_hT(*args, **kw):
            for th in mh_thunks(*args, **kw):
                th()

        hp_cnt = [0]

        def head_post(psrc, psrc_b, ntok, vgain, dst, dst_b, rope_off, bss, brot):
            par = hp_cnt[0] % 2
            hp_cnt[0] += 1
            qg, qg_b = tbf[:, 2 * par, 0:ntok], tbf_b[2 * par]
            sq, sq_b = tbf[:, 2 * par + 1, 0:ntok], tbf_b[2 * par + 1]
            P.op("act", lambda e: e.activation(out=qg, in_=psrc, func=AF.Identity, scale=vecs[:, vgain:vgain + 1]),
                 reads=[psrc_b, vecs_b], writes=[qg_b])
            P.op("act", lambda e: e.activation(out=sq, in_=psrc, func=AF.Square), reads=[psrc_b], writes=[sq_b])

            def pe_part():
                P.op("pe", lambda e: e.matmul(ps[bss][:, 0:ntok], ones_bf, sq, start=True, stop=True),
                     reads=[sq_b, cbf_b], writes=[ps_b[bss]])
                if rope_off is not None:
                    P.op("pe", lambda e: e.matmul(ps[brot][:, 0:ntok], rperm_bf, qg, start=True, stop=True),
                         reads=[qg_b, cbf_b], writes=[ps_b[brot]])

            def post_part():
                P.op("act", lambda e: e.activation(out=tmpf[:, 0, 0:ntok], in_=ps[bss][:, 0:ntok], func=AF.Ln,
                                                   bias=epsc[:, 0:1], scale=1.0 / 128),
                     reads=[ps_b[bss], eps_b], writes=[tmpf_b[0]])
                P.op("act", lambda e: e.activation(out=tmpf[:, 1, 0:ntok], in_=tmpf[:, 0, 0:ntok], func=AF.Exp, scale=-0.5),
                     reads=[tmpf_b[0]], writes=[tmpf_b[1]])
                if rope_off is not None:
                    P.op("pool", lambda e: e.tensor_tensor(out=tmpf[:, 2, 0:ntok], in0=qg, in1=cos_bf[:, rope_off:rope_off + ntok],
                                                            op=ALU.mult), reads=[qg_b, cbf_b], writes=[tmpf_b[2]])
                    P.op("dve", lambda e: e.tensor_tensor(out=tmpf[:, 3, 0:ntok], in0=ps[brot][:, 0:ntok],
                                                           in1=sin_bf[:, rope_off:rope_off + ntok], op=ALU.mult),
                         reads=[ps_b[brot], cbf_b], writes=[tmpf_b[3]])
                    P.op("dve", lambda e: e.tensor_tensor(out=tmpf[:, 2, 0:ntok], in0=tmpf[:, 2, 0:ntok], in1=tmpf[:, 3, 0:ntok],
                                                           op=ALU.add), reads=[tmpf_b[2], tmpf_b[3]], writes=[tmpf_b[2]])
                    P.op("dve", lambda e: e.tensor_tensor(out=dst, in0=tmpf[:, 2, 0:ntok], in1=tmpf[:, 1, 0:ntok], op=ALU.mult),
                         reads=[tmpf_b[2], tmpf_b[1]], writes=[dst_b])
                else:
                    P.op("dve", lambda e: e.tensor_tensor(out=dst, in0=qg, in1=tmpf[:, 1, 0:ntok], op=ALU.mult),
                         reads=[qg_b, tmpf_b[1]], writes=[dst_b])
            return pe_part, post_part

        def lat_block(t):
            return dict(xsl=lambda m, t=t: xT[:, m, t * 512:(t + 1) * 512], xb=[xT_b[m][t] for m in range(NCH)], ntok=512)
        ctx_block = dict(xsl=lambda m: cxT[:, m, :], xb=cxT_b, ntok=256)

        GROUPS = [(0, 2), (2, 2), (10, 1), (4, 2), (6, 2), (8, 2)]

        def ffn_requests(wi_d, wo_d, gi):
            t0, nt = GROUPS[gi]
            return ([W.request(wi_d[t0 + i]) for i in range(nt)], [W.request(wi_d[11 + t0 + i]) for i in range(nt)],
                    [W.request(wo_d[t0 + i]) for i in range(nt)])

        hT_all_v = arena[:, 0:18432].rearrange("p (k n) -> p k n", n=2304)

        def ffn_prepare(blocks):
            if "hoff" in blocks[0]:
                return
            hoff = 0
            for bi, b in enumerate(blocks):
                b["hoff"] = hoff
                b["hb"] = [Buf("hT%d_%d" % (bi, m)) for m in range(NCH)]
                hoff += b["ntok"]

        def ffn_mh_thunks(blocks, bi, aff_eng="pool"):
            b = blocks[bi]
            return mh_thunks(b["xsl"], b["xb"], b["ntok"], b["ia"], b["ish"],
                             lambda m, hoff=b["hoff"], n=b["ntok"]: hT_all_v[:, m, hoff:hoff + n], b["hb"], 6, aff_eng=aff_eng)

        def ffn(wi_d, wo_d, blocks, pre, with_ada=False, after_out=None, pre_block=None):
            hT_all = arena[:, 0:18432].rearrange("p (k n) -> p k n", n=2304)
            act = [arena[:, 18432 + i * 2048:18432 + (i + 1) * 2048].rearrange("p (k n) -> p k n", n=512) for i in range(2)]
            act_b = [[Buf("act%d_%d" % (i, j)) for j in range(4)] for i in range(2)]
            extra = xring
            if not with_ada:
                W.free.extend(extra)
            tiles = {0: pre}

            def ada_req4():
                got = []
                if with_ada:
                    for _ in range(4):
                        if ada_pending:
                            t = ada_pending.pop(0)
                            ada_request(t)
                            got.append(t)
                return got
            if with_ada:
                for t in ada_early:
                    ada_request(t)
            ada_now = ada_req4()
            tiles[1] = ffn_requests(wi_d, wo_d, 1)
            ffn_prepare(blocks)

            stage_b = {}

            def mh_a(bi):
                if bi >= len(blocks) or blocks[bi].get("built"):
                    stage_b[bi] = []
                    return []
                pre = pre_block(bi) if (pre_block is not None and bi > 0) else []
                th = ffn_mh_thunks(blocks, bi, aff_eng=("act" if bi == 0 else "pool"))
                stage_b[bi] = th[12:]
                return pre + th[:12]

            def mh_b(bi):
                return stage_b.pop(bi, [])
            side = []
            side_k = [1]

            def pop_side():
                for _ in range(side_k[0]):
                    if side:
                        side.pop(0)()
            cnt = [0]
            for gi, (t0, nt) in enumerate(GROUPS):
                hg, hu, ho = tiles[gi]
                cg = 2 * nt
                wg = [W.use(h) for h in hg]
                wu = [W.use(h) for h in hu]
                wo = [W.use(h) for h in ho]
                def ffn_in(b, par, wg=wg, wu=wu, cg=cg):
                    n, hoff = b["ntok"], b["hoff"]
                    for jj in range(cg):
                        c = cnt[0]
                        cnt[0] += 1
                        pg, pu = c % 2, 2 + c % 2
                        wgv, wg_b = wg[jj // 2]
                        wuv, wu_b = wu[jj // 2]
                        wg3, wu3 = kv(wgv, 256), kv(wuv, 256)
                        co = (jj % 2) * 128

                        def mmg(e, pg=pg, wg3=wg3, co=co):
                            for k in range(8):
                                ins = e.matmul(ps[pg][:, 0:n], wg3[:, k, co:co + 128], hT_all[:, k, hoff:hoff + n],
                                               start=(k == 0), stop=(k == 7))
                            return ins

                        def mmu(e, pu=pu, wu3=wu3, co=co):
                            for k in range(8):
                                ins = e.matmul(ps[pu][:, 0:n], wu3[:, k, co:co + 128], hT_all[:, k, hoff:hoff + n],
                                               start=(k == 0), stop=(k == 7))
                            return ins
                        P.op("pe", mmg, reads=[wg_b] + b["hb"], writes=[ps_b[pg]])
                        pop_side()
                        P.op("pe", mmu, reads=[wu_b] + b["hb"], writes=[ps_b[pu]])
                        pop_side()
                        sg = 4 + c % 2
                        P.op("act", lambda e, pg=pg, sg=sg: e.activation(out=tmpf[:, sg, 0:n], in_=ps[pg][:, 0:n], func=AF.Silu),
                             reads=[ps_b[pg]], writes=[tmpf_b[sg]])
                        P.op("dve", lambda e, pu=pu, sg=sg, jj=jj: e.tensor_tensor(out=act[par][:, jj, 0:n], in0=ps[pu][:, 0:n],
                                                                                     in1=tmpf[:, sg, 0:n], op=ALU.mult),
                             reads=[ps_b[pu], tmpf_b[sg]], writes=[act_b[par][jj]])

                def ffn_out(b, par, wo=wo, cg=cg, after_m=None):
                    n = b["ntok"]
                    for m in range(NCH):
                        po = 4 + m % 2

                        def mmo(e, m=m, po=po):
                            for jj in range(cg):
                                wo3 = kv(wo[jj // 2][0], 1024)
                                ins = e.matmul(ps[po][:, 0:n], wo3[:, jj % 2, m * 128:(m + 1) * 128], act[par][:, jj, 0:n],
                                               start=(jj == 0), stop=(jj == cg - 1))
                            return ins
                        P.op("pe", mmo, reads=[w[1] for w in wo] + act_b[par][0:cg], writes=[ps_b[po]])
                        pop_side()
                        P.op("dve", lambda e, m=m, po=po: e.scalar_tensor_tensor(out=b["xsl"](m), in0=ps[po][:, 0:n],
                                                                                  scalar=cols[:, b["ig"], m:m + 1],
                                                                                  in1=b["xsl"](m), op0=ALU.mult, op1=ALU.add),
                             reads=[ps_b[po], cols_b[b["ig"]], b["xb"][m]], writes=[b["xb"][m]])
                        if after_m is not None:
                            after_m(m)
                last = (gi == len(GROUPS) - 1)
                for bi, b in enumerate(blocks):
                    if gi == 0:
                        if bi == 0:
                            for th in mh_a(0) + mh_b(0):
                                th()
                            side.extend(mh_a(1) + mh_b(1) + mh_a(2))
                        else:
                            side.extend(mh_b(bi + 1))
                        side_k[0] = max(1, (len(side) + 6) // 7)
                    ffn_in(b, bi % 2)
                    while side:
                        side.pop(0)()
                    if gi == 0 and bi > 0:
                        side.extend(mh_a(bi + 2))
                        side_k[0] = max(1, (len(side) + 6) // 7)
                    if with_ada and gi == 0 and bi == 0:
                        while ada_early:
                            ada_consume(ada_early.pop(0))
                    if bi == min(2, len(blocks) - 1):
                        for t in ada_now:
                            ada_consume(t)
                        ada_now = []
                    if bi > 0:
                        ffn_out(blocks[bi - 1], (bi - 1) % 2)
                        while side:
                            side.pop(0)()
                        if last and after_out is not None:
                            side.extend(after_out(bi - 1))
                            side_k[0] = 1
                if last and after_out is not None:
                    fin = after_out(len(blocks) - 1)
                    fin_a, fin_b = fin[0::2], fin[1::2]

                    def fin_hook(m):
                        if m >= 3:
                            while side:
                                side.pop(0)()
                            if fin_a:
                                fin_a.pop(0)()
                    side_k[0] = 2
                    ffn_out(blocks[-1], (len(blocks) - 1) % 2, after_m=fin_hook)
                    for th in fin_a + fin_b:
                        th()
                else:
                    ffn_out(blocks[-1], (len(blocks) - 1) % 2)
                for h in hg + hu + ho:
                    W.done(h)
                ada_now = ada_req4()
                if gi + 2 < len(GROUPS):
                    tiles[gi + 2] = ffn_requests(wi_d, wo_d, gi + 2)
            for s in extra:
                W.free.remove(s)

        if stage >= 2:
            P.wait("pool", [ada_req[PRIO_TILE]["tok"]])
            pre = ffn_requests(w1i_d, w1o_d, 0)
            blocks = []
            for t in range(4):
                b = lat_block(t); b.update(ia=0, ish=1, ig=2); blocks.append(b)
            b = dict(ctx_block); b.update(ia=9, ish=10, ig=11); blocks.append(b)
            ffn(w1i_d, w1o_d, blocks, pre, with_ada=True, pre_block=lambda bi: load_block_thunks(bi, True))
        for t in ada_early + ada_pending:
            ada_request(t)
            ada_consume(t)

        kT = arena[:, 0:4608].rearrange("p (h n) -> p h n", n=2304)
        Vt = arena[:, 4608:9216].rearrange("p (s n) -> p s n", n=256)
        yfour = arena[:, 9216:17408].rearrange("p (g n) -> p g n", n=2048)
        UW = arena[:, 17408:33792].rearrange("p (g t n) -> p g t n", t=16, n=256)
        kT_b = [[Buf("kT%d_%d" % (h, t)) for t in range(5)] for h in range(2)]
        V_b = [Buf("V%d" % s) for s in range(18)]
        UW_b = [[Buf("UW%d_%d" % (g, i)) for i in range(16)] for g in range(4)]
        yf_b = [[Buf("yf%d_%d" % (g, t)) for t in range(4)] for g in range(4)]

        ffn2_pre = []
        ffn2_blocks = []
        for t in range(4):
            b = lat_block(t); b.update(ia=6, ish=7, ig=8); ffn2_blocks.append(b)

        if stage >= 3:
            p1_blocks = []
            b = dict(ctx_block); b.update(ia=12, ish=13, key0=0, kb=0, lat=None); p1_blocks.append(b)
            for t in range(4):
                b = lat_block(t); b.update(ia=3, ish=4, key0=256 + t * 512, kb=t + 1, lat=t); p1_blocks.append(b)

            def p1_req(b):
                r = [W.request(win_d[4]), W.request(win_d[5])]
                if b["lat"] is not None:
                    r += [W.request(win_d[6]), W.request(win_d[7])]
                return r
            reqs = {0: p1_req(p1_blocks[0]), 1: p1_req(p1_blocks[1])}
            P.barrier()
            h2 = [arena[:, 9216 + i * 4096:9216 + (i + 1) * 4096].rearrange("p (k n) -> p k n", n=512) for i in range(2)]
            h2_b = [[Buf("h2_%d_%d" % (i, m)) for m in range(NCH)] for i in range(2)]
            fT = [arena[:, 33792 + i * 512:33792 + (i + 1) * 512] for i in range(2)]
            fT_b = [Buf("fT0"), Buf("fT1")]
            vcnt = 0
            ucnt = 0
            def p1_mh(bi):
                b = p1_blocks[bi]
                make_hT(b["xsl"], b["xb"], b["ntok"], b["ia"], b["ish"],
                        lambda m, hh=h2[bi % 2], n=b["ntok"]: hh[:, m, 0:n], h2_b[bi % 2], 5,
                        aff_eng=("act" if bi == 0 else "pool"))
            p1_mh(0)
            for bi, b in enumerate(p1_blocks):
                n = b["ntok"]
                hh, hh_b = h2[bi % 2], h2_b[bi % 2]
                if bi + 1 < len(p1_blocks):
                    p1_mh(bi + 1)
                rq = reqs[bi]
                (wk, wk_b), (wv, wv_b) = W.use(rq[0]), W.use(rq[1])
                wk3, wv3 = kv(wk, 256), kv(wv, 256)
                for kh in range(2):
                    def mmk(e, kh=kh, hh=hh, n=n, wk3=wk3):
                        for k in range(8):
                            ins = e.matmul(ps[kh][:, 0:n], wk3[:, k, kh * 128:(kh + 1) * 128], hh[:, k, 0:n],
                                           start=(k == 0), stop=(k == 7))
                        return ins
                    P.op("pe", mmk, reads=[wk_b] + hh_b, writes=[ps_b[kh]])
                rope = (b["lat"] * 512) if b["lat"] is not None else None
                posts = [head_post(ps[kh][:, 0:n], ps_b[kh], n, V_KN, kT[:, kh, b["key0"]:b["key0"] + n], kT_b[kh][b["kb"]],
                                   rope, (6, 2)[kh], (7, 3)[kh]) for kh in range(2)]
                for i in range(n // 128):
                    s = b["key0"] // 128 + i
                    pv = 2 + vcnt % 2
                    vcnt += 1

                    def mmv(e, i=i, hh=hh, wv3=wv3, pv=pv):
                        for k in range(8):
                            ins = e.matmul(ps[pv][:, 0:256], hh[:, k, i * 128:(i + 1) * 128], wv3[:, k, :],
                                           start=(k == 0), stop=(k == 7))
                        return ins
                    P.op("pe", mmv, reads=[wv_b] + hh_b, writes=[ps_b[pv]])
                    P.op("act", lambda e, s=s, pv=pv: e.copy(out=Vt[:, s, :], in_=ps[pv][:, 0:256]), reads=[ps_b[pv]],
                         writes=[V_b[s]])
                W.done(rq[0]); W.done(rq[1])
                posts[0][0](); posts[0][1]()
                if b["lat"] is None:
                    posts[1][0](); posts[1][1]()
                else:
                    t = b["lat"]
                    wf = [W.use(rq[2]), W.use(rq[3])]

                    def mmf_op(g):
                        pf = 4 + g % 2
                        wf3 = kv(wf[g // 2][0], 256)
                        co = (g % 2) * 128

                        def mmf(e, pf=pf, hh=hh, wf3=wf3, co=co):
                            for k in range(8):
                                ins = e.matmul(ps[pf][:, 0:512], wf3[:, k, co:co + 128], hh[:, k, 0:512],
                                               start=(k == 0), stop=(k == 7))
                            return ins
                        P.op("pe", mmf, reads=[wf[g // 2][1]] + hh_b, writes=[ps_b[pf]])
                    def evac_f(g):
                        pf = 4 + g % 2
                        fs = g % 2
                        P.op("act", lambda e, pf=pf, fs=fs: e.copy(out=fT[fs], in_=ps[pf][:, 0:512]), reads=[ps_b[pf]],
                             writes=[fT_b[fs]])
                    mmf_op(0)
                    mmf_op(1)
                    evac_f(0)
                    posts[1][0](); posts[1][1]()
                    mmf_op(2)
                    for g in range(4):
                        if g > 0:
                            evac_f(g)
                        if g == 2:
                            mmf_op(3)
                        fs = g % 2
                        for i2 in range(2):
                            pu = 6 + ucnt % 2
                            ucnt += 1
                            ti = i2 * 8 + t * 2

                            def mmu2(e, fs=fs, i2=i2, pu=pu):
                                fpar = fT[fs].rearrange("p (j two) -> p two j", two=2)
                                for a in range(2):
                                    ins = e.matmul(ps[pu][:, a * 256:(a + 1) * 256], fpar[:, i2, a * 128:(a + 1) * 128], csc_bf,
                                                   start=True, stop=True)
                                return ins
                            P.op("pe", mmu2, reads=[fT_b[fs], cbf_b], writes=[ps_b[pu]])
                            P.op("act", lambda e, g=g, ti=ti, pu=pu: e.copy(
                                out=UW[:, g, ti:ti + 2, :], in_=ps[pu][:].rearrange("p (a b) -> p a b", b=256)),
                                reads=[ps_b[pu]], writes=[UW_b[g][ti], UW_b[g][ti + 1]])
                    W.done(rq[2]); W.done(rq[3])
                if bi + 2 < len(p1_blocks):
                    reqs[bi + 2] = p1_req(p1_blocks[bi + 2])

            P.barrier()
            tslots = [(ring[:, i // 2, (i % 2) * 1024:(i % 2 + 1) * 1024], Buf("tab%d" % i)) for i in range(8)]
            tab_ring = [sl for sl in W.free if any(sl is r for r in ring_slots[0:4])]
            assert len(tab_ring) == 4 and len(W.free) == 8 and not W.pending
            for sl in tab_ring:
                W.free.remove(sl)
            p2req = [dict(q=None, mid=[], o=[]) for _ in range(4)]
            p2req[0]["q"] = [W.request(win_d[i]) for i in range(4)]
            for t in range(4):
                r = p2req[t]
                if t + 1 < 4:
                    p2req[t + 1]["q"] = []
                for i in range(4):
                    if t + 1 < 4:
                        p2req[t + 1]["q"].append(W.request(win_d[i]))
                    r["mid"].append((W.request(wab_d[i]), W.request(win_d[8 + i]), W.request(wfb_d[i], 1024),
                                     W.request(win_d[12 + i])))
                r["o"] = [W.request(wo_d[i]) for i in range(4)]

            for mb in range(2):
                for ti in range(16):
                    idx = mb * 16 + ti
                    tap, tb = tslots[idx % 8]
                    P.dma("sp", lambda e, tap=tap, idx=idx: [e.dma_start(out=tap, in_=tabs_d[idx])], tb)
                    bk0 = 0 if ti < 8 else 4

                    def mmy(e, ti=ti, tap=tap, bk0=bk0):
                        for g in range(4):
                            e.matmul(ps[bk0 + g][:, 0:512], UW[:, g, ti, 0:128], tap[:, 0:512], start=(ti % 8 == 0), stop=False)
                            ins = e.matmul(ps[bk0 + g][:, 0:512], UW[:, g, ti, 128:256], tap[:, 512:1024], start=False,
                                           stop=(ti % 8 == 7))
                        return ins
                    P.op("pe", mmy, reads=[tb] + [UW_b[g][ti] for g in range(4)], writes=[ps_b[bk0 + g] for g in range(4)])
                for g in range(4):
                    ta = 4 + g % 2
                    P.op("act", lambda e, g=g, ta=ta: e.activation(out=tmpf[:, ta, :], in_=ps[g][:, 0:512], func=AF.Copy,
                                                                    scale=1.0 / 512.0),
                         reads=[ps_b[g]], writes=[tmpf_b[ta]])
                    P.op("dve", lambda e, g=g, ta=ta, mb=mb: e.scalar_tensor_tensor(
                        out=yfour[:, g, mb * 512:(mb + 1) * 512], in0=ps[4 + g][:, 0:512], scalar=1.0 / 512.0, in1=tmpf[:, ta, :],
                        op0=ALU.mult, op1=ALU.add), reads=[ps_b[4 + g], tmpf_b[ta]], writes=[yf_b[g][mb]])
                    P.op("dve", lambda e, g=g, ta=ta, mb=mb: e.scalar_tensor_tensor(
                        out=yfour[:, g, 1024 + mb * 512:1024 + (mb + 1) * 512], in0=ps[4 + g][:, 0:512], scalar=-1.0 / 512.0,
                        in1=tmpf[:, ta, :], op0=ALU.mult, op1=ALU.add), reads=[ps_b[4 + g], tmpf_b[ta]], writes=[yf_b[g][2 + mb]])

            P.barrier()
            W.free.extend(tab_ring)
            W._pump()
            base = 17408
            hq = [arena[:, base:base + 4096].rearrange("p (k n) -> p k n", n=512),
                  arena[:, 30720:34816].rearrange("p (k n) -> p k n", n=512)]
            qt = [arena[:, base + 4096:base + 8192].rearrange("p (k n) -> p k n", n=512),
                  cxT[:].rearrange("p k n -> p (k n)").bitcast(BF16).rearrange("p (k n) -> p k n", n=512)]
            mg = arena[:, base + 8192:base + 12288].rearrange("p (k n) -> p k n", n=512)
            PT = [arena[:, base + 12288 + i * 512:base + 12288 + (i + 1) * 512] for i in range(2)] + [pt2[:, 0, :], pt2[:, 1, :]]
            hq_b = [[Buf("hq%d_%d" % (i, m)) for m in range(NCH)] for i in range(2)]
            qt_b = [[Buf("qt%d_%d" % (i, m)) for m in range(NCH)] for i in range(2)]
            mg_b = [Buf("mg%d" % m) for m in range(NCH)]
            PT_b = [Buf("PT%d" % i) for i in range(4)]
            dacc, dacc_b = tmpf[:, 7, :], tmpf_b[7]
            daccB, daccB_b = tmpf[:, 4, :], tmpf_b[4]
            DEN_ROLE = "PPPDPPPDPPPDPPPDPP"

            def mm8(e, pb, w3, c0, rhs, nk):
                for k in range(nk):
                    ins = e.matmul(ps[pb][:, 0:512], w3[:, k, c0:c0 + 128], rhs(k), start=(k == 0), stop=(k == nk - 1))
                return ins

            def mh_steps(t, aff_eng="pool"):
                b = lat_block(t)
                hh, hh_b = hq[t % 2], hq_b[t % 2]
                return mh_thunks(b["xsl"], b["xb"], 512, 3, 4, lambda m: hh[:, m, :], hh_b, 7, sq_eng="dve", aff_eng=aff_eng)

            def q_mm(t, h):
                r = p2req[t]
                wq, wq_b = W.use(r["q"][h // 2])
                wq3 = kv(wq, 256)
                pq = 4 + h % 2
                co = (h % 2) * 128
                hh, hh_b = hq[t % 2], hq_b[t % 2]

                def mmq(e):
                    for k in range(8):
                        ins = e.matmul(ps[pq][:, 0:512], wq3[:, k, co:co + 128], hh[:, k, :], start=(k == 0), stop=(k == 7))
                    return ins
                P.op("pe", mmq, reads=[wq_b] + hh_b, writes=[ps_b[pq]])
                if h % 2 == 1:
                    W.done(r["q"][h // 2])
                return head_post(ps[pq][:, 0:512], ps_b[pq], 512, V_QN, qt[t % 2][:, h, :], qt_b[t % 2][h], t * 512, 3, 6)

            def attention(t, side):
                Q, Q_b = qt[t % 2], qt_b[t % 2]
                for h in range(8):
                    kvh = h // 4
                    po = h % 2
                    SB = [2, 4, 5]

                    def s_op(s, h=h, kvh=kvh):
                        pb = SB[s % 3]
                        kb = kT_b[kvh][0] if s < 2 else kT_b[kvh][1 + (s - 2) // 4]
                        P.op("pe", lambda e: e.matmul(ps[pb][:, 0:512], kT[:, kvh, s * 128:(s + 1) * 128], Q[:, h, :],
                                                      start=True, stop=True),
                             reads=[kb, Q_b[h]], writes=[ps_b[pb]])

                    def e_op(s):
                        pb = SB[s % 3]
                        pi = s % 4
                        P.op("act", lambda e: e.activation(out=PT[pi], in_=ps[pb][:, 0:512], func=AF.Exp, scale=ATTN_SCALE),
                             reads=[ps_b[pb]], writes=[PT_b[pi]])

                    pd = 3 if h % 2 == 0 else 6

                    def pv_op(s, kvh=kvh, po=po, pd=pd):
                        pi = s % 4
                        role = DEN_ROLE[s]
                        if role == "P":
                            def f(e):
                                e.matmul(ps[po][:, 0:512], Vt[:, s, kvh * 128:(kvh + 1) * 128], PT[pi], start=(s == 0), stop=(s == 17))
                                return e.matmul(ps[pd][:, 0:512], ones_bf, PT[pi], start=(s == 0), stop=False)
                            P.op("pe", f, reads=[V_b[s], PT_b[pi], cbf_b], writes=[ps_b[po], ps_b[pd]])
                            return
                        P.op("pe", lambda e: e.matmul(ps[po][:, 0:512], Vt[:, s, kvh * 128:(kvh + 1) * 128], PT[pi],
                                                      start=(s == 0), stop=(s == 17)),
                             reads=[V_b[s], PT_b[pi]], writes=[ps_b[po]])
                        if role == "D":
                            if s == DEN_ROLE.index("D"):
                                P.op("dve", lambda e: e.tensor_copy(out=dacc, in_=PT[pi]), reads=[PT_b[pi]], writes=[dacc_b])
                            else:
                                P.op("dve", lambda e: e.tensor_tensor(out=dacc, in0=dacc, in1=PT[pi], op=ALU.add),
                                     reads=[PT_b[pi], dacc_b], writes=[dacc_b])
                        elif s == 1:
                            pass
                        elif s == 2:
                            P.op("pool", lambda e: e.tensor_tensor(out=daccB, in0=PT[1], in1=PT[2], op=ALU.add),
                                 reads=[PT_b[1], PT_b[2]], writes=[daccB_b])
                        else:
                            P.op("pool", lambda e: e.tensor_tensor(out=daccB, in0=daccB, in1=PT[pi], op=ALU.add),
                                 reads=[PT_b[pi], daccB_b], writes=[daccB_b])
                    s_op(0); s_op(1)
                    for s in range(18):
                        e_op(s)
                        if s + 2 < 18:
                            s_op(s + 2)
                        pv_op(s)
                        if side:
                            side.pop(0)()
                    if "G" in DEN_ROLE:
                        P.op("dve", lambda e: e.tensor_tensor(out=dacc, in0=dacc, in1=daccB, op=ALU.add),
                             reads=[dacc_b, daccB_b], writes=[dacc_b])
                    P.op("pe", lambda e, pd=pd: e.matmul(ps[pd][:, 0:512], ones_f[:], dacc, start=False, stop=True),
                         reads=[dacc_b, onesf_b], writes=[ps_b[pd]])
                    if USE_APPROX_RECIP:
                        P.op("dve", lambda e, pd=pd: e.reciprocal_approx_accurate(tmpf[:, 6, :], ps[pd][:, 0:512], tmpf[:, 5, :]),
                             reads=[ps_b[pd]], writes=[tmpf_b[6], tmpf_b[5]])
                    else:
                        P.op("dve", lambda e, pd=pd: e.reciprocal(out=tmpf[:, 6, :], in_=ps[pd][:, 0:512]),
                             reads=[ps_b[pd]], writes=[tmpf_b[6]])
                    P.op("dve", lambda e, h=h, po=po: e.tensor_tensor(out=Q[:, h, :], in0=ps[po][:, 0:512], in1=tmpf[:, 6, :],
                                                                       op=ALU.mult),
                         reads=[ps_b[po], tmpf_b[6]], writes=[Q_b[h]])
                while side:
                    side.pop(0)()

            def merge_step(t, m):
                r = p2req[t]
                Y, Y_b = qt[t % 2], qt_b[t % 2]
                hh, hh_b = hq[t % 2], hq_b[t % 2]
                hab, hga, hfb, hgf = r["mid"][m // 2]
                (wab, wab_b), (wga, wga_b), (wfb, wfb_b), (wgf, wgf_b) = W.use(hab), W.use(hga), W.use(hfb), W.use(hgf)
                wab3, wga3, wfb3, wgf3 = kv(wab, 256), kv(wga, 256), kv(wfb, 256), kv(wgf, 256)
                c0 = (m % 2) * 128
                P.op("pe", lambda e: mm8(e, 0, wab3, c0, lambda k: Y[:, k, :], 8), reads=[wab_b] + Y_b, writes=[ps_b[0]])
                P.op("pe", lambda e: mm8(e, 1, wga3, c0, lambda k: hh[:, k, :], 8), reads=[wga_b] + hh_b, writes=[ps_b[1]])
                P.op("pe", lambda e: mm8(e, 2, wfb3, c0, lambda k: yfour[:, k, t * 512:(t + 1) * 512], 4),
                     reads=[wfb_b] + [yf_b[g][t] for g in range(4)], writes=[ps_b[2]])
                P.op("pe", lambda e: mm8(e, 7, wgf3, c0, lambda k: hh[:, k, :], 8), reads=[wgf_b] + hh_b, writes=[ps_b[7]])
                P.op("act", lambda e: e.activation(out=tmpf[:, 4, :], in_=ps[1][:, 0:512], func=AF.Sigmoid),
                     reads=[ps_b[1]], writes=[tmpf_b[4]])
                P.op("act", lambda e: e.activation(out=tmpf[:, 5, :], in_=ps[7][:, 0:512], func=AF.Sigmoid),
                     reads=[ps_b[7]], writes=[tmpf_b[5]])
                P.op("dve", lambda e: e.tensor_tensor(out=tmpf[:, 4, :], in0=ps[0][:, 0:512], in1=tmpf[:, 4, :], op=ALU.mult),
                     reads=[ps_b[0], tmpf_b[4]], writes=[tmpf_b[4]])
                P.op("dve", lambda e: e.tensor_tensor(out=tmpf[:, 5, :], in0=ps[2][:, 0:512], in1=tmpf[:, 5, :], op=ALU.mult),
                     reads=[ps_b[2], tmpf_b[5]], writes=[tmpf_b[5]])
                P.op("dve", lambda e: e.tensor_tensor(out=mg[:, m, :], in0=tmpf[:, 4, :], in1=tmpf[:, 5, :], op=ALU.add),
                     reads=[tmpf_b[4], tmpf_b[5]], writes=[mg_b[m]])
                if m % 2 == 1:
                    for hnd in r["mid"][m // 2]:
                        W.done(hnd)

            def outproj(t, early=()):
                r = p2req[t]
                b = lat_block(t)
                early = list(early)
                ek = (len(early) + 7) // 8
                for m in range(NCH):
                    for _ in range(ek):
                        if early:
                            early.pop(0)()
                    wo, wo_b = W.use(r["o"][m // 2])
                    wo3 = kv(wo, 256)
                    c0 = (m % 2) * 128
                    po = m % 2

                    def mmo(e, wo3=wo3, c0=c0, po=po):
                        for k in range(8):
                            ins = e.matmul(ps[po][:, 0:512], wo3[:, k, c0:c0 + 128], mg[:, k, :], start=(k == 0), stop=(k == 7))
                        return ins
                    P.op("pe", mmo, reads=[wo_b] + mg_b, writes=[ps_b[po]])
                    P.op("dve", lambda e, m=m, po=po: e.scalar_tensor_tensor(out=b["xsl"](m), in0=ps[po][:, 0:512],
                                                                              scalar=cols[:, 5, m:m + 1], in1=b["xsl"](m),
                                                                              op0=ALU.mult, op1=ALU.add),
                         reads=[ps_b[po], cols_b[5], b["xb"][m]], writes=[b["xb"][m]])
                    if m % 2 == 1:
                        W.done(r["o"][m // 2])

            for th in mh_steps(0, aff_eng="act"):
                th()
            prev = None
            for h in range(8):
                parts = q_mm(0, h)
                if prev is not None:
                    prev[0](); prev[1]()
                prev = parts
            prev[0](); prev[1]()
            for t in range(4):
                nxt = t + 1 < 4
                attention(t, mh_steps(t + 1) if nxt else [])
                prev = None
                for m in range(NCH):
                    if nxt:
                        parts = q_mm(t + 1, m)
                    merge_step(t, m)
                    if nxt:
                        if prev is not None:
                            prev[0](); prev[1]()
                        prev = parts
                if nxt:
                    prev[0](); prev[1]()
                early = []
                if t == 3 and stage >= 4:
                    ffn_prepare(ffn2_blocks)
                    ffn2_pre.append(ffn_requests(w2i_d, w2o_d, 0))
                    P.wait("pool", [("pe", P.cnt["pe"])])
                    for bi in (0, 1):
                        early += ffn_mh_thunks(ffn2_blocks, bi)
                        ffn2_blocks[bi]["built"] = True
                outproj(t, early)

        st_b = [Buf("store%d" % i) for i in range(8)]

        def emit_output(t):
            stiles = [6, 7, 0, 1, 2, 3] if (t < 3 and stage >= 4) else [6, 7, 0, 1, 2, 3, 4, 5]
            ths = []
            k = 0
            for i in range(t * 4, t * 4 + 4):
                for half in range(2):
                    pb = 6 + half
                    sti = stiles[k % len(stiles)]
                    k += 1

                    def th(i=i, half=half, pb=pb, sti=sti):
                        def tr(e):
                            for j in range(4):
                                m = half * 4 + j
                                ins = e.transpose(ps[pb][:, j * 128:(j + 1) * 128], xT[:, m, i * 128:(i + 1) * 128], ident[:])
                            return ins
                        P.op("pe", tr, reads=[xT_b[m][t] for m in range(half * 4, half * 4 + 4)] + [ident_b], writes=[ps_b[pb]])
                        if half == 0:
                            P.op("act", lambda e: e.copy(out=tmpf[:, sti, :], in_=ps[pb][:]), reads=[ps_b[pb]],
                                 writes=[tmpf_b[sti]])
                        else:
                            P.op("dve", lambda e: e.tensor_copy(out=tmpf[:, sti, :], in_=ps[pb][:]), reads=[ps_b[pb]],
                                 writes=[tmpf_b[sti]])
                        P.dma("sp", lambda e: [e.dma_start(
                            out=out_d[i * 128:(i + 1) * 128, half * 512:(half + 1) * 512], in_=tmpf[:, sti, :])],
                              st_b[sti], reads=[tmpf_b[sti]], writes=[])
                    ths.append(th)
            return ths

        if stage >= 4:
            pre = ffn2_pre[0] if ffn2_pre else ffn_requests(w2i_d, w2o_d, 0)
            P.barrier()
            ffn(w2i_d, w2o_d, ffn2_blocks, pre, after_out=emit_output)
        else:
            P.barrier()
            for t in range(4):
                for th in emit_output(t):
                    th()
        P.wait("sp", [(b.dsem, b.dcnt) for b in st_b if b.dsem is not None])
        P.build(st)
    return nc


_CACHE = {}


def _consts():
    if "c" in _CACHE:
        return _CACHE["c"]
    bf = ml_dtypes.bfloat16
    ident = np.eye(128, dtype=np.float32)
    cb = np.zeros((128, C_END), dtype=np.float32)
    cb[:, C_ONES:C_ONES + 128] = 1.0
    R = np.zeros((128, 128), dtype=np.float32)
    for m in range(64):
        R[m + 64, m] = -1.0
        R[m, m + 64] = 1.0
    cb[:, C_RPERM:C_RPERM + 128] = R
    cc = np.arange(128, dtype=np.float64)
    ang = 2.0 * np.pi * np.outer(cc, cc) / 128.0
    cb[:, C_CSC:C_CSC + 128] = np.cos(ang)
    cb[:, C_CSC + 128:C_CSC + 256] = np.sin(ang)
    rows = SEQ // 64
    row_ids = np.repeat(np.arange(rows, dtype=np.float32), 64)
    col_ids = np.tile(np.arange(64, dtype=np.float32), rows)
    inv_freq = (np.float32(10000.0) ** (-np.arange(0, 64, 2, dtype=np.float32) / np.float32(64))).astype(np.float32)
    angr = np.concatenate([row_ids[:, None] * inv_freq, col_ids[:, None] * inv_freq], axis=-1).astype(np.float32)
    cosT = np.cos(angr).T
    sinT = np.sin(angr).T
    cb[0:64, C_COS:C_COS + 2048] = cosT
    cb[64:128, C_COS:C_COS + 2048] = cosT
    cb[0:64, C_SIN:C_SIN + 2048] = sinT
    cb[64:128, C_SIN:C_SIN + 2048] = sinT
    cbf = cb.astype(bf)
    tabs = np.zeros((32, 128, 1024), dtype=np.float32)
    j = np.arange(128, dtype=np.int64)
    for mb in range(2):
        m = np.arange(mb * 512, (mb + 1) * 512, dtype=np.int64)
        for ti in range(16):
            n = 2 * (128 * (ti % 8) + j) + (1 if ti >= 8 else 0)
            ang = (np.outer(n, m) % SEQ).astype(np.float64) * (2.0 * np.pi / SEQ)
            tabs[mb * 16 + ti, :, 0:512] = np.cos(ang)
            tabs[mb * 16 + ti, :, 512:1024] = -np.sin(ang)
    _CACHE["c"] = (ident, cbf, tabs.astype(bf))
    return _CACHE["c"]


def _colz(v):
    v = np.asarray(v, dtype=np.float32).reshape(-1, 128)
    return np.ascontiguousarray(v.T)


def _pack_k(w, ncols=256):
    K, N = w.shape
    kc = K // 128
    t = w.reshape(kc, 128, N // ncols, ncols)
    return np.ascontiguousarray(t.transpose(2, 1, 0, 3)).reshape(N // ncols, 128, kc * ncols)


def _pack_o(w):
    t = w.reshape(11, 2, 128, 1024)
    return np.ascontiguousarray(t.transpose(0, 2, 1, 3)).reshape(11, 128, 2048)


def kernel(x, c, ctx, c_ctx, w_ada, b_ada, norm_ffn1, w_ffn1_in, w_ffn1_out, norm_mix, w_in, q_norm, k_norm,
           w_attn_branch, w_fourier_branch, w_out, norm_ffn2, w_ffn2_in, w_ffn2_out, _stage=9, _ncores=8):
    f = lambda a: np.ascontiguousarray(np.asarray(a, dtype=np.float32))
    ident, cbf, tabs = _consts()
    key = ("nc", _stage)
    if key not in _CACHE:
        _CACHE[key] = build_program(_stage)
    nc = _CACHE[key]
    x = f(x); ctx = f(ctx); c = f(c)
    shared = dict(ident=ident, cbf=cbf, tabs=tabs, w_ada=_pack_k(f(w_ada)[0]), w_ffn1_in=_pack_k(f(w_ffn1_in)[0]),
                  w_ffn1_out=_pack_o(f(w_ffn1_out)[0]), w_in=_pack_k(f(w_in)[0]), w_ab=_pack_k(f(w_attn_branch)[0]),
                  w_fb=_pack_k(f(w_fourier_branch)[0]), w_o=_pack_k(f(w_out)[0]),
                  w_ffn2_in=_pack_k(f(w_ffn2_in)[0]), w_ffn2_out=_pack_o(f(w_ffn2_out)[0]))
    in_maps = []
    for b in range(_ncores):
        vec = np.zeros((128, NV), dtype=np.float32)
        vec[:, V_C:V_C + 8] = _colz(c[b])
        vec[:, V_CC:V_CC + 8] = _colz(f(c_ctx))
        vec[:, V_BADA:V_BADA + 72] = _colz(f(b_ada)[0])
        vec[:, V_NF1:V_NF1 + 8] = _colz(f(norm_ffn1)[0])
        vec[:, V_NMIX:V_NMIX + 8] = _colz(f(norm_mix)[0])
        vec[:, V_NF2:V_NF2 + 8] = _colz(f(norm_ffn2)[0])
        vec[:, V_QN] = f(q_norm)[0]
        vec[:, V_KN] = f(k_norm)[0]
        m = dict(shared)
        m.update(x=x[b], ctx=ctx[b], vecs=vec)
        in_maps.append(m)
    res = run_bass_kernel_spmd(nc, in_maps, core_ids=list(range(_ncores)))
    return np.stack([np.asarray(r["out"], dtype=np.float32) for r in res.results], axis=0)
```

```python
import os
import numpy as np
import ml_dtypes
from contextlib import ExitStack
import concourse.bass as bass
import concourse.mybir as mybir
from concourse.bass_utils import run_bass_kernel_spmd

F32 = mybir.dt.float32
BF16 = mybir.dt.bfloat16
AF = mybir.ActivationFunctionType
ALU = mybir.AluOpType

D = 1024
SEQ = 2048
CTX = 256
DFF = 2816
NCH = 8
EPS = 1e-6
ATTN_SCALE = 128 ** -0.5
NV = 128
USE_APPROX_RECIP = False
PRIO_TILE = 7
V_C, V_CC, V_BADA, V_NF1, V_NMIX, V_NF2, V_QN, V_KN = 0, 8, 16, 88, 96, 104, 112, 113
C_ONES, C_RPERM, C_CSC, C_COS, C_SIN, C_END = 0, 128, 256, 512, 2560, 4608


class Buf:
    __slots__ = ("name", "w", "r", "dsem", "dcnt")

    def __init__(self, name=""):
        self.name = name
        self.w = None
        self.r = []
        self.dsem = None
        self.dcnt = 0


class Prog:
    ENG = ("pe", "act", "dve", "pool", "sp")

    def __init__(self, nc):
        self.nc = nc
        self.q = {k: [] for k in self.ENG}
        self.cnt = {k: 0 for k in self.ENG}
        self.waited = {k: {} for k in self.ENG}
        self.semkeys = list(self.ENG)
        self.sems = {}
        self.n_dsem = 0
        self.dbufs = []

    def _collect(self, eng, reads, writes, extra):
        need = {}

        def add(tok):
            if tok is None:
                return
            k, v = tok
            if need.get(k, 0) < v:
                need[k] = v
        for b in reads:
            add(b.w)
        for b in writes:
            add(b.w)
            for t in b.r:
                add(t)
        for t in extra:
            add(t)
        out = []
        wd = self.waited[eng]
        for k, v in need.items():
            if k == eng and eng == "pe":
                continue
            if wd.get(k, 0) >= v:
                continue
            wd[k] = v
            out.append((k, v))
        return out

    def _commit(self, tok, reads, writes):
        for b in writes:
            b.w = tok
            b.r = []
        for b in reads:
            b.r.append(tok)

    def op(self, eng, fn, reads=(), writes=(), extra=()):
        waits = self._collect(eng, reads, writes, extra)
        self.cnt[eng] += 1
        tok = (eng, self.cnt[eng])
        sems = self.sems

        def run(e, waits=waits, fn=fn, eng=eng):
            for k, v in waits:
                e.wait_ge(sems[k], v)
            ins = fn(e)
            ins.then_inc(sems[eng], 1)
        self.q[eng].append(run)
        self._commit(tok, reads, writes)
        return tok

    def dma(self, eng, fn, buf, reads=(), writes=None, n=1):
        if writes is None:
            writes = (buf,)
        waits = self._collect(eng, reads, writes, ())
        if buf.dsem is None:
            buf.dsem = ("d", self.n_dsem)
            self.n_dsem += 1
            self.semkeys.append(buf.dsem)
            self.dbufs.append(buf)
        buf.dcnt += 16 * n
        tok = (buf.dsem, buf.dcnt)
        sems = self.sems

        def run(e, waits=waits, fn=fn, key=buf.dsem):
            for k, v in waits:
                e.wait_ge(sems[k], v)
            for ins in fn(e):
                ins.then_inc(sems[key], 16)
        self.q[eng].append(run)
        self._commit(tok, reads, writes)
        return tok

    def wait(self, eng, toks):
        waits = self._collect(eng, (), (), toks)
        if not waits:
            return
        sems = self.sems

        def run(e, waits=waits):
            for k, v in waits:
                e.wait_ge(sems[k], v)
        self.q[eng].append(run)

    def barrier(self):
        toks = [(k, self.cnt[k]) for k in self.ENG if self.cnt[k] > 0]
        toks += [(b.dsem, b.dcnt) for b in self.dbufs]
        for eng in self.ENG:
            self.wait(eng, toks)

    def build(self, stack):
        nc = self.nc
        for k in self.semkeys:
            nm = k if isinstance(k, str) else "d%d" % k[1]
            self.sems[k] = stack.enter_context(nc.semaphore("s_" + nm))
        block = stack.enter_context(nc.Block())
        q = self.q

        @block.tensor
        def _(e):
            for c in q["pe"]:
                c(e)

        @block.scalar
        def _(e):
            for c in q["act"]:
                c(e)

        @block.vector
        def _(e):
            for c in q["dve"]:
                c(e)

        @block.gpsimd
        def _(e):
            for c in q["pool"]:
                c(e)

        @block.sync
        def _(e):
            for c in q["sp"]:
                c(e)


class WStream:
    def __init__(self, P, slots):
        self.P = P
        self.free = list(slots)
        self.pending = []

    def request(self, src, nel=2048):
        h = {"src": src, "nel": nel, "slot": None}
        self.pending.append(h)
        self._pump()
        return h

    def _pump(self):
        while self.pending and self.free:
            h = self.pending.pop(0)
            slot = self.free.pop(0)
            h["slot"] = slot
            ap, buf = slot
            view = ap[:, 0:h["nel"]]
            h["view"] = view
            src = h["src"]
            h["tok"] = self.P.dma("pool", lambda e, view=view, src=src: [e.dma_start(out=view, in_=src, max_dma_last_dim=8192)], buf)

    def use(self, h):
        assert h["slot"] is not None, "weight tile not issued (ring too small for access order)"
        return h["view"], h["slot"][1]

    def done(self, h):
        self.free.append(h["slot"])
        self._pump()


def kv(ap, c):
    return ap.rearrange("p (k c) -> p k c", c=c)


def build_program(stage=9):
    nc = bass.Bass("TRN2", target_bir_lowering=False)
    dt = nc.dram_tensor
    x_d = dt("x", [SEQ, D], F32, kind="ExternalInput").ap()
    ctx_d = dt("ctx", [CTX, D], F32, kind="ExternalInput").ap()
    vecs_d = dt("vecs", [128, NV], F32, kind="ExternalInput").ap()
    ident_d = dt("ident", [128, 128], F32, kind="ExternalInput").ap()
    cbf_d = dt("cbf", [128, C_END], BF16, kind="ExternalInput").ap()
    tabs_d = dt("tabs", [32, 128, 1024], BF16, kind="ExternalInput").ap()
    w_ada_d = dt("w_ada", [36, 128, 2048], F32, kind="ExternalInput").ap()
    w1i_d = dt("w_ffn1_in", [22, 128, 2048], F32, kind="ExternalInput").ap()
    w1o_d = dt("w_ffn1_out", [11, 128, 2048], F32, kind="ExternalInput").ap()
    win_d = dt("w_in", [16, 128, 2048], F32, kind="ExternalInput").ap()
    wab_d = dt("w_ab", [4, 128, 2048], F32, kind="ExternalInput").ap()
    wfb_d = dt("w_fb", [4, 128, 1024], F32, kind="ExternalInput").ap()
    wo_d = dt("w_o", [4, 128, 2048], F32, kind="ExternalInput").ap()
    w2i_d = dt("w_ffn2_in", [22, 128, 2048], F32, kind="ExternalInput").ap()
    w2o_d = dt("w_ffn2_out", [11, 128, 2048], F32, kind="ExternalInput").ap()
    out_d = dt("out", [SEQ, D], F32, kind="ExternalOutput").ap()

    with ExitStack() as st:
        P = Prog(nc)
        sb = lambda name, shape, dtype: st.enter_context(nc.sbuf_tensor(name, shape, dtype))
        xT = sb("xT", [128, NCH, SEQ], F32)
        cxT = sb("cxT", [128, NCH, CTX], F32)
        ring = sb("ring", [128, 8, 2048], BF16)
        arena = sb("arena", [128, 34816], BF16)
        tmpf = sb("tmpf", [128, 8, 512], F32)
        tbf = sb("tbf", [128, 4, 512], BF16)
        cbf = sb("cbf_sb", [128, C_END], BF16)
        ident = sb("ident_sb", [128, 128], F32)
        vecs = sb("vecs_sb", [128, NV], F32)
        mods = sb("mods", [128, 2, 72], F32)
        cols = sb("cols", [128, 16, 8], F32)
        csb = sb("csb", [128, 8, 2], BF16)
        csf = sb("csf", [128, 16], F32)
        epsc = sb("epsc", [128, 1], F32)
        pt2 = sb("pt2", [128, 2, 512], BF16)
        ones_f = sb("ones_f", [128, 128], F32)
        ps = [st.enter_context(nc.psum_tensor("ps%d" % i, [128, 512], F32)) for i in range(8)]

        ps_b = [Buf("ps%d" % i) for i in range(8)]
        tmpf_b = [Buf("tmpf%d" % i) for i in range(8)]
        tbf_b = [Buf("tbf%d" % i) for i in range(4)]
        xT_b = [[Buf("xT%d_%d" % (m, t)) for t in range(4)] for m in range(NCH)]
        cxT_b = [Buf("cxT%d" % m) for m in range(NCH)]
        ring_slots = [(ring[:, i, :], Buf("ring%d" % i)) for i in range(8)]
        cbf_b, ident_b, vecs_b, csb_b, eps_b = (Buf("cbf"), Buf("ident"), Buf("vecs"), Buf("csb"), Buf("eps"))
        mods_b = [Buf("mods%d" % i) for i in range(3)]
        cols_b = [Buf("cols%d" % i) for i in range(16)]
        ones_bf = cbf[:, C_ONES:C_ONES + 128]
        rperm_bf = cbf[:, C_RPERM:C_RPERM + 128]
        csc_bf = cbf[:, C_CSC:C_CSC + 256]
        cos_bf = cbf[:, C_COS:C_COS + 2048]
        sin_bf = cbf[:, C_SIN:C_SIN + 2048]

        W = WStream(P, ring_slots)
        xring = [(arena[:, 24576 + i * 2048:24576 + (i + 1) * 2048], Buf("xring%d" % i)) for i in range(4)]
        if stage >= 2:
            W.free.extend(xring)
        lstg = [(tmpf[:, 6, :], tmpf_b[6]), (tmpf[:, 7, :], tmpf_b[7])]
        for i, off in enumerate((22528, 23552, 32768, 33792)):
            lstg.append((arena[:, off:off + 1024].bitcast(F32), Buf("lstg%d" % i)))
        lstg_n = [0]

        P.dma("sp", lambda e: [e.dma_start(out=vecs[:], in_=vecs_d)], vecs_b)
        P.dma("sp", lambda e: [e.dma_start(out=ident[:], in_=ident_d)], ident_b)
        P.dma("sp", lambda e: [e.dma_start(out=cbf[:], in_=cbf_d)], cbf_b)
        P.op("dve", lambda e: e.memset(epsc[:], EPS), writes=[eps_b])
        onesf_b = Buf("onesf")
        P.op("dve", lambda e: e.memset(ones_f[:], 1.0), writes=[onesf_b])

        ada_req = {}

        def ada_request(t):
            ada_req[t] = W.request(w_ada_d[t])

        for t in range(8):
            ada_request(t)

        def load_tile_thunks(src_rows, dst_fn, dst_bufs, banks):
            th = []
            for half in range(2):
                pb = banks[half]
                sap, sbuf_ = lstg[lstg_n[0] % len(lstg)]
                lstg_n[0] += 1

                def Dm(half=half, sap=sap, sbuf_=sbuf_):
                    P.dma("sp", lambda e: [e.dma_start(out=sap, in_=src_rows[:, half * 512:(half + 1) * 512])], sbuf_)

                def Tr(half=half, pb=pb, sap=sap, sbuf_=sbuf_):
                    def tr(e):
                        for j in range(4):
                            ins = e.transpose(ps[pb][:, j * 128:(j + 1) * 128], sap[:, j * 128:(j + 1) * 128], ident[:])
                        return ins
                    P.op("pe", tr, reads=[sbuf_, ident_b], writes=[ps_b[pb]])
                    dst = dst_fn(half)
                    if half == 0 or pb == 6:
                        P.op("act", lambda e: e.copy(out=dst, in_=ps[pb][:].rearrange("p (a b) -> p a b", b=128)),
                             reads=[ps_b[pb]], writes=dst_bufs[half * 4:half * 4 + 4])
                    else:
                        P.op("dve", lambda e: e.tensor_copy(out=dst, in_=ps[pb][:].rearrange("p (a b) -> p a b", b=128)),
                             reads=[ps_b[pb]], writes=dst_bufs[half * 4:half * 4 + 4])
                th.append((Dm, Tr))
            return th

        def load_pairs(bi, banks):
            pairs = []
            if bi == 4:
                for i in range(2):
                    pairs += load_tile_thunks(ctx_d[i * 128:(i + 1) * 128, :],
                                              lambda half, i=i: cxT[:, half * 4:half * 4 + 4, i * 128:(i + 1) * 128], cxT_b, banks)
            else:
                for i in range(bi * 4, bi * 4 + 4):
                    pairs += load_tile_thunks(x_d[i * 128:(i + 1) * 128, :],
                                              lambda half, i=i: xT[:, half * 4:half * 4 + 4, i * 128:(i + 1) * 128],
                                              [xT_b[m][bi] for m in range(NCH)], banks)
            return pairs

        LA = len(lstg)

        all_pairs = {0: load_pairs(0, (0, 1))}
        if stage >= 2:
            for bi in range(1, 5):
                all_pairs[bi] = load_pairs(bi, (6, 6))
        else:
            for bi in range(1, 5):
                all_pairs[bi] = load_pairs(bi, (0, 1))

        def load_block_thunks(bi, first_issued):
            pairs = all_pairs[bi]
            la = min(LA, len(pairs))
            th = []
            if not first_issued:
                th.append(lambda: [pairs[k][0]() for k in range(la)])
            for k in range(len(pairs)):
                if k + la < len(pairs):
                    th.append(lambda k=k: (pairs[k][1](), pairs[k + la][0]()))
                else:
                    th.append(pairs[k][1])
            if bi + 1 in all_pairs:
                nxt = all_pairs[bi + 1]

                def issue_next():
                    if bi == 0 and stage >= 2:
                        P.wait("sp", [ada_req[PRIO_TILE]["tok"]])
                    for k in range(min(LA, len(nxt))):
                        nxt[k][0]()
                th.append(issue_next)
            return th

        for th in load_block_thunks(0, False):
            th()
        if stage < 2:
            for bi in range(1, 5):
                for th in load_block_thunks(bi, True):
                    th()

        P.op("act", lambda e: e.activation(out=csf[:], in_=vecs[:, V_C:V_C + 16], func=AF.Silu), reads=[vecs_b], writes=[csb_b])
        P.op("dve", lambda e: e.tensor_copy(out=csb[:, :, 0], in_=csf[:, 0:8]), reads=[csb_b], writes=[csb_b])
        P.op("dve", lambda e: e.tensor_copy(out=csb[:, :, 1], in_=csf[:, 8:16]), reads=[csb_b], writes=[csb_b])

        def mcol(r, i):
            return mods[:, r, i * 8:(i + 1) * 8]

        def derive(dst, r, isc, vnorm, mb):
            P.op("dve", lambda e: e.scalar_tensor_tensor(out=cols[:, dst, :], in0=mcol(r, isc), scalar=1.0,
                                                          in1=vecs[:, vnorm:vnorm + 8], op0=ALU.add, op1=ALU.mult),
                 reads=[mb, vecs_b], writes=[cols_b[dst]])

        def cpy(dst, r, i, mul, mb):
            P.op("dve", lambda e: e.tensor_scalar(out=cols[:, dst, :], in0=mcol(r, i), scalar1=float(mul), scalar2=None,
                                                   op0=ALU.mult), reads=[mb], writes=[cols_b[dst]])

        def ada_tile(t):
            wv, wb = W.use(ada_req[t])
            wv3 = kv(wv, 256)

            def mm(e, t=t, wv3=wv3):
                for j in range(2):
                    ch = t * 2 + j
                    for k in range(8):
                        ins = e.matmul(ps[7][:, 2 * ch:2 * ch + 2], wv3[:, k, j * 128:(j + 1) * 128], csb[:, k, :],
                                       start=(k == 0), stop=(k == 7))
                return ins
            P.op("pe", mm, reads=[wb, csb_b], writes=[ps_b[7]])
            W.done(ada_req[t])

        ADA_PARTS = {"0a": (0, 16), "0b": (16, 24), "1": (24, 48), "2": (48, 72)}
        mods_pb = {k: Buf("mods" + k) for k in ADA_PARTS}

        def ada_finish(part):
            c0, c1 = ADA_PARTS[part]
            mb = mods_pb[part]
            for r in range(2):
                P.op("dve", lambda e, r=r: e.tensor_tensor(out=mods[:, r, c0:c1],
                                                            in0=ps[7][:, 2 * c0:2 * c1].rearrange("p (c r) -> p c r", r=2)[:, :, r],
                                                            in1=vecs[:, V_BADA + c0:V_BADA + c1], op=ALU.add),
                     reads=[ps_b[7], vecs_b], writes=[mb])
            if part == "0a":
                derive(0, 0, 1, V_NF1, mb); cpy(1, 0, 0, 1.0, mb)
                derive(9, 1, 1, V_NF1, mb); cpy(10, 1, 0, 1.0, mb)
            elif part == "0b":
                cpy(2, 0, 2, 0.5, mb); cpy(11, 1, 2, 0.5, mb)
            elif part == "1":
                derive(3, 0, 4, V_NMIX, mb); cpy(4, 0, 3, 1.0, mb); cpy(5, 0, 5, 1.0, mb)
                derive(12, 1, 4, V_NMIX, mb); cpy(13, 1, 3, 1.0, mb)
            else:
                derive(6, 0, 7, V_NF2, mb); cpy(7, 0, 6, 1.0, mb); cpy(8, 0, 8, 0.5, mb)

        def ada_consume(t):
            ada_tile(t)
            if t == 7:
                ada_finish("0a")
            if t == 11:
                ada_finish("0b")
            if t == 23:
                ada_finish("1")
            if t == 35:
                ada_finish("2")

        for t in range(8):
            ada_consume(t)
        ada_early = list(range(8, 12))
        ada_pending = list(range(12, 36))

        def mh_thunks(xsl, xbufs, ntok, ia, ish, hsl, hbufs, statbank, sq_eng="act", aff_eng="pool"):
            def A(m):
                sq = tbf[:, m % 3, 0:ntok]
                if sq_eng == "act":
                    P.op("act", lambda e: e.activation(out=sq, in_=xsl(m), func=AF.Square), reads=[xbufs[m]],
                         writes=[tbf_b[m % 3]])
                else:
                    P.op("dve", lambda e: e.tensor_tensor(out=sq, in0=xsl(m), in1=xsl(m), op=ALU.mult), reads=[xbufs[m]],
                         writes=[tbf_b[m % 3]])

            def B(m):
                sq = tbf[:, m % 3, 0:ntok]
                P.op("pe", lambda e: e.matmul(ps[statbank][:, 0:ntok], ones_bf, sq, start=(m == 0), stop=(m == 7)),
                     reads=[tbf_b[m % 3], cbf_b], writes=[ps_b[statbank]])

            def lnexp():
                P.op("act", lambda e: e.activation(out=tmpf[:, 0, 0:ntok], in_=ps[statbank][:, 0:ntok], func=AF.Ln,
                                                   bias=epsc[:, 0:1], scale=1.0 / D),
                     reads=[ps_b[statbank], eps_b], writes=[tmpf_b[0]])
                P.op("act", lambda e: e.activation(out=tmpf[:, 1, 0:ntok], in_=tmpf[:, 0, 0:ntok], func=AF.Exp, scale=-0.5),
                     reads=[tmpf_b[0]], writes=[tmpf_b[1]])

            def pair(m):
                tt = 2 + m % 2
                P.op("dve", lambda e: e.tensor_tensor(out=tmpf[:, tt, 0:ntok], in0=xsl(m), in1=tmpf[:, 1, 0:ntok], op=ALU.mult),
                     reads=[xbufs[m], tmpf_b[1]], writes=[tmpf_b[tt]])
                if aff_eng == "act":
                    P.op("act", lambda e: e.activation(out=hsl(m), in_=tmpf[:, tt, 0:ntok], func=AF.Identity,
                                                       bias=cols[:, ish, m:m + 1], scale=cols[:, ia, m:m + 1]),
                         reads=[tmpf_b[tt], cols_b[ia], cols_b[ish]], writes=[hbufs[m]])
                else:
                    P.op(aff_eng, lambda e: e.tensor_scalar(out=hsl(m), in0=tmpf[:, tt, 0:ntok], scalar1=cols[:, ia, m:m + 1],
                                                            scalar2=cols[:, ish, m:m + 1], op0=ALU.mult, op1=ALU.add),
                         reads=[tmpf_b[tt], cols_b[ia], cols_b[ish]], writes=[hbufs[m]])
            th = [lambda: A(0), lambda: A(1), lambda: A(2)]
            for m in range(NCH):
                if m + 3 < NCH:
                    th.append(lambda m=m: (B(m), A(m + 3)))
                else:
                    th.append(lambda m=m: B(m))
            th.append(lnexp)
            for m in range(NCH):
                th.append(lambda m=m: pair(m))
            return th

        def make_hT(*args, **kw):
            for th in mh_thunks(*args, **kw):
                th()

        hp_cnt = [0]

        def head_post(psrc, psrc_b, ntok, vgain, dst, dst_b, rope_off, bss, brot):
            par = hp_cnt[0] % 2
            hp_cnt[0] += 1
            qg, qg_b = tbf[:, 2 * par, 0:ntok], tbf_b[2 * par]
            sq, sq_b = tbf[:, 2 * par + 1, 0:ntok], tbf_b[2 * par + 1]
            P.op("act", lambda e: e.activation(out=qg, in_=psrc, func=AF.Identity, scale=vecs[:, vgain:vgain + 1]),
                 reads=[psrc_b, vecs_b], writes=[qg_b])
            P.op("act", lambda e: e.activation(out=sq, in_=psrc, func=AF.Square), reads=[psrc_b], writes=[sq_b])

            def pe_part():
                P.op("pe", lambda e: e.matmul(ps[bss][:, 0:ntok], ones_bf, sq, start=True, stop=True),
                     reads=[sq_b, cbf_b], writes=[ps_b[bss]])
                if rope_off is not None:
                    P.op("pe", lambda e: e.matmul(ps[brot][:, 0:ntok], rperm_bf, qg, start=True, stop=True),
                         reads=[qg_b, cbf_b], writes=[ps_b[brot]])

            def post_part():
                P.op("act", lambda e: e.activation(out=tmpf[:, 0, 0:ntok], in_=ps[bss][:, 0:ntok], func=AF.Ln,
                                                   bias=epsc[:, 0:1], scale=1.0 / 128),
                     reads=[ps_b[bss], eps_b], writes=[tmpf_b[0]])
                P.op("act", lambda e: e.activation(out=tmpf[:, 1, 0:ntok], in_=tmpf[:, 0, 0:ntok], func=AF.Exp, scale=-0.5),
                     reads=[tmpf_b[0]], writes=[tmpf_b[1]])
                if rope_off is not None:
                    P.op("pool", lambda e: e.tensor_tensor(out=tmpf[:, 2, 0:ntok], in0=qg, in1=cos_bf[:, rope_off:rope_off + ntok],
                                                            op=ALU.mult), reads=[qg_b, cbf_b], writes=[tmpf_b[2]])
                    P.op("dve", lambda e: e.tensor_tensor(out=tmpf[:, 3, 0:ntok], in0=ps[brot][:, 0:ntok],
                                                           in1=sin_bf[:, rope_off:rope_off + ntok], op=ALU.mult),
                         reads=[ps_b[brot], cbf_b], writes=[tmpf_b[3]])
                    P.op("dve", lambda e: e.tensor_tensor(out=tmpf[:, 2, 0:ntok], in0=tmpf[:, 2, 0:ntok], in1=tmpf[:, 3, 0:ntok],
                                                           op=ALU.add), reads=[tmpf_b[2], tmpf_b[3]], writes=[tmpf_b[2]])
                    P.op("dve", lambda e: e.tensor_tensor(out=dst, in0=tmpf[:, 2, 0:ntok], in1=tmpf[:, 1, 0:ntok], op=ALU.mult),
                         reads=[tmpf_b[2], tmpf_b[1]], writes=[dst_b])
                else:
                    P.op("dve", lambda e: e.tensor_tensor(out=dst, in0=qg, in1=tmpf[:, 1, 0:ntok], op=ALU.mult),
                         reads=[qg_b, tmpf_b[1]], writes=[dst_b])
            return pe_part, post_part

        def lat_block(t):
            return dict(xsl=lambda m, t=t: xT[:, m, t * 512:(t + 1) * 512], xb=[xT_b[m][t] for m in range(NCH)], ntok=512)
        ctx_block = dict(xsl=lambda m: cxT[:, m, :], xb=cxT_b, ntok=256)

        GROUPS = [(0, 2), (2, 2), (10, 1), (4, 2), (6, 2), (8, 2)]

        def ffn_requests(wi_d, wo_d, gi):
            t0, nt = GROUPS[gi]
            return ([W.request(wi_d[t0 + i]) for i in range(nt)], [W.request(wi_d[11 + t0 + i]) for i in range(nt)],
                    [W.request(wo_d[t0 + i]) for i in range(nt)])

        hT_all_v = arena[:, 0:18432].rearrange("p (k n) -> p k n", n=2304)

        def ffn_prepare(blocks):
            if "hoff" in blocks[0]:
                return
            hoff = 0
            for bi, b in enumerate(blocks):
                b["hoff"] = hoff
                b["hb"] = [Buf("hT%d_%d" % (bi, m)) for m in range(NCH)]
                hoff += b["ntok"]

        def ffn_mh_thunks(blocks, bi, aff_eng="pool"):
            b = blocks[bi]
            return mh_thunks(b["xsl"], b["xb"], b["ntok"], b["ia"], b["ish"],
                             lambda m, hoff=b["hoff"], n=b["ntok"]: hT_all_v[:, m, hoff:hoff + n], b["hb"], 6, aff_eng=aff_eng)

        def ffn(wi_d, wo_d, blocks, pre, with_ada=False, after_out=None, pre_block=None):
            hT_all = arena[:, 0:18432].rearrange("p (k n) -> p k n", n=2304)
            act = [arena[:, 18432 + i * 2048:18432 + (i + 1) * 2048].rearrange("p (k n) -> p k n", n=512) for i in range(2)]
            act_b = [[Buf("act%d_%d" % (i, j)) for j in range(4)] for i in range(2)]
            extra = xring
            if not with_ada:
                W.free.extend(extra)
            tiles = {0: pre}

            def ada_req4():
                got = []
                if with_ada:
                    for _ in range(4):
                        if ada_pending:
                            t = ada_pending.pop(0)
                            ada_request(t)
                            got.append(t)
                return got
            if with_ada:
                for t in ada_early:
                    ada_request(t)
            ada_now = ada_req4()
            tiles[1] = ffn_requests(wi_d, wo_d, 1)
            ffn_prepare(blocks)

            stage_b = {}

            def mh_a(bi):
                if bi >= len(blocks) or blocks[bi].get("built"):
                    stage_b[bi] = []
                    return []
                pre = pre_block(bi) if (pre_block is not None and bi > 0) else []
                th = ffn_mh_thunks(blocks, bi)
                stage_b[bi] = th[12:]
                return pre + th[:12]

            def mh_b(bi):
                return stage_b.pop(bi, [])
            side = []
            side_k = [1]

            def pop_side():
                for _ in range(side_k[0]):
                    if side:
                        side.pop(0)()
            cnt = [0]
            for gi, (t0, nt) in enumerate(GROUPS):
                hg, hu, ho = tiles[gi]
                cg = 2 * nt
                wg = [W.use(h) for h in hg]
                wu = [W.use(h) for h in hu]
                wo = [W.use(h) for h in ho]
                def ffn_in(b, par, wg=wg, wu=wu, cg=cg):
                    n, hoff = b["ntok"], b["hoff"]
                    for jj in range(cg):
                        c = cnt[0]
                        cnt[0] += 1
                        pg, pu = c % 2, 2 + c % 2
                        wgv, wg_b = wg[jj // 2]
                        wuv, wu_b = wu[jj // 2]
                        wg3, wu3 = kv(wgv, 256), kv(wuv, 256)
                        co = (jj % 2) * 128

                        def mmg(e, pg=pg, wg3=wg3, co=co):
                            for k in range(8):
                                ins = e.matmul(ps[pg][:, 0:n], wg3[:, k, co:co + 128], hT_all[:, k, hoff:hoff + n],
                                               start=(k == 0), stop=(k == 7))
                            return ins

                        def mmu(e, pu=pu, wu3=wu3, co=co):
                            for k in range(8):
                                ins = e.matmul(ps[pu][:, 0:n], wu3[:, k, co:co + 128], hT_all[:, k, hoff:hoff + n],
                                               start=(k == 0), stop=(k == 7))
                            return ins
                        P.op("pe", mmg, reads=[wg_b] + b["hb"], writes=[ps_b[pg]])
                        pop_side()
                        P.op("pe", mmu, reads=[wu_b] + b["hb"], writes=[ps_b[pu]])
                        pop_side()
                        sg = 4 + c % 2
                        P.op("act", lambda e, pg=pg, sg=sg: e.activation(out=tmpf[:, sg, 0:n], in_=ps[pg][:, 0:n], func=AF.Silu),
                             reads=[ps_b[pg]], writes=[tmpf_b[sg]])
                        P.op("dve", lambda e, pu=pu, sg=sg, jj=jj: e.tensor_tensor(out=act[par][:, jj, 0:n], in0=ps[pu][:, 0:n],
                                                                                     in1=tmpf[:, sg, 0:n], op=ALU.mult),
                             reads=[ps_b[pu], tmpf_b[sg]], writes=[act_b[par][jj]])

                def ffn_out(b, par, wo=wo, cg=cg, after_m=None):
                    n = b["ntok"]
                    for m in range(NCH):
                        po = 4 + m % 2

                        def mmo(e, m=m, po=po):
                            for jj in range(cg):
                                wo3 = kv(wo[jj // 2][0], 1024)
                                ins = e.matmul(ps[po][:, 0:n], wo3[:, jj % 2, m * 128:(m + 1) * 128], act[par][:, jj, 0:n],
                                               start=(jj == 0), stop=(jj == cg - 1))
                            return ins
                        P.op("pe", mmo, reads=[w[1] for w in wo] + act_b[par][0:cg], writes=[ps_b[po]])
                        pop_side()
                        P.op("dve", lambda e, m=m, po=po: e.scalar_tensor_tensor(out=b["xsl"](m), in0=ps[po][:, 0:n],
                                                                                  scalar=cols[:, b["ig"], m:m + 1],
                                                                                  in1=b["xsl"](m), op0=ALU.mult, op1=ALU.add),
                             reads=[ps_b[po], cols_b[b["ig"]], b["xb"][m]], writes=[b["xb"][m]])
                        if after_m is not None:
                            after_m(m)
                last = (gi == len(GROUPS) - 1)
                for bi, b in enumerate(blocks):
                    if gi == 0:
                        if bi == 0:
                            for th in mh_a(0) + mh_b(0):
                                th()
                            side.extend(mh_a(1) + mh_b(1) + mh_a(2))
                        else:
                            side.extend(mh_b(bi + 1))
                        side_k[0] = max(1, (len(side) + 6) // 7)
                    ffn_in(b, bi % 2)
                    while side:
                        side.pop(0)()
                    if gi == 0 and bi > 0:
                        side.extend(mh_a(bi + 2))
                        side_k[0] = max(1, (len(side) + 6) // 7)
                    if with_ada and gi == 0 and bi == 0:
                        while ada_early:
                            ada_consume(ada_early.pop(0))
                    if bi == min(2, len(blocks) - 1):
                        for t in ada_now:
                            ada_consume(t)
                        ada_now = []
                    if bi > 0:
                        ffn_out(blocks[bi - 1], (bi - 1) % 2)
                        while side:
                            side.pop(0)()
                        if last and after_out is not None:
                            side.extend(after_out(bi - 1))
                            side_k[0] = 1
                if last and after_out is not None:
                    fin = after_out(len(blocks) - 1)
                    fin_a, fin_b = fin[0::2], fin[1::2]

                    def fin_hook(m):
                        if m >= 3:
                            while side:
                                side.pop(0)()
                            if fin_a:
                                fin_a.pop(0)()
                    side_k[0] = 2
                    ffn_out(blocks[-1], (len(blocks) - 1) % 2, after_m=fin_hook)
                    for th in fin_a + fin_b:
                        th()
                else:
                    ffn_out(blocks[-1], (len(blocks) - 1) % 2)
                for h in hg + hu + ho:
                    W.done(h)
                ada_now = ada_req4()
                if gi + 2 < len(GROUPS):
                    tiles[gi + 2] = ffn_requests(wi_d, wo_d, gi + 2)
            for s in extra:
                W.free.remove(s)

        if stage >= 2:
            P.wait("pool", [ada_req[PRIO_TILE]["tok"]])
            pre = ffn_requests(w1i_d, w1o_d, 0)
            blocks = []
            for t in range(4):
                b = lat_block(t); b.update(ia=0, ish=1, ig=2); blocks.append(b)
            b = dict(ctx_block); b.update(ia=9, ish=10, ig=11); blocks.append(b)
            ffn(w1i_d, w1o_d, blocks, pre, with_ada=True, pre_block=lambda bi: load_block_thunks(bi, True))
        for t in ada_early + ada_pending:
            ada_request(t)
            ada_consume(t)

        kT = arena[:, 0:4608].rearrange("p (h n) -> p h n", n=2304)
        Vt = arena[:, 4608:9216].rearrange("p (s n) -> p s n", n=256)
        yfour = arena[:, 9216:17408].rearrange("p (g n) -> p g n", n=2048)
        UW = arena[:, 17408:33792].rearrange("p (g t n) -> p g t n", t=16, n=256)
        kT_b = [[Buf("kT%d_%d" % (h, t)) for t in range(5)] for h in range(2)]
        V_b = [Buf("V%d" % s) for s in range(18)]
        UW_b = [[Buf("UW%d_%d" % (g, i)) for i in range(16)] for g in range(4)]
        yf_b = [[Buf("yf%d_%d" % (g, t)) for t in range(4)] for g in range(4)]

        ffn2_pre = []
        ffn2_blocks = []
        for t in range(4):
            b = lat_block(t); b.update(ia=6, ish=7, ig=8); ffn2_blocks.append(b)

        if stage >= 3:
            p1_blocks = []
            b = dict(ctx_block); b.update(ia=12, ish=13, key0=0, kb=0, lat=None); p1_blocks.append(b)
            for t in range(4):
                b = lat_block(t); b.update(ia=3, ish=4, key0=256 + t * 512, kb=t + 1, lat=t); p1_blocks.append(b)

            def p1_req(b):
                r = [W.request(win_d[4]), W.request(win_d[5])]
                if b["lat"] is not None:
                    r += [W.request(win_d[6]), W.request(win_d[7])]
                return r
            reqs = {0: p1_req(p1_blocks[0]), 1: p1_req(p1_blocks[1])}
            P.barrier()
            h2 = [arena[:, 9216 + i * 4096:9216 + (i + 1) * 4096].rearrange("p (k n) -> p k n", n=512) for i in range(2)]
            h2_b = [[Buf("h2_%d_%d" % (i, m)) for m in range(NCH)] for i in range(2)]
            fT = [arena[:, 33792 + i * 512:33792 + (i + 1) * 512] for i in range(2)]
            fT_b = [Buf("fT0"), Buf("fT1")]
            vcnt = 0
            ucnt = 0
            def p1_mh(bi):
                b = p1_blocks[bi]
                make_hT(b["xsl"], b["xb"], b["ntok"], b["ia"], b["ish"],
                        lambda m, hh=h2[bi % 2], n=b["ntok"]: hh[:, m, 0:n], h2_b[bi % 2], 5,
                        aff_eng=("act" if bi == 0 else "pool"))
            p1_mh(0)
            for bi, b in enumerate(p1_blocks):
                n = b["ntok"]
                hh, hh_b = h2[bi % 2], h2_b[bi % 2]
                if bi + 1 < len(p1_blocks):
                    p1_mh(bi + 1)
                rq = reqs[bi]
                (wk, wk_b), (wv, wv_b) = W.use(rq[0]), W.use(rq[1])
                wk3, wv3 = kv(wk, 256), kv(wv, 256)
                for kh in range(2):
                    def mmk(e, kh=kh, hh=hh, n=n, wk3=wk3):
                        for k in range(8):
                            ins = e.matmul(ps[kh][:, 0:n], wk3[:, k, kh * 128:(kh + 1) * 128], hh[:, k, 0:n],
                                           start=(k == 0), stop=(k == 7))
                        return ins
                    P.op("pe", mmk, reads=[wk_b] + hh_b, writes=[ps_b[kh]])
                rope = (b["lat"] * 512) if b["lat"] is not None else None
                posts = [head_post(ps[kh][:, 0:n], ps_b[kh], n, V_KN, kT[:, kh, b["key0"]:b["key0"] + n], kT_b[kh][b["kb"]],
                                   rope, (6, 2)[kh], (7, 3)[kh]) for kh in range(2)]
                for i in range(n // 128):
                    s = b["key0"] // 128 + i
                    pv = 2 + vcnt % 2
                    vcnt += 1

                    def mmv(e, i=i, hh=hh, wv3=wv3, pv=pv):
                        for k in range(8):
                            ins = e.matmul(ps[pv][:, 0:256], hh[:, k, i * 128:(i + 1) * 128], wv3[:, k, :],
                                           start=(k == 0), stop=(k == 7))
                        return ins
                    P.op("pe", mmv, reads=[wv_b] + hh_b, writes=[ps_b[pv]])
                    P.op("act", lambda e, s=s, pv=pv: e.copy(out=Vt[:, s, :], in_=ps[pv][:, 0:256]), reads=[ps_b[pv]],
                         writes=[V_b[s]])
                W.done(rq[0]); W.done(rq[1])
                posts[0][0](); posts[0][1]()
                if b["lat"] is None:
                    posts[1][0](); posts[1][1]()
                else:
                    t = b["lat"]
                    wf = [W.use(rq[2]), W.use(rq[3])]

                    def mmf_op(g):
                        pf = 4 + g % 2
                        wf3 = kv(wf[g // 2][0], 256)
                        co = (g % 2) * 128

                        def mmf(e, pf=pf, hh=hh, wf3=wf3, co=co):
                            for k in range(8):
                                ins = e.matmul(ps[pf][:, 0:512], wf3[:, k, co:co + 128], hh[:, k, 0:512],
                                               start=(k == 0), stop=(k == 7))
                            return ins
                        P.op("pe", mmf, reads=[wf[g // 2][1]] + hh_b, writes=[ps_b[pf]])
                    def evac_f(g):
                        pf = 4 + g % 2
                        fs = g % 2
                        P.op("act", lambda e, pf=pf, fs=fs: e.copy(out=fT[fs], in_=ps[pf][:, 0:512]), reads=[ps_b[pf]],
                             writes=[fT_b[fs]])
                    mmf_op(0)
                    mmf_op(1)
                    evac_f(0)
                    posts[1][0](); posts[1][1]()
                    mmf_op(2)
                    for g in range(4):
                        if g > 0:
                            evac_f(g)
                        if g == 2:
                            mmf_op(3)
                        fs = g % 2
                        for i2 in range(2):
                            pu = 6 + ucnt % 2
                            ucnt += 1
                            ti = i2 * 8 + t * 2

                            def mmu2(e, fs=fs, i2=i2, pu=pu):
                                fpar = fT[fs].rearrange("p (j two) -> p two j", two=2)
                                for a in range(2):
                                    ins = e.matmul(ps[pu][:, a * 256:(a + 1) * 256], fpar[:, i2, a * 128:(a + 1) * 128], csc_bf,
                                                   start=True, stop=True)
                                return ins
                            P.op("pe", mmu2, reads=[fT_b[fs], cbf_b], writes=[ps_b[pu]])
                            P.op("act", lambda e, g=g, ti=ti, pu=pu: e.copy(
                                out=UW[:, g, ti:ti + 2, :], in_=ps[pu][:].rearrange("p (a b) -> p a b", b=256)),
                                reads=[ps_b[pu]], writes=[UW_b[g][ti], UW_b[g][ti + 1]])
                    W.done(rq[2]); W.done(rq[3])
                if bi + 2 < len(p1_blocks):
                    reqs[bi + 2] = p1_req(p1_blocks[bi + 2])

            P.barrier()
            tslots = [(ring[:, i // 2, (i % 2) * 1024:(i % 2 + 1) * 1024], Buf("tab%d" % i)) for i in range(8)]
            tab_ring = [sl for sl in W.free if any(sl is r for r in ring_slots[0:4])]
            assert len(tab_ring) == 4 and len(W.free) == 8 and not W.pending
            for sl in tab_ring:
                W.free.remove(sl)
            p2req = [dict(q=None, mid=[], o=[]) for _ in range(4)]
            p2req[0]["q"] = [W.request(win_d[i]) for i in range(4)]
            for t in range(4):
                r = p2req[t]
                if t + 1 < 4:
                    p2req[t + 1]["q"] = []
                for i in range(4):
                    if t + 1 < 4:
                        p2req[t + 1]["q"].append(W.request(win_d[i]))
                    r["mid"].append((W.request(wab_d[i]), W.request(win_d[8 + i]), W.request(wfb_d[i], 1024),
                                     W.request(win_d[12 + i])))
                r["o"] = [W.request(wo_d[i]) for i in range(4)]

            for mb in range(2):
                for ti in range(16):
                    idx = mb * 16 + ti
                    tap, tb = tslots[idx % 8]
                    P.dma("sp", lambda e, tap=tap, idx=idx: [e.dma_start(out=tap, in_=tabs_d[idx])], tb)
                    bk0 = 0 if ti < 8 else 4

                    def mmy(e, ti=ti, tap=tap, bk0=bk0):
                        for g in range(4):
                            e.matmul(ps[bk0 + g][:, 0:512], UW[:, g, ti, 0:128], tap[:, 0:512], start=(ti % 8 == 0), stop=False)
                            ins = e.matmul(ps[bk0 + g][:, 0:512], UW[:, g, ti, 128:256], tap[:, 512:1024], start=False,
                                           stop=(ti % 8 == 7))
                        return ins
                    P.op("pe", mmy, reads=[tb] + [UW_b[g][ti] for g in range(4)], writes=[ps_b[bk0 + g] for g in range(4)])
                for g in range(4):
                    ta = 4 + g % 2
                    P.op("act", lambda e, g=g, ta=ta: e.activation(out=tmpf[:, ta, :], in_=ps[g][:, 0:512], func=AF.Copy,
                                                                    scale=1.0 / 512.0),
                         reads=[ps_b[g]], writes=[tmpf_b[ta]])
                    P.op("dve", lambda e, g=g, ta=ta, mb=mb: e.scalar_tensor_tensor(
                        out=yfour[:, g, mb * 512:(mb + 1) * 512], in0=ps[4 + g][:, 0:512], scalar=1.0 / 512.0, in1=tmpf[:, ta, :],
                        op0=ALU.mult, op1=ALU.add), reads=[ps_b[4 + g], tmpf_b[ta]], writes=[yf_b[g][mb]])
                    P.op("dve", lambda e, g=g, ta=ta, mb=mb: e.scalar_tensor_tensor(
                        out=yfour[:, g, 1024 + mb * 512:1024 + (mb + 1) * 512], in0=ps[4 + g][:, 0:512], scalar=-1.0 / 512.0,
                        in1=tmpf[:, ta, :], op0=ALU.mult, op1=ALU.add), reads=[ps_b[4 + g], tmpf_b[ta]], writes=[yf_b[g][2 + mb]])

            P.barrier()
            W.free.extend(tab_ring)
            W._pump()
            base = 17408
            hq = [arena[:, base:base + 4096].rearrange("p (k n) -> p k n", n=512),
                  arena[:, 30720:34816].rearrange("p (k n) -> p k n", n=512)]
            qt = [arena[:, base + 4096:base + 8192].rearrange("p (k n) -> p k n", n=512),
                  cxT[:].rearrange("p k n -> p (k n)").bitcast(BF16).rearrange("p (k n) -> p k n", n=512)]
            mg = arena[:, base + 8192:base + 12288].rearrange("p (k n) -> p k n", n=512)
            PT = [arena[:, base + 12288 + i * 512:base + 12288 + (i + 1) * 512] for i in range(2)] + [pt2[:, 0, :], pt2[:, 1, :]]
            hq_b = [[Buf("hq%d_%d" % (i, m)) for m in range(NCH)] for i in range(2)]
            qt_b = [[Buf("qt%d_%d" % (i, m)) for m in range(NCH)] for i in range(2)]
            mg_b = [Buf("mg%d" % m) for m in range(NCH)]
            PT_b = [Buf("PT%d" % i) for i in range(4)]
            dacc, dacc_b = tmpf[:, 7, :], tmpf_b[7]
            daccB, daccB_b = tmpf[:, 4, :], tmpf_b[4]
            DEN_ROLE = "PPPDPPPDPPPDPPPDPP"

            def mm8(e, pb, w3, c0, rhs, nk):
                for k in range(nk):
                    ins = e.matmul(ps[pb][:, 0:512], w3[:, k, c0:c0 + 128], rhs(k), start=(k == 0), stop=(k == nk - 1))
                return ins

            def mh_steps(t, aff_eng="pool"):
                b = lat_block(t)
                hh, hh_b = hq[t % 2], hq_b[t % 2]
                return mh_thunks(b["xsl"], b["xb"], 512, 3, 4, lambda m: hh[:, m, :], hh_b, 7, sq_eng="dve", aff_eng=aff_eng)

            def q_mm(t, h):
                r = p2req[t]
                wq, wq_b = W.use(r["q"][h // 2])
                wq3 = kv(wq, 256)
                pq = 4 + h % 2
                co = (h % 2) * 128
                hh, hh_b = hq[t % 2], hq_b[t % 2]

                def mmq(e):
                    for k in range(8):
                        ins = e.matmul(ps[pq][:, 0:512], wq3[:, k, co:co + 128], hh[:, k, :], start=(k == 0), stop=(k == 7))
                    return ins
                P.op("pe", mmq, reads=[wq_b] + hh_b, writes=[ps_b[pq]])
                if h % 2 == 1:
                    W.done(r["q"][h // 2])
                return head_post(ps[pq][:, 0:512], ps_b[pq], 512, V_QN, qt[t % 2][:, h, :], qt_b[t % 2][h], t * 512, 3, 6)

            def attention(t, side):
                Q, Q_b = qt[t % 2], qt_b[t % 2]
                for h in range(8):
                    kvh = h // 4
                    po = h % 2
                    SB = [2, 4, 5]

                    def s_op(s, h=h, kvh=kvh):
                        pb = SB[s % 3]
                        kb = kT_b[kvh][0] if s < 2 else kT_b[kvh][1 + (s - 2) // 4]
                        P.op("pe", lambda e: e.matmul(ps[pb][:, 0:512], kT[:, kvh, s * 128:(s + 1) * 128], Q[:, h, :],
                                                      start=True, stop=True),
                             reads=[kb, Q_b[h]], writes=[ps_b[pb]])

                    def e_op(s):
                        pb = SB[s % 3]
                        pi = s % 4
                        P.op("act", lambda e: e.activation(out=PT[pi], in_=ps[pb][:, 0:512], func=AF.Exp, scale=ATTN_SCALE),
                             reads=[ps_b[pb]], writes=[PT_b[pi]])

                    pd = 3 if h % 2 == 0 else 6

                    def pv_op(s, kvh=kvh, po=po, pd=pd):
                        pi = s % 4
                        role = DEN_ROLE[s]
                        if role == "P":
                            def f(e):
                                e.matmul(ps[po][:, 0:512], Vt[:, s, kvh * 128:(kvh + 1) * 128], PT[pi], start=(s == 0), stop=(s == 17))
                                return e.matmul(ps[pd][:, 0:512], ones_bf, PT[pi], start=(s == 0), stop=False)
                            P.op("pe", f, reads=[V_b[s], PT_b[pi], cbf_b], writes=[ps_b[po], ps_b[pd]])
                            return
                        P.op("pe", lambda e: e.matmul(ps[po][:, 0:512], Vt[:, s, kvh * 128:(kvh + 1) * 128], PT[pi],
                                                      start=(s == 0), stop=(s == 17)),
                             reads=[V_b[s], PT_b[pi]], writes=[ps_b[po]])
                        if role == "D":
                            if s == DEN_ROLE.index("D"):
                                P.op("dve", lambda e: e.tensor_copy(out=dacc, in_=PT[pi]), reads=[PT_b[pi]], writes=[dacc_b])
                            else:
                                P.op("dve", lambda e: e.tensor_tensor(out=dacc, in0=dacc, in1=PT[pi], op=ALU.add),
                                     reads=[PT_b[pi], dacc_b], writes=[dacc_b])
                        elif s == 1:
                            pass
                        elif s == 2:
                            P.op("pool", lambda e: e.tensor_tensor(out=daccB, in0=PT[1], in1=PT[2], op=ALU.add),
                                 reads=[PT_b[1], PT_b[2]], writes=[daccB_b])
                        else:
                            P.op("pool", lambda e: e.tensor_tensor(out=daccB, in0=daccB, in1=PT[pi], op=ALU.add),
                                 reads=[PT_b[pi], daccB_b], writes=[daccB_b])
                    s_op(0); s_op(1)
                    for s in range(18):
                        e_op(s)
                        if s + 2 < 18:
                            s_op(s + 2)
                        pv_op(s)
                        if side:
                            side.pop(0)()
                    if "G" in DEN_ROLE:
                        P.op("dve", lambda e: e.tensor_tensor(out=dacc, in0=dacc, in1=daccB, op=ALU.add),
                             reads=[dacc_b, daccB_b], writes=[dacc_b])
                    P.op("pe", lambda e, pd=pd: e.matmul(ps[pd][:, 0:512], ones_f[:], dacc, start=False, stop=True),
                         reads=[dacc_b, onesf_b], writes=[ps_b[pd]])
                    if USE_APPROX_RECIP:
                        P.op("dve", lambda e, pd=pd: e.reciprocal_approx_accurate(tmpf[:, 6, :], ps[pd][:, 0:512], tmpf[:, 5, :]),
                             reads=[ps_b[pd]], writes=[tmpf_b[6], tmpf_b[5]])
                    else:
                        P.op("dve", lambda e, pd=pd: e.reciprocal(out=tmpf[:, 6, :], in_=ps[pd][:, 0:512]),
                             reads=[ps_b[pd]], writes=[tmpf_b[6]])
                    P.op("dve", lambda e, h=h, po=po: e.tensor_tensor(out=Q[:, h, :], in0=ps[po][:, 0:512], in1=tmpf[:, 6, :],
                                                                       op=ALU.mult),
                         reads=[ps_b[po], tmpf_b[6]], writes=[Q_b[h]])
                while side:
                    side.pop(0)()

            def merge_step(t, m):
                r = p2req[t]
                Y, Y_b = qt[t % 2], qt_b[t % 2]
                hh, hh_b = hq[t % 2], hq_b[t % 2]
                hab, hga, hfb, hgf = r["mid"][m // 2]
                (wab, wab_b), (wga, wga_b), (wfb, wfb_b), (wgf, wgf_b) = W.use(hab), W.use(hga), W.use(hfb), W.use(hgf)
                wab3, wga3, wfb3, wgf3 = kv(wab, 256), kv(wga, 256), kv(wfb, 256), kv(wgf, 256)
                c0 = (m % 2) * 128
                P.op("pe", lambda e: mm8(e, 0, wab3, c0, lambda k: Y[:, k, :], 8), reads=[wab_b] + Y_b, writes=[ps_b[0]])
                P.op("pe", lambda e: mm8(e, 1, wga3, c0, lambda k: hh[:, k, :], 8), reads=[wga_b] + hh_b, writes=[ps_b[1]])
                P.op("pe", lambda e: mm8(e, 2, wfb3, c0, lambda k: yfour[:, k, t * 512:(t + 1) * 512], 4),
                     reads=[wfb_b] + [yf_b[g][t] for g in range(4)], writes=[ps_b[2]])
                P.op("pe", lambda e: mm8(e, 7, wgf3, c0, lambda k: hh[:, k, :], 8), reads=[wgf_b] + hh_b, writes=[ps_b[7]])
                P.op("act", lambda e: e.activation(out=tmpf[:, 4, :], in_=ps[1][:, 0:512], func=AF.Sigmoid),
                     reads=[ps_b[1]], writes=[tmpf_b[4]])
                P.op("act", lambda e: e.activation(out=tmpf[:, 5, :], in_=ps[7][:, 0:512], func=AF.Sigmoid),
                     reads=[ps_b[7]], writes=[tmpf_b[5]])
                P.op("dve", lambda e: e.tensor_tensor(out=tmpf[:, 4, :], in0=ps[0][:, 0:512], in1=tmpf[:, 4, :], op=ALU.mult),
                     reads=[ps_b[0], tmpf_b[4]], writes=[tmpf_b[4]])
                P.op("dve", lambda e: e.tensor_tensor(out=tmpf[:, 5, :], in0=ps[2][:, 0:512], in1=tmpf[:, 5, :], op=ALU.mult),
                     reads=[ps_b[2], tmpf_b[5]], writes=[tmpf_b[5]])
                P.op("dve", lambda e: e.tensor_tensor(out=mg[:, m, :], in0=tmpf[:, 4, :], in1=tmpf[:, 5, :], op=ALU.add),
                     reads=[tmpf_b[4], tmpf_b[5]], writes=[mg_b[m]])
                if m % 2 == 1:
                    for hnd in r["mid"][m // 2]:
                        W.done(hnd)

            def outproj(t, early=()):
                r = p2req[t]
                b = lat_block(t)
                early = list(early)
                ek = (len(early) + 7) // 8
                for m in range(NCH):
                    for _ in range(ek):
                        if early:
                            early.pop(0)()
                    wo, wo_b = W.use(r["o"][m // 2])
                    wo3 = kv(wo, 256)
                    c0 = (m % 2) * 128
                    po = m % 2

                    def mmo(e, wo3=wo3, c0=c0, po=po):
                        for k in range(8):
                            ins = e.matmul(ps[po][:, 0:512], wo3[:, k, c0:c0 + 128], mg[:, k, :], start=(k == 0), stop=(k == 7))
                        return ins
                    P.op("pe", mmo, reads=[wo_b] + mg_b, writes=[ps_b[po]])
                    P.op("dve", lambda e, m=m, po=po: e.scalar_tensor_tensor(out=b["xsl"](m), in0=ps[po][:, 0:512],
                                                                              scalar=cols[:, 5, m:m + 1], in1=b["xsl"](m),
                                                                              op0=ALU.mult, op1=ALU.add),
                         reads=[ps_b[po], cols_b[5], b["xb"][m]], writes=[b["xb"][m]])
                    if m % 2 == 1:
                        W.done(r["o"][m // 2])

            for th in mh_steps(0, aff_eng="act"):
                th()
            prev = None
            for h in range(8):
                parts = q_mm(0, h)
                if prev is not None:
                    prev[0](); prev[1]()
                prev = parts
            prev[0](); prev[1]()
            for t in range(4):
                nxt = t + 1 < 4
                attention(t, mh_steps(t + 1) if nxt else [])
                prev = None
                for m in range(NCH):
                    if nxt:
                        parts = q_mm(t + 1, m)
                    merge_step(t, m)
                    if nxt:
                        if prev is not None:
                            prev[0](); prev[1]()
                        prev = parts
                if nxt:
                    prev[0](); prev[1]()
                early = []
                if t == 3 and stage >= 4:
                    ffn_prepare(ffn2_blocks)
                    ffn2_pre.append(ffn_requests(w2i_d, w2o_d, 0))
                    P.wait("pool", [("pe", P.cnt["pe"])])
                    for bi in (0, 1):
                        early += ffn_mh_thunks(ffn2_blocks, bi)
                        ffn2_blocks[bi]["built"] = True
                outproj(t, early)

        st_b = [Buf("store%d" % i) for i in range(8)]

        def emit_output(t):
            stiles = [6, 7, 0, 1, 2, 3] if (t < 3 and stage >= 4) else [6, 7, 0, 1, 2, 3, 4, 5]
            ths = []
            k = 0
            for i in range(t * 4, t * 4 + 4):
                for half in range(2):
                    pb = 6 + half
                    sti = stiles[k % len(stiles)]
                    k += 1

                    def th(i=i, half=half, pb=pb, sti=sti):
                        def tr(e):
                            for j in range(4):
                                m = half * 4 + j
                                ins = e.transpose(ps[pb][:, j * 128:(j + 1) * 128], xT[:, m, i * 128:(i + 1) * 128], ident[:])
                            return ins
                        P.op("pe", tr, reads=[xT_b[m][t] for m in range(half * 4, half * 4 + 4)] + [ident_b], writes=[ps_b[pb]])
                        if half == 0:
                            P.op("act", lambda e: e.copy(out=tmpf[:, sti, :], in_=ps[pb][:]), reads=[ps_b[pb]],
                                 writes=[tmpf_b[sti]])
                        else:
                            P.op("dve", lambda e: e.tensor_copy(out=tmpf[:, sti, :], in_=ps[pb][:]), reads=[ps_b[pb]],
                                 writes=[tmpf_b[sti]])
                        P.dma("sp", lambda e: [e.dma_start(
                            out=out_d[i * 128:(i + 1) * 128, half * 512:(half + 1) * 512], in_=tmpf[:, sti, :])],
                              st_b[sti], reads=[tmpf_b[sti]], writes=[])
                    ths.append(th)
            return ths

        if stage >= 4:
            pre = ffn2_pre[0] if ffn2_pre else ffn_requests(w2i_d, w2o_d, 0)
            P.barrier()
            ffn(w2i_d, w2o_d, ffn2_blocks, pre, after_out=emit_output)
        else:
            P.barrier()
            for t in range(4):
                for th in emit_output(t):
                    th()
        P.wait("sp", [(b.dsem, b.dcnt) for b in st_b if b.dsem is not None])
        P.build(st)
    return nc


_CACHE = {}


def _consts():
    if "c" in _CACHE:
        return _CACHE["c"]
    bf = ml_dtypes.bfloat16
    ident = np.eye(128, dtype=np.float32)
    cb = np.zeros((128, C_END), dtype=np.float32)
    cb[:, C_ONES:C_ONES + 128] = 1.0
    R = np.zeros((128, 128), dtype=np.float32)
    for m in range(64):
        R[m + 64, m] = -1.0
        R[m, m + 64] = 1.0
    cb[:, C_RPERM:C_RPERM + 128] = R
    cc = np.arange(128, dtype=np.float64)
    ang = 2.0 * np.pi * np.outer(cc, cc) / 128.0
    cb[:, C_CSC:C_CSC + 128] = np.cos(ang)
    cb[:, C_CSC + 128:C_CSC + 256] = np.sin(ang)
    rows = SEQ // 64
    row_ids = np.repeat(np.arange(rows, dtype=np.float32), 64)
    col_ids = np.tile(np.arange(64, dtype=np.float32), rows)
    inv_freq = (np.float32(10000.0) ** (-np.arange(0, 64, 2, dtype=np.float32) / np.float32(64))).astype(np.float32)
    angr = np.concatenate([row_ids[:, None] * inv_freq, col_ids[:, None] * inv_freq], axis=-1).astype(np.float32)
    cosT = np.cos(angr).T
    sinT = np.sin(angr).T
    cb[0:64, C_COS:C_COS + 2048] = cosT
    cb[64:128, C_COS:C_COS + 2048] = cosT
    cb[0:64, C_SIN:C_SIN + 2048] = sinT
    cb[64:128, C_SIN:C_SIN + 2048] = sinT
    cbf = cb.astype(bf)
    tabs = np.zeros((32, 128, 1024), dtype=np.float32)
    j = np.arange(128, dtype=np.int64)
    for mb in range(2):
        m = np.arange(mb * 512, (mb + 1) * 512, dtype=np.int64)
        for ti in range(16):
            n = 2 * (128 * (ti % 8) + j) + (1 if ti >= 8 else 0)
            ang = (np.outer(n, m) % SEQ).astype(np.float64) * (2.0 * np.pi / SEQ)
            tabs[mb * 16 + ti, :, 0:512] = np.cos(ang)
            tabs[mb * 16 + ti, :, 512:1024] = -np.sin(ang)
    _CACHE["c"] = (ident, cbf, tabs.astype(bf))
    return _CACHE["c"]


def _colz(v):
    v = np.asarray(v, dtype=np.float32).reshape(-1, 128)
    return np.ascontiguousarray(v.T)


def _pack_k(w, ncols=256):
    K, N = w.shape
    kc = K // 128
    t = w.reshape(kc, 128, N // ncols, ncols)
    return np.ascontiguousarray(t.transpose(2, 1, 0, 3)).reshape(N // ncols, 128, kc * ncols)


def _pack_o(w):
    t = w.reshape(11, 2, 128, 1024)
    return np.ascontiguousarray(t.transpose(0, 2, 1, 3)).reshape(11, 128, 2048)


def kernel(x, c, ctx, c_ctx, w_ada, b_ada, norm_ffn1, w_ffn1_in, w_ffn1_out, norm_mix, w_in, q_norm, k_norm,
           w_attn_branch, w_fourier_branch, w_out, norm_ffn2, w_ffn2_in, w_ffn2_out, _stage=9, _ncores=8):
    f = lambda a: np.ascontiguousarray(np.asarray(a, dtype=np.float32))
    ident, cbf, tabs = _consts()
    key = ("nc", _stage)
    if key not in _CACHE:
        _CACHE[key] = build_program(_stage)
    nc = _CACHE[key]
    x = f(x); ctx = f(ctx); c = f(c)
    shared = dict(ident=ident, cbf=cbf, tabs=tabs, w_ada=_pack_k(f(w_ada)[0]), w_ffn1_in=_pack_k(f(w_ffn1_in)[0]),
                  w_ffn1_out=_pack_o(f(w_ffn1_out)[0]), w_in=_pack_k(f(w_in)[0]), w_ab=_pack_k(f(w_attn_branch)[0]),
                  w_fb=_pack_k(f(w_fourier_branch)[0]), w_o=_pack_k(f(w_out)[0]),
                  w_ffn2_in=_pack_k(f(w_ffn2_in)[0]), w_ffn2_out=_pack_o(f(w_ffn2_out)[0]))
    in_maps = []
    for b in range(_ncores):
        vec = np.zeros((128, NV), dtype=np.float32)
        vec[:, V_C:V_C + 8] = _colz(c[b])
        vec[:, V_CC:V_CC + 8] = _colz(f(c_ctx))
        vec[:, V_BADA:V_BADA + 72] = _colz(f(b_ada)[0])
        vec[:, V_NF1:V_NF1 + 8] = _colz(f(norm_ffn1)[0])
        vec[:, V_NMIX:V_NMIX + 8] = _colz(f(norm_mix)[0])
        vec[:, V_NF2:V_NF2 + 8] = _colz(f(norm_ffn2)[0])
        vec[:, V_QN] = f(q_norm)[0]
        vec[:, V_KN] = f(k_norm)[0]
        m = dict(shared)
        m.update(x=x[b], ctx=ctx[b], vecs=vec)
        in_maps.append(m)
    res = run_bass_kernel_spmd(nc, in_maps, core_ids=list(range(_ncores)))
    return np.stack([np.asarray(r["out"], dtype=np.float32) for r in res.results], axis=0)
```

```python
import os
import numpy as np
import ml_dtypes
from contextlib import ExitStack
import concourse.bass as bass
import concourse.mybir as mybir
from concourse.bass_utils import run_bass_kernel_spmd

F32 = mybir.dt.float32
BF16 = mybir.dt.bfloat16
AF = mybir.ActivationFunctionType
ALU = mybir.AluOpType

D = 1024
SEQ = 2048
CTX = 256
DFF = 2816
NCH = 8
EPS = 1e-6
ATTN_SCALE = 128 ** -0.5
NV = 128
USE_APPROX_RECIP = False
PRIO_TILE = 7
V_C, V_CC, V_BADA, V_NF1, V_NMIX, V_NF2, V_QN, V_KN = 0, 8, 16, 88, 96, 104, 112, 113
C_ONES, C_RPERM, C_CSC, C_COS, C_SIN, C_END = 0, 128, 256, 512, 2560, 4608


class Buf:
    __slots__ = ("name", "w", "r", "dsem", "dcnt")

    def __init__(self, name=""):
        self.name = name
        self.w = None
        self.r = []
        self.dsem = None
        self.dcnt = 0


class Prog:
    ENG = ("pe", "act", "dve", "pool", "sp")

    def __init__(self, nc):
        self.nc = nc
        self.q = {k: [] for k in self.ENG}
        self.cnt = {k: 0 for k in self.ENG}
        self.waited = {k: {} for k in self.ENG}
        self.semkeys = list(self.ENG)
        self.sems = {}
        self.n_dsem = 0
        self.dbufs = []

    def _collect(self, eng, reads, writes, extra):
        need = {}

        def add(tok):
            if tok is None:
                return
            k, v = tok
            if need.get(k, 0) < v:
                need[k] = v
        for b in reads:
            add(b.w)
        for b in writes:
            add(b.w)
            for t in b.r:
                add(t)
        for t in extra:
            add(t)
        out = []
        wd = self.waited[eng]
        for k, v in need.items():
            if k == eng and eng == "pe":
                continue
            if wd.get(k, 0) >= v:
                continue
            wd[k] = v
            out.append((k, v))
        return out

    def _commit(self, tok, reads, writes):
        for b in writes:
            b.w = tok
            b.r = []
        for b in reads:
            b.r.append(tok)

    def op(self, eng, fn, reads=(), writes=(), extra=()):
        waits = self._collect(eng, reads, writes, extra)
        self.cnt[eng] += 1
        tok = (eng, self.cnt[eng])
        sems = self.sems

        def run(e, waits=waits, fn=fn, eng=eng):
            for k, v in waits:
                e.wait_ge(sems[k], v)
            ins = fn(e)
            ins.then_inc(sems[eng], 1)
        self.q[eng].append(run)
        self._commit(tok, reads, writes)
        return tok

    def dma(self, eng, fn, buf, reads=(), writes=None, n=1):
        if writes is None:
            writes = (buf,)
        waits = self._collect(eng, reads, writes, ())
        if buf.dsem is None:
            buf.dsem = ("d", self.n_dsem)
            self.n_dsem += 1
            self.semkeys.append(buf.dsem)
            self.dbufs.append(buf)
        buf.dcnt += 16 * n
        tok = (buf.dsem, buf.dcnt)
        sems = self.sems

        def run(e, waits=waits, fn=fn, key=buf.dsem):
            for k, v in waits:
                e.wait_ge(sems[k], v)
            for ins in fn(e):
                ins.then_inc(sems[key], 16)
        self.q[eng].append(run)
        self._commit(tok, reads, writes)
        return tok

    def wait(self, eng, toks):
        waits = self._collect(eng, (), (), toks)
        if not waits:
            return
        sems = self.sems

        def run(e, waits=waits):
            for k, v in waits:
                e.wait_ge(sems[k], v)
        self.q[eng].append(run)

    def barrier(self):
        toks = [(k, self.cnt[k]) for k in self.ENG if self.cnt[k] > 0]
        toks += [(b.dsem, b.dcnt) for b in self.dbufs]
        for eng in self.ENG:
            self.wait(eng, toks)

    def build(self, stack):
        nc = self.nc
        for k in self.semkeys:
            nm = k if isinstance(k, str) else "d%d" % k[1]
            self.sems[k] = stack.enter_context(nc.semaphore("s_" + nm))
        block = stack.enter_context(nc.Block())
        q = self.q

        @block.tensor
        def _(e):
            for c in q["pe"]:
                c(e)

        @block.scalar
        def _(e):
            for c in q["act"]:
                c(e)

        @block.vector
        def _(e):
            for c in q["dve"]:
                c(e)

        @block.gpsimd
        def _(e):
            for c in q["pool"]:
                c(e)

        @block.sync
        def _(e):
            for c in q["sp"]:
                c(e)


class WStream:
    def __init__(self, P, slots):
        self.P = P
        self.free = list(slots)
        self.pending = []

    def request(self, src, nel=2048):
        h = {"src": src, "nel": nel, "slot": None}
        self.pending.append(h)
        self._pump()
        return h

    def _pump(self):
        while self.pending and self.free:
            h = self.pending.pop(0)
            slot = self.free.pop(0)
            h["slot"] = slot
            ap, buf = slot
            view = ap[:, 0:h["nel"]]
            h["view"] = view
            src = h["src"]
            h["tok"] = self.P.dma("pool", lambda e, view=view, src=src: [e.dma_start(out=view, in_=src, max_dma_last_dim=8192)], buf)

    def use(self, h):
        assert h["slot"] is not None, "weight tile not issued (ring too small for access order)"
        return h["view"], h["slot"][1]

    def done(self, h):
        self.free.append(h["slot"])
        self._pump()


def kv(ap, c):
    return ap.rearrange("p (k c) -> p k c", c=c)


def build_program(stage=9):
    nc = bass.Bass("TRN2", target_bir_lowering=False)
    dt = nc.dram_tensor
    x_d = dt("x", [SEQ, D], F32, kind="ExternalInput").ap()
    ctx_d = dt("ctx", [CTX, D], F32, kind="ExternalInput").ap()
    vecs_d = dt("vecs", [128, NV], F32, kind="ExternalInput").ap()
    ident_d = dt("ident", [128, 128], F32, kind="ExternalInput").ap()
    cbf_d = dt("cbf", [128, C_END], BF16, kind="ExternalInput").ap()
    tabs_d = dt("tabs", [32, 128, 1024], BF16, kind="ExternalInput").ap()
    w_ada_d = dt("w_ada", [36, 128, 2048], F32, kind="ExternalInput").ap()
    w1i_d = dt("w_ffn1_in", [22, 128, 2048], F32, kind="ExternalInput").ap()
    w1o_d = dt("w_ffn1_out", [11, 128, 2048], F32, kind="ExternalInput").ap()
    win_d = dt("w_in", [16, 128, 2048], F32, kind="ExternalInput").ap()
    wab_d = dt("w_ab", [4, 128, 2048], F32, kind="ExternalInput").ap()
    wfb_d = dt("w_fb", [4, 128, 1024], F32, kind="ExternalInput").ap()
    wo_d = dt("w_o", [4, 128, 2048], F32, kind="ExternalInput").ap()
    w2i_d = dt("w_ffn2_in", [22, 128, 2048], F32, kind="ExternalInput").ap()
    w2o_d = dt("w_ffn2_out", [11, 128, 2048], F32, kind="ExternalInput").ap()
    out_d = dt("out", [SEQ, D], F32, kind="ExternalOutput").ap()

    with ExitStack() as st:
        P = Prog(nc)
        sb = lambda name, shape, dtype: st.enter_context(nc.sbuf_tensor(name, shape, dtype))
        xT = sb("xT", [128, NCH, SEQ], F32)
        cxT = sb("cxT", [128, NCH, CTX], F32)
        ring = sb("ring", [128, 8, 2048], BF16)
        arena = sb("arena", [128, 34816], BF16)
        tmpf = sb("tmpf", [128, 8, 512], F32)
        tbf = sb("tbf", [128, 4, 512], BF16)
        cbf = sb("cbf_sb", [128, C_END], BF16)
        ident = sb("ident_sb", [128, 128], F32)
        vecs = sb("vecs_sb", [128, NV], F32)
        mods = sb("mods", [128, 2, 72], F32)
        cols = sb("cols", [128, 16, 8], F32)
        csb = sb("csb", [128, 8, 2], BF16)
        csf = sb("csf", [128, 16], F32)
        epsc = sb("epsc", [128, 1], F32)
        pt2 = sb("pt2", [128, 2, 512], BF16)
        ones_f = sb("ones_f", [128, 128], F32)
        ps = [st.enter_context(nc.psum_tensor("ps%d" % i, [128, 512], F32)) for i in range(8)]

        ps_b = [Buf("ps%d" % i) for i in range(8)]
        tmpf_b = [Buf("tmpf%d" % i) for i in range(8)]
        tbf_b = [Buf("tbf%d" % i) for i in range(4)]
        xT_b = [[Buf("xT%d_%d" % (m, t)) for t in range(4)] for m in range(NCH)]
        cxT_b = [Buf("cxT%d" % m) for m in range(NCH)]
        ring_slots = [(ring[:, i, :], Buf("ring%d" % i)) for i in range(8)]
        cbf_b, ident_b, vecs_b, csb_b, eps_b = (Buf("cbf"), Buf("ident"), Buf("vecs"), Buf("csb"), Buf("eps"))
        mods_b = [Buf("mods%d" % i) for i in range(3)]
        cols_b = [Buf("cols%d" % i) for i in range(16)]
        ones_bf = cbf[:, C_ONES:C_ONES + 128]
        rperm_bf = cbf[:, C_RPERM:C_RPERM + 128]
        csc_bf = cbf[:, C_CSC:C_CSC + 256]
        cos_bf = cbf[:, C_COS:C_COS + 2048]
        sin_bf = cbf[:, C_SIN:C_SIN + 2048]

        W = WStream(P, ring_slots)
        xring = [(arena[:, 24576 + i * 2048:24576 + (i + 1) * 2048], Buf("xring%d" % i)) for i in range(4)]
        if stage >= 2:
            W.free.extend(xring)
        lstg = [(tmpf[:, 6, :], tmpf_b[6]), (tmpf[:, 7, :], tmpf_b[7])]
        for i, off in enumerate((22528, 23552, 32768, 33792)):
            lstg.append((arena[:, off:off + 1024].bitcast(F32), Buf("lstg%d" % i)))
        lstg_n = [0]

        P.dma("sp", lambda e: [e.dma_start(out=vecs[:], in_=vecs_d)], vecs_b)
        P.dma("sp", lambda e: [e.dma_start(out=ident[:], in_=ident_d)], ident_b)
        P.dma("sp", lambda e: [e.dma_start(out=cbf[:], in_=cbf_d)], cbf_b)
        P.op("dve", lambda e: e.memset(epsc[:], EPS), writes=[eps_b])
        onesf_b = Buf("onesf")
        P.op("dve", lambda e: e.memset(ones_f[:], 1.0), writes=[onesf_b])

        ada_req = {}

        def ada_request(t):
            ada_req[t] = W.request(w_ada_d[t])

        for t in range(8):
            ada_request(t)

        def load_tile_thunks(src_rows, dst_fn, dst_bufs, banks):
            th = []
            for half in range(2):
                pb = banks[half]
                sap, sbuf_ = lstg[lstg_n[0] % len(lstg)]
                lstg_n[0] += 1

                def Dm(half=half, sap=sap, sbuf_=sbuf_):
                    P.dma("sp", lambda e: [e.dma_start(out=sap, in_=src_rows[:, half * 512:(half + 1) * 512])], sbuf_)

                def Tr(half=half, pb=pb, sap=sap, sbuf_=sbuf_):
                    def tr(e):
                        for j in range(4):
                            ins = e.transpose(ps[pb][:, j * 128:(j + 1) * 128], sap[:, j * 128:(j + 1) * 128], ident[:])
                        return ins
                    P.op("pe", tr, reads=[sbuf_, ident_b], writes=[ps_b[pb]])
                    dst = dst_fn(half)
                    if half == 0 or pb == 6:
                        P.op("act", lambda e: e.copy(out=dst, in_=ps[pb][:].rearrange("p (a b) -> p a b", b=128)),
                             reads=[ps_b[pb]], writes=dst_bufs[half * 4:half * 4 + 4])
                    else:
                        P.op("dve", lambda e: e.tensor_copy(out=dst, in_=ps[pb][:].rearrange("p (a b) -> p a b", b=128)),
                             reads=[ps_b[pb]], writes=dst_bufs[half * 4:half * 4 + 4])
                th.append((Dm, Tr))
            return th

        def load_pairs(bi, banks):
            pairs = []
            if bi == 4:
                for i in range(2):
                    pairs += load_tile_thunks(ctx_d[i * 128:(i + 1) * 128, :],
                                              lambda half, i=i: cxT[:, half * 4:half * 4 + 4, i * 128:(i + 1) * 128], cxT_b, banks)
            else:
                for i in range(bi * 4, bi * 4 + 4):
                    pairs += load_tile_thunks(x_d[i * 128:(i + 1) * 128, :],
                                              lambda half, i=i: xT[:, half * 4:half * 4 + 4, i * 128:(i + 1) * 128],
                                              [xT_b[m][bi] for m in range(NCH)], banks)
            return pairs

        LA = len(lstg)

        all_pairs = {0: load_pairs(0, (0, 1))}
        if stage >= 2:
            for bi in range(1, 5):
                all_pairs[bi] = load_pairs(bi, (6, 6))
        else:
            for bi in range(1, 5):
                all_pairs[bi] = load_pairs(bi, (0, 1))

        def load_block_thunks(bi, first_issued):
            pairs = all_pairs[bi]
            la = min(LA, len(pairs))
            th = []
            if not first_issued:
                th.append(lambda: [pairs[k][0]() for k in range(la)])
            for k in range(len(pairs)):
                if k + la < len(pairs):
                    th.append(lambda k=k: (pairs[k][1](), pairs[k + la][0]()))
                else:
                    th.append(pairs[k][1])
            if bi + 1 in all_pairs:
                nxt = all_pairs[bi + 1]

                def issue_next():
                    if bi == 0 and stage >= 2:
                        P.wait("sp", [ada_req[PRIO_TILE]["tok"]])
                    for k in range(min(LA, len(nxt))):
                        nxt[k][0]()
                th.append(issue_next)
            return th

        for th in load_block_thunks(0, False):
            th()
        if stage < 2:
            for bi in range(1, 5):
                for th in load_block_thunks(bi, True):
                    th()

        P.op("act", lambda e: e.activation(out=csf[:], in_=vecs[:, V_C:V_C + 16], func=AF.Silu), reads=[vecs_b], writes=[csb_b])
        P.op("dve", lambda e: e.tensor_copy(out=csb[:, :, 0], in_=csf[:, 0:8]), reads=[csb_b], writes=[csb_b])
        P.op("dve", lambda e: e.tensor_copy(out=csb[:, :, 1], in_=csf[:, 8:16]), reads=[csb_b], writes=[csb_b])

        def mcol(r, i):
            return mods[:, r, i * 8:(i + 1) * 8]

        def derive(dst, r, isc, vnorm, mb):
            P.op("dve", lambda e: e.scalar_tensor_tensor(out=cols[:, dst, :], in0=mcol(r, isc), scalar=1.0,
                                                          in1=vecs[:, vnorm:vnorm + 8], op0=ALU.add, op1=ALU.mult),
                 reads=[mb, vecs_b], writes=[cols_b[dst]])

        def cpy(dst, r, i, mul, mb):
            P.op("dve", lambda e: e.tensor_scalar(out=cols[:, dst, :], in0=mcol(r, i), scalar1=float(mul), scalar2=None,
                                                   op0=ALU.mult), reads=[mb], writes=[cols_b[dst]])

        def ada_tile(t):
            wv, wb = W.use(ada_req[t])
            wv3 = kv(wv, 256)

            def mm(e, t=t, wv3=wv3):
                for j in range(2):
                    ch = t * 2 + j
                    for k in range(8):
                        ins = e.matmul(ps[7][:, 2 * ch:2 * ch + 2], wv3[:, k, j * 128:(j + 1) * 128], csb[:, k, :],
                                       start=(k == 0), stop=(k == 7))
                return ins
            P.op("pe", mm, reads=[wb, csb_b], writes=[ps_b[7]])
            W.done(ada_req[t])

        ADA_PARTS = {"0a": (0, 16), "0b": (16, 24), "1": (24, 48), "2": (48, 72)}
        mods_pb = {k: Buf("mods" + k) for k in ADA_PARTS}

        def ada_finish(part):
            c0, c1 = ADA_PARTS[part]
            mb = mods_pb[part]
            for r in range(2):
                P.op("dve", lambda e, r=r: e.tensor_tensor(out=mods[:, r, c0:c1],
                                                            in0=ps[7][:, 2 * c0:2 * c1].rearrange("p (c r) -> p c r", r=2)[:, :, r],
                                                            in1=vecs[:, V_BADA + c0:V_BADA + c1], op=ALU.add),
                     reads=[ps_b[7], vecs_b], writes=[mb])
            if part == "0a":
                derive(0, 0, 1, V_NF1, mb); cpy(1, 0, 0, 1.0, mb)
                derive(9, 1, 1, V_NF1, mb); cpy(10, 1, 0, 1.0, mb)
            elif part == "0b":
                cpy(2, 0, 2, 0.5, mb); cpy(11, 1, 2, 0.5, mb)
            elif part == "1":
                derive(3, 0, 4, V_NMIX, mb); cpy(4, 0, 3, 1.0, mb); cpy(5, 0, 5, 1.0, mb)
                derive(12, 1, 4, V_NMIX, mb); cpy(13, 1, 3, 1.0, mb)
            else:
                derive(6, 0, 7, V_NF2, mb); cpy(7, 0, 6, 1.0, mb); cpy(8, 0, 8, 0.5, mb)

        def ada_consume(t):
            ada_tile(t)
            if t == 7:
                ada_finish("0a")
            if t == 11:
                ada_finish("0b")
            if t == 23:
                ada_finish("1")
            if t == 35:
                ada_finish("2")

        for t in range(8):
            ada_consume(t)
        ada_early = list(range(8, 12))
        ada_pending = list(range(12, 36))

        def mh_thunks(xsl, xbufs, ntok, ia, ish, hsl, hbufs, statbank, sq_eng="act", aff_eng="pool"):
            def A(m):
                sq = tbf[:, m % 3, 0:ntok]
                if sq_eng == "act":
                    P.op("act", lambda e: e.activation(out=sq, in_=xsl(m), func=AF.Square), reads=[xbufs[m]],
                         writes=[tbf_b[m % 3]])
                else:
                    P.op("dve", lambda e: e.tensor_tensor(out=sq, in0=xsl(m), in1=xsl(m), op=ALU.mult), reads=[xbufs[m]],
                         writes=[tbf_b[m % 3]])

            def B(m):
                sq = tbf[:, m % 3, 0:ntok]
                P.op("pe", lambda e: e.matmul(ps[statbank][:, 0:ntok], ones_bf, sq, start=(m == 0), stop=(m == 7)),
                     reads=[tbf_b[m % 3], cbf_b], writes=[ps_b[statbank]])

            def lnexp():
                P.op("act", lambda e: e.activation(out=tmpf[:, 0, 0:ntok], in_=ps[statbank][:, 0:ntok], func=AF.Ln,
                                                   bias=epsc[:, 0:1], scale=1.0 / D),
                     reads=[ps_b[statbank], eps_b], writes=[tmpf_b[0]])
                P.op("act", lambda e: e.activation(out=tmpf[:, 1, 0:ntok], in_=tmpf[:, 0, 0:ntok], func=AF.Exp, scale=-0.5),
                     reads=[tmpf_b[0]], writes=[tmpf_b[1]])

            def pair(m):
                tt = 2 + m % 2
                P.op("dve", lambda e: e.tensor_tensor(out=tmpf[:, tt, 0:ntok], in0=xsl(m), in1=tmpf[:, 1, 0:ntok], op=ALU.mult),
                     reads=[xbufs[m], tmpf_b[1]], writes=[tmpf_b[tt]])
                if aff_eng == "act":
                    P.op("act", lambda e: e.activation(out=hsl(m), in_=tmpf[:, tt, 0:ntok], func=AF.Identity,
                                                       bias=cols[:, ish, m:m + 1], scale=cols[:, ia, m:m + 1]),
                         reads=[tmpf_b[tt], cols_b[ia], cols_b[ish]], writes=[hbufs[m]])
                else:
                    P.op(aff_eng, lambda e: e.tensor_scalar(out=hsl(m), in0=tmpf[:, tt, 0:ntok], scalar1=cols[:, ia, m:m + 1],
                                                            scalar2=cols[:, ish, m:m + 1], op0=ALU.mult, op1=ALU.add),
                         reads=[tmpf_b[tt], cols_b[ia], cols_b[ish]], writes=[hbufs[m]])
            th = [lambda: A(0), lambda: A(1), lambda: A(2)]
            for m in range(NCH):
                if m + 3 < NCH:
                    th.append(lambda m=m: (B(m), A(m + 3)))
                else:
                    th.append(lambda m=m: B(m))
            th.append(lnexp)
            for m in range(NCH):
                th.append(lambda m=m: pair(m))
            return th

        def make_hT(*args):
            for th in mh_thunks(*args):
                th()

        hp_cnt = [0]

        def head_post(psrc, psrc_b, ntok, vgain, dst, dst_b, rope_off, bss, brot):
            par = hp_cnt[0] % 2
            hp_cnt[0] += 1
            qg, qg_b = tbf[:, 2 * par, 0:ntok], tbf_b[2 * par]
            sq, sq_b = tbf[:, 2 * par + 1, 0:ntok], tbf_b[2 * par + 1]
            P.op("act", lambda e: e.activation(out=qg, in_=psrc, func=AF.Identity, scale=vecs[:, vgain:vgain + 1]),
                 reads=[psrc_b, vecs_b], writes=[qg_b])
            P.op("act", lambda e: e.activation(out=sq, in_=psrc, func=AF.Square), reads=[psrc_b], writes=[sq_b])

            def pe_part():
                P.op("pe", lambda e: e.matmul(ps[bss][:, 0:ntok], ones_bf, sq, start=True, stop=True),
                     reads=[sq_b, cbf_b], writes=[ps_b[bss]])
                if rope_off is not None:
                    P.op("pe", lambda e: e.matmul(ps[brot][:, 0:ntok], rperm_bf, qg, start=True, stop=True),
                         reads=[qg_b, cbf_b], writes=[ps_b[brot]])

            def post_part():
                P.op("act", lambda e: e.activation(out=tmpf[:, 0, 0:ntok], in_=ps[bss][:, 0:ntok], func=AF.Ln,
                                                   bias=epsc[:, 0:1], scale=1.0 / 128),
                     reads=[ps_b[bss], eps_b], writes=[tmpf_b[0]])
                P.op("act", lambda e: e.activation(out=tmpf[:, 1, 0:ntok], in_=tmpf[:, 0, 0:ntok], func=AF.Exp, scale=-0.5),
                     reads=[tmpf_b[0]], writes=[tmpf_b[1]])
                if rope_off is not None:
                    P.op("pool", lambda e: e.tensor_tensor(out=tmpf[:, 2, 0:ntok], in0=qg, in1=cos_bf[:, rope_off:rope_off + ntok],
                                                            op=ALU.mult), reads=[qg_b, cbf_b], writes=[tmpf_b[2]])
                    P.op("dve", lambda e: e.tensor_tensor(out=tmpf[:, 3, 0:ntok], in0=ps[brot][:, 0:ntok],
                                                           in1=sin_bf[:, rope_off:rope_off + ntok], op=ALU.mult),
                         reads=[ps_b[brot], cbf_b], writes=[tmpf_b[3]])
                    P.op("dve", lambda e: e.tensor_tensor(out=tmpf[:, 2, 0:ntok], in0=tmpf[:, 2, 0:ntok], in1=tmpf[:, 3, 0:ntok],
                                                           op=ALU.add), reads=[tmpf_b[2], tmpf_b[3]], writes=[tmpf_b[2]])
                    P.op("dve", lambda e: e.tensor_tensor(out=dst, in0=tmpf[:, 2, 0:ntok], in1=tmpf[:, 1, 0:ntok], op=ALU.mult),
                         reads=[tmpf_b[2], tmpf_b[1]], writes=[dst_b])
                else:
                    P.op("dve", lambda e: e.tensor_tensor(out=dst, in0=qg, in1=tmpf[:, 1, 0:ntok], op=ALU.mult),
                         reads=[qg_b, tmpf_b[1]], writes=[dst_b])
            return pe_part, post_part

        def lat_block(t):
            return dict(xsl=lambda m, t=t: xT[:, m, t * 512:(t + 1) * 512], xb=[xT_b[m][t] for m in range(NCH)], ntok=512)
        ctx_block = dict(xsl=lambda m: cxT[:, m, :], xb=cxT_b, ntok=256)

        GROUPS = [(0, 2), (2, 2), (10, 1), (4, 2), (6, 2), (8, 2)]

        def ffn_requests(wi_d, wo_d, gi):
            t0, nt = GROUPS[gi]
            return ([W.request(wi_d[t0 + i]) for i in range(nt)], [W.request(wi_d[11 + t0 + i]) for i in range(nt)],
                    [W.request(wo_d[t0 + i]) for i in range(nt)])

        hT_all_v = arena[:, 0:18432].rearrange("p (k n) -> p k n", n=2304)

        def ffn_prepare(blocks):
            if "hoff" in blocks[0]:
                return
            hoff = 0
            for bi, b in enumerate(blocks):
                b["hoff"] = hoff
                b["hb"] = [Buf("hT%d_%d" % (bi, m)) for m in range(NCH)]
                hoff += b["ntok"]

        def ffn_mh_thunks(blocks, bi):
            b = blocks[bi]
            return mh_thunks(b["xsl"], b["xb"], b["ntok"], b["ia"], b["ish"],
                             lambda m, hoff=b["hoff"], n=b["ntok"]: hT_all_v[:, m, hoff:hoff + n], b["hb"], 6)

        def ffn(wi_d, wo_d, blocks, pre, with_ada=False, after_out=None, pre_block=None):
            hT_all = arena[:, 0:18432].rearrange("p (k n) -> p k n", n=2304)
            act = [arena[:, 18432 + i * 2048:18432 + (i + 1) * 2048].rearrange("p (k n) -> p k n", n=512) for i in range(2)]
            act_b = [[Buf("act%d_%d" % (i, j)) for j in range(4)] for i in range(2)]
            extra = xring
            if not with_ada:
                W.free.extend(extra)
            tiles = {0: pre}

            def ada_req4():
                got = []
                if with_ada:
                    for _ in range(4):
                        if ada_pending:
                            t = ada_pending.pop(0)
                            ada_request(t)
                            got.append(t)
                return got
            if with_ada:
                for t in ada_early:
                    ada_request(t)
            ada_now = ada_req4()
            tiles[1] = ffn_requests(wi_d, wo_d, 1)
            ffn_prepare(blocks)

            stage_b = {}

            def mh_a(bi):
                if bi >= len(blocks) or blocks[bi].get("built"):
                    stage_b[bi] = []
                    return []
                pre = pre_block(bi) if (pre_block is not None and bi > 0) else []
                th = ffn_mh_thunks(blocks, bi)
                stage_b[bi] = th[12:]
                return pre + th[:12]

            def mh_b(bi):
                return stage_b.pop(bi, [])
            side = []
            side_k = [1]

            def pop_side():
                for _ in range(side_k[0]):
                    if side:
                        side.pop(0)()
            cnt = [0]
            for gi, (t0, nt) in enumerate(GROUPS):
                hg, hu, ho = tiles[gi]
                cg = 2 * nt
                wg = [W.use(h) for h in hg]
                wu = [W.use(h) for h in hu]
                wo = [W.use(h) for h in ho]
                def ffn_in(b, par, wg=wg, wu=wu, cg=cg):
                    n, hoff = b["ntok"], b["hoff"]
                    for jj in range(cg):
                        c = cnt[0]
                        cnt[0] += 1
                        pg, pu = c % 2, 2 + c % 2
                        wgv, wg_b = wg[jj // 2]
                        wuv, wu_b = wu[jj // 2]
                        wg3, wu3 = kv(wgv, 256), kv(wuv, 256)
                        co = (jj % 2) * 128

                        def mmg(e, pg=pg, wg3=wg3, co=co):
                            for k in range(8):
                                ins = e.matmul(ps[pg][:, 0:n], wg3[:, k, co:co + 128], hT_all[:, k, hoff:hoff + n],
                                               start=(k == 0), stop=(k == 7))
                            return ins

                        def mmu(e, pu=pu, wu3=wu3, co=co):
                            for k in range(8):
                                ins = e.matmul(ps[pu][:, 0:n], wu3[:, k, co:co + 128], hT_all[:, k, hoff:hoff + n],
                                               start=(k == 0), stop=(k == 7))
                            return ins
                        P.op("pe", mmg, reads=[wg_b] + b["hb"], writes=[ps_b[pg]])
                        pop_side()
                        P.op("pe", mmu, reads=[wu_b] + b["hb"], writes=[ps_b[pu]])
                        pop_side()
                        sg = 4 + c % 2
                        P.op("act", lambda e, pg=pg, sg=sg: e.activation(out=tmpf[:, sg, 0:n], in_=ps[pg][:, 0:n], func=AF.Silu),
                             reads=[ps_b[pg]], writes=[tmpf_b[sg]])
                        P.op("dve", lambda e, pu=pu, sg=sg, jj=jj: e.tensor_tensor(out=act[par][:, jj, 0:n], in0=ps[pu][:, 0:n],
                                                                                     in1=tmpf[:, sg, 0:n], op=ALU.mult),
                             reads=[ps_b[pu], tmpf_b[sg]], writes=[act_b[par][jj]])

                def ffn_out(b, par, wo=wo, cg=cg, after_m=None):
                    n = b["ntok"]
                    for m in range(NCH):
                        po = 4 + m % 2

                        def mmo(e, m=m, po=po):
                            for jj in range(cg):
                                wo3 = kv(wo[jj // 2][0], 1024)
                                ins = e.matmul(ps[po][:, 0:n], wo3[:, jj % 2, m * 128:(m + 1) * 128], act[par][:, jj, 0:n],
                                               start=(jj == 0), stop=(jj == cg - 1))
                            return ins
                        P.op("pe", mmo, reads=[w[1] for w in wo] + act_b[par][0:cg], writes=[ps_b[po]])
                        pop_side()
                        P.op("dve", lambda e, m=m, po=po: e.scalar_tensor_tensor(out=b["xsl"](m), in0=ps[po][:, 0:n],
                                                                                  scalar=cols[:, b["ig"], m:m + 1],
                                                                                  in1=b["xsl"](m), op0=ALU.mult, op1=ALU.add),
                             reads=[ps_b[po], cols_b[b["ig"]], b["xb"][m]], writes=[b["xb"][m]])
                        if after_m is not None:
                            after_m(m)
                last = (gi == len(GROUPS) - 1)
                for bi, b in enumerate(blocks):
                    if gi == 0:
                        if bi == 0:
                            for th in mh_a(0) + mh_b(0):
                                th()
                            side.extend(mh_a(1) + mh_b(1) + mh_a(2))
                        else:
                            side.extend(mh_b(bi + 1))
                        side_k[0] = max(1, (len(side) + 6) // 7)
                    ffn_in(b, bi % 2)
                    while side:
                        side.pop(0)()
                    if gi == 0 and bi > 0:
                        side.extend(mh_a(bi + 2))
                        side_k[0] = max(1, (len(side) + 6) // 7)
                    if with_ada and gi == 0 and bi == 0:
                        while ada_early:
                            ada_consume(ada_early.pop(0))
                    if bi == min(2, len(blocks) - 1):
                        for t in ada_now:
                            ada_consume(t)
                        ada_now = []
                    if bi > 0:
                        ffn_out(blocks[bi - 1], (bi - 1) % 2)
                        while side:
                            side.pop(0)()
                        if last and after_out is not None:
                            side.extend(after_out(bi - 1))
                            side_k[0] = 1
                if last and after_out is not None:
                    fin = after_out(len(blocks) - 1)
                    fin_a, fin_b = fin[0::2], fin[1::2]

                    def fin_hook(m):
                        if m >= 3:
                            while side:
                                side.pop(0)()
                            if fin_a:
                                fin_a.pop(0)()
                    side_k[0] = 2
                    ffn_out(blocks[-1], (len(blocks) - 1) % 2, after_m=fin_hook)
                    for th in fin_a + fin_b:
                        th()
                else:
                    ffn_out(blocks[-1], (len(blocks) - 1) % 2)
                for h in hg + hu + ho:
                    W.done(h)
                ada_now = ada_req4()
                if gi + 2 < len(GROUPS):
                    tiles[gi + 2] = ffn_requests(wi_d, wo_d, gi + 2)
            for s in extra:
                W.free.remove(s)

        if stage >= 2:
            P.wait("pool", [ada_req[PRIO_TILE]["tok"]])
            pre = ffn_requests(w1i_d, w1o_d, 0)
            blocks = []
            for t in range(4):
                b = lat_block(t); b.update(ia=0, ish=1, ig=2); blocks.append(b)
            b = dict(ctx_block); b.update(ia=9, ish=10, ig=11); blocks.append(b)
            ffn(w1i_d, w1o_d, blocks, pre, with_ada=True, pre_block=lambda bi: load_block_thunks(bi, True))
        for t in ada_early + ada_pending:
            ada_request(t)
            ada_consume(t)

        kT = arena[:, 0:4608].rearrange("p (h n) -> p h n", n=2304)
        Vt = arena[:, 4608:9216].rearrange("p (s n) -> p s n", n=256)
        yfour = arena[:, 9216:17408].rearrange("p (g n) -> p g n", n=2048)
        UW = arena[:, 17408:33792].rearrange("p (g t n) -> p g t n", t=16, n=256)
        kT_b = [[Buf("kT%d_%d" % (h, t)) for t in range(5)] for h in range(2)]
        V_b = [Buf("V%d" % s) for s in range(18)]
        UW_b = [[Buf("UW%d_%d" % (g, i)) for i in range(16)] for g in range(4)]
        yf_b = [[Buf("yf%d_%d" % (g, t)) for t in range(4)] for g in range(4)]

        ffn2_pre = []
        ffn2_blocks = []
        for t in range(4):
            b = lat_block(t); b.update(ia=6, ish=7, ig=8); ffn2_blocks.append(b)

        if stage >= 3:
            p1_blocks = []
            b = dict(ctx_block); b.update(ia=12, ish=13, key0=0, kb=0, lat=None); p1_blocks.append(b)
            for t in range(4):
                b = lat_block(t); b.update(ia=3, ish=4, key0=256 + t * 512, kb=t + 1, lat=t); p1_blocks.append(b)

            def p1_req(b):
                r = [W.request(win_d[4]), W.request(win_d[5])]
                if b["lat"] is not None:
                    r += [W.request(win_d[6]), W.request(win_d[7])]
                return r
            reqs = {0: p1_req(p1_blocks[0]), 1: p1_req(p1_blocks[1])}
            P.barrier()
            h2 = [arena[:, 9216 + i * 4096:9216 + (i + 1) * 4096].rearrange("p (k n) -> p k n", n=512) for i in range(2)]
            h2_b = [[Buf("h2_%d_%d" % (i, m)) for m in range(NCH)] for i in range(2)]
            fT = [arena[:, 33792 + i * 512:33792 + (i + 1) * 512] for i in range(2)]
            fT_b = [Buf("fT0"), Buf("fT1")]
            vcnt = 0
            ucnt = 0
            def p1_mh(bi):
                b = p1_blocks[bi]
                make_hT(b["xsl"], b["xb"], b["ntok"], b["ia"], b["ish"],
                        lambda m, hh=h2[bi % 2], n=b["ntok"]: hh[:, m, 0:n], h2_b[bi % 2], 5)
            p1_mh(0)
            for bi, b in enumerate(p1_blocks):
                n = b["ntok"]
                hh, hh_b = h2[bi % 2], h2_b[bi % 2]
                if bi + 1 < len(p1_blocks):
                    p1_mh(bi + 1)
                rq = reqs[bi]
                (wk, wk_b), (wv, wv_b) = W.use(rq[0]), W.use(rq[1])
                wk3, wv3 = kv(wk, 256), kv(wv, 256)
                for kh in range(2):
                    def mmk(e, kh=kh, hh=hh, n=n, wk3=wk3):
                        for k in range(8):
                            ins = e.matmul(ps[kh][:, 0:n], wk3[:, k, kh * 128:(kh + 1) * 128], hh[:, k, 0:n],
                                           start=(k == 0), stop=(k == 7))
                        return ins
                    P.op("pe", mmk, reads=[wk_b] + hh_b, writes=[ps_b[kh]])
                rope = (b["lat"] * 512) if b["lat"] is not None else None
                posts = [head_post(ps[kh][:, 0:n], ps_b[kh], n, V_KN, kT[:, kh, b["key0"]:b["key0"] + n], kT_b[kh][b["kb"]],
                                   rope, (6, 2)[kh], (7, 3)[kh]) for kh in range(2)]
                for i in range(n // 128):
                    s = b["key0"] // 128 + i
                    pv = 2 + vcnt % 2
                    vcnt += 1

                    def mmv(e, i=i, hh=hh, wv3=wv3, pv=pv):
                        for k in range(8):
                            ins = e.matmul(ps[pv][:, 0:256], hh[:, k, i * 128:(i + 1) * 128], wv3[:, k, :],
                                           start=(k == 0), stop=(k == 7))
                        return ins
                    P.op("pe", mmv, reads=[wv_b] + hh_b, writes=[ps_b[pv]])
                    P.op("act", lambda e, s=s, pv=pv: e.copy(out=Vt[:, s, :], in_=ps[pv][:, 0:256]), reads=[ps_b[pv]],
                         writes=[V_b[s]])
                W.done(rq[0]); W.done(rq[1])
                posts[0][0](); posts[0][1]()
                if b["lat"] is None:
                    posts[1][0](); posts[1][1]()
                else:
                    t = b["lat"]
                    wf = [W.use(rq[2]), W.use(rq[3])]

                    def mmf_op(g):
                        pf = 4 + g % 2
                        wf3 = kv(wf[g // 2][0], 256)
                        co = (g % 2) * 128

                        def mmf(e, pf=pf, hh=hh, wf3=wf3, co=co):
                            for k in range(8):
                                ins = e.matmul(ps[pf][:, 0:512], wf3[:, k, co:co + 128], hh[:, k, 0:512],
                                               start=(k == 0), stop=(k == 7))
                            return ins
                        P.op("pe", mmf, reads=[wf[g // 2][1]] + hh_b, writes=[ps_b[pf]])
                    def evac_f(g):
                        pf = 4 + g % 2
                        fs = g % 2
                        P.op("act", lambda e, pf=pf, fs=fs: e.copy(out=fT[fs], in_=ps[pf][:, 0:512]), reads=[ps_b[pf]],
                             writes=[fT_b[fs]])
                    mmf_op(0)
                    mmf_op(1)
                    evac_f(0)
                    posts[1][0](); posts[1][1]()
                    mmf_op(2)
                    for g in range(4):
                        if g > 0:
                            evac_f(g)
                        if g == 2:
                            mmf_op(3)
                        fs = g % 2
                        for i2 in range(2):
                            pu = 6 + ucnt % 2
                            ucnt += 1
                            ti = i2 * 8 + t * 2

                            def mmu2(e, fs=fs, i2=i2, pu=pu):
                                fpar = fT[fs].rearrange("p (j two) -> p two j", two=2)
                                for a in range(2):
                                    ins = e.matmul(ps[pu][:, a * 256:(a + 1) * 256], fpar[:, i2, a * 128:(a + 1) * 128], csc_bf,
                                                   start=True, stop=True)
                                return ins
                            P.op("pe", mmu2, reads=[fT_b[fs], cbf_b], writes=[ps_b[pu]])
                            P.op("act", lambda e, g=g, ti=ti, pu=pu: e.copy(
                                out=UW[:, g, ti:ti + 2, :], in_=ps[pu][:].rearrange("p (a b) -> p a b", b=256)),
                                reads=[ps_b[pu]], writes=[UW_b[g][ti], UW_b[g][ti + 1]])
                    W.done(rq[2]); W.done(rq[3])
                if bi + 2 < len(p1_blocks):
                    reqs[bi + 2] = p1_req(p1_blocks[bi + 2])

            P.barrier()
            tslots = [(ring[:, i // 2, (i % 2) * 1024:(i % 2 + 1) * 1024], Buf("tab%d" % i)) for i in range(8)]
            tab_ring = [sl for sl in W.free if any(sl is r for r in ring_slots[0:4])]
            assert len(tab_ring) == 4 and len(W.free) == 8 and not W.pending
            for sl in tab_ring:
                W.free.remove(sl)
            p2req = [dict(q=None, mid=[], o=[]) for _ in range(4)]
            p2req[0]["q"] = [W.request(win_d[i]) for i in range(4)]
            for t in range(4):
                r = p2req[t]
                if t + 1 < 4:
                    p2req[t + 1]["q"] = []
                for i in range(4):
                    if t + 1 < 4:
                        p2req[t + 1]["q"].append(W.request(win_d[i]))
                    r["mid"].append((W.request(wab_d[i]), W.request(win_d[8 + i]), W.request(wfb_d[i], 1024),
                                     W.request(win_d[12 + i])))
                r["o"] = [W.request(wo_d[i]) for i in range(4)]

            def tab_dma(idx):
                tap, tb = tslots[idx % 8]
                P.dma("sp" if idx % 2 == 0 else "act", lambda e, tap=tap, idx=idx: [e.dma_start(out=tap, in_=tabs_d[idx])], tb)
            for idx in range(8):
                tab_dma(idx)
            for mb in range(2):
                for ti in range(16):
                    idx = mb * 16 + ti
                    tap, tb = tslots[idx % 8]
                    bk0 = 0 if ti < 8 else 4

                    def mmy(e, ti=ti, tap=tap, bk0=bk0):
                        for g in range(4):
                            e.matmul(ps[bk0 + g][:, 0:512], UW[:, g, ti, 0:128], tap[:, 0:512], start=(ti % 8 == 0), stop=False)
                            ins = e.matmul(ps[bk0 + g][:, 0:512], UW[:, g, ti, 128:256], tap[:, 512:1024], start=False,
                                           stop=(ti % 8 == 7))
                        return ins
                    P.op("pe", mmy, reads=[tb] + [UW_b[g][ti] for g in range(4)], writes=[ps_b[bk0 + g] for g in range(4)])
                    if idx + 8 < 32:
                        tab_dma(idx + 8)
                for g in range(4):
                    ta = 4 + g % 2
                    P.op("act", lambda e, g=g, ta=ta: e.activation(out=tmpf[:, ta, :], in_=ps[g][:, 0:512], func=AF.Copy,
                                                                    scale=1.0 / 512.0),
                         reads=[ps_b[g]], writes=[tmpf_b[ta]])
                    P.op("dve", lambda e, g=g, ta=ta, mb=mb: e.scalar_tensor_tensor(
                        out=yfour[:, g, mb * 512:(mb + 1) * 512], in0=ps[4 + g][:, 0:512], scalar=1.0 / 512.0, in1=tmpf[:, ta, :],
                        op0=ALU.mult, op1=ALU.add), reads=[ps_b[4 + g], tmpf_b[ta]], writes=[yf_b[g][mb]])
                    P.op("dve", lambda e, g=g, ta=ta, mb=mb: e.scalar_tensor_tensor(
                        out=yfour[:, g, 1024 + mb * 512:1024 + (mb + 1) * 512], in0=ps[4 + g][:, 0:512], scalar=-1.0 / 512.0,
                        in1=tmpf[:, ta, :], op0=ALU.mult, op1=ALU.add), reads=[ps_b[4 + g], tmpf_b[ta]], writes=[yf_b[g][2 + mb]])

            P.barrier()
            W.free.extend(tab_ring)
            W._pump()
            base = 17408
            hq = [arena[:, base:base + 4096].rearrange("p (k n) -> p k n", n=512),
                  arena[:, 30720:34816].rearrange("p (k n) -> p k n", n=512)]
            qt = [arena[:, base + 4096:base + 8192].rearrange("p (k n) -> p k n", n=512),
                  cxT[:].rearrange("p k n -> p (k n)").bitcast(BF16).rearrange("p (k n) -> p k n", n=512)]
            mg = arena[:, base + 8192:base + 12288].rearrange("p (k n) -> p k n", n=512)
            PT = [arena[:, base + 12288 + i * 512:base + 12288 + (i + 1) * 512] for i in range(2)] + [pt2[:, 0, :], pt2[:, 1, :]]
            hq_b = [[Buf("hq%d_%d" % (i, m)) for m in range(NCH)] for i in range(2)]
            qt_b = [[Buf("qt%d_%d" % (i, m)) for m in range(NCH)] for i in range(2)]
            mg_b = [Buf("mg%d" % m) for m in range(NCH)]
            PT_b = [Buf("PT%d" % i) for i in range(4)]
            dacc, dacc_b = tmpf[:, 7, :], tmpf_b[7]
            daccB, daccB_b = tmpf[:, 4, :], tmpf_b[4]
            DEN_ROLE = "PPPDPPPDPPPDPPPDPP"

            def mm8(e, pb, w3, c0, rhs, nk):
                for k in range(nk):
                    ins = e.matmul(ps[pb][:, 0:512], w3[:, k, c0:c0 + 128], rhs(k), start=(k == 0), stop=(k == nk - 1))
                return ins

            def mh_steps(t):
                b = lat_block(t)
                hh, hh_b = hq[t % 2], hq_b[t % 2]
                return mh_thunks(b["xsl"], b["xb"], 512, 3, 4, lambda m: hh[:, m, :], hh_b, 7, sq_eng="dve", aff_eng="pool")

            def q_mm(t, h):
                r = p2req[t]
                wq, wq_b = W.use(r["q"][h // 2])
                wq3 = kv(wq, 256)
                pq = 4 + h % 2
                co = (h % 2) * 128
                hh, hh_b = hq[t % 2], hq_b[t % 2]

                def mmq(e):
                    for k in range(8):
                        ins = e.matmul(ps[pq][:, 0:512], wq3[:, k, co:co + 128], hh[:, k, :], start=(k == 0), stop=(k == 7))
                    return ins
                P.op("pe", mmq, reads=[wq_b] + hh_b, writes=[ps_b[pq]])
                if h % 2 == 1:
                    W.done(r["q"][h // 2])
                return head_post(ps[pq][:, 0:512], ps_b[pq], 512, V_QN, qt[t % 2][:, h, :], qt_b[t % 2][h], t * 512, 3, 6)

            def attention(t, side):
                Q, Q_b = qt[t % 2], qt_b[t % 2]
                for h in range(8):
                    kvh = h // 4
                    po = h % 2
                    SB = [2, 4, 5]

                    def s_op(s, h=h, kvh=kvh):
                        pb = SB[s % 3]
                        kb = kT_b[kvh][0] if s < 2 else kT_b[kvh][1 + (s - 2) // 4]
                        P.op("pe", lambda e: e.matmul(ps[pb][:, 0:512], kT[:, kvh, s * 128:(s + 1) * 128], Q[:, h, :],
                                                      start=True, stop=True),
                             reads=[kb, Q_b[h]], writes=[ps_b[pb]])

                    def e_op(s):
                        pb = SB[s % 3]
                        pi = s % 4
                        P.op("act", lambda e: e.activation(out=PT[pi], in_=ps[pb][:, 0:512], func=AF.Exp, scale=ATTN_SCALE),
                             reads=[ps_b[pb]], writes=[PT_b[pi]])

                    pd = 3 if h % 2 == 0 else 6

                    def pv_op(s, kvh=kvh, po=po, pd=pd):
                        pi = s % 4
                        role = DEN_ROLE[s]
                        if role == "P":
                            def f(e):
                                e.matmul(ps[po][:, 0:512], Vt[:, s, kvh * 128:(kvh + 1) * 128], PT[pi], start=(s == 0), stop=(s == 17))
                                return e.matmul(ps[pd][:, 0:512], ones_bf, PT[pi], start=(s == 0), stop=False)
                            P.op("pe", f, reads=[V_b[s], PT_b[pi], cbf_b], writes=[ps_b[po], ps_b[pd]])
                            return
                        P.op("pe", lambda e: e.matmul(ps[po][:, 0:512], Vt[:, s, kvh * 128:(kvh + 1) * 128], PT[pi],
                                                      start=(s == 0), stop=(s == 17)),
                             reads=[V_b[s], PT_b[pi]], writes=[ps_b[po]])
                        if role == "D":
                            if s == DEN_ROLE.index("D"):
                                P.op("dve", lambda e: e.tensor_copy(out=dacc, in_=PT[pi]), reads=[PT_b[pi]], writes=[dacc_b])
                            else:
                                P.op("dve", lambda e: e.tensor_tensor(out=dacc, in0=dacc, in1=PT[pi], op=ALU.add),
                                     reads=[PT_b[pi], dacc_b], writes=[dacc_b])
                        elif s == 1:
                            pass
                        elif s == 2:
                            P.op("pool", lambda e: e.tensor_tensor(out=daccB, in0=PT[1], in1=PT[2], op=ALU.add),
                                 reads=[PT_b[1], PT_b[2]], writes=[daccB_b])
                        else:
                            P.op("pool", lambda e: e.tensor_tensor(out=daccB, in0=daccB, in1=PT[pi], op=ALU.add),
                                 reads=[PT_b[pi], daccB_b], writes=[daccB_b])
                    s_op(0); s_op(1)
                    for s in range(18):
                        e_op(s)
                        if s + 2 < 18:
                            s_op(s + 2)
                        pv_op(s)
                        if side:
                            side.pop(0)()
                    if "G" in DEN_ROLE:
                        P.op("dve", lambda e: e.tensor_tensor(out=dacc, in0=dacc, in1=daccB, op=ALU.add),
                             reads=[dacc_b, daccB_b], writes=[dacc_b])
                    P.op("pe", lambda e, pd=pd: e.matmul(ps[pd][:, 0:512], ones_f[:], dacc, start=False, stop=True),
                         reads=[dacc_b, onesf_b], writes=[ps_b[pd]])
                    if USE_APPROX_RECIP:
                        P.op("dve", lambda e, pd=pd: e.reciprocal_approx_accurate(tmpf[:, 6, :], ps[pd][:, 0:512], tmpf[:, 5, :]),
                             reads=[ps_b[pd]], writes=[tmpf_b[6], tmpf_b[5]])
                    else:
                        P.op("dve", lambda e, pd=pd: e.reciprocal(out=tmpf[:, 6, :], in_=ps[pd][:, 0:512]),
                             reads=[ps_b[pd]], writes=[tmpf_b[6]])
                    P.op("dve", lambda e, h=h, po=po: e.tensor_tensor(out=Q[:, h, :], in0=ps[po][:, 0:512], in1=tmpf[:, 6, :],
                                                                       op=ALU.mult),
                         reads=[ps_b[po], tmpf_b[6]], writes=[Q_b[h]])
                while side:
                    side.pop(0)()

            def merge_step(t, m):
                r = p2req[t]
                Y, Y_b = qt[t % 2], qt_b[t % 2]
                hh, hh_b = hq[t % 2], hq_b[t % 2]
                hab, hga, hfb, hgf = r["mid"][m // 2]
                (wab, wab_b), (wga, wga_b), (wfb, wfb_b), (wgf, wgf_b) = W.use(hab), W.use(hga), W.use(hfb), W.use(hgf)
                wab3, wga3, wfb3, wgf3 = kv(wab, 256), kv(wga, 256), kv(wfb, 256), kv(wgf, 256)
                c0 = (m % 2) * 128
                P.op("pe", lambda e: mm8(e, 0, wab3, c0, lambda k: Y[:, k, :], 8), reads=[wab_b] + Y_b, writes=[ps_b[0]])
                P.op("pe", lambda e: mm8(e, 1, wga3, c0, lambda k: hh[:, k, :], 8), reads=[wga_b] + hh_b, writes=[ps_b[1]])
                P.op("pe", lambda e: mm8(e, 2, wfb3, c0, lambda k: yfour[:, k, t * 512:(t + 1) * 512], 4),
                     reads=[wfb_b] + [yf_b[g][t] for g in range(4)], writes=[ps_b[2]])
                P.op("pe", lambda e: mm8(e, 7, wgf3, c0, lambda k: hh[:, k, :], 8), reads=[wgf_b] + hh_b, writes=[ps_b[7]])
                P.op("act", lambda e: e.activation(out=tmpf[:, 4, :], in_=ps[1][:, 0:512], func=AF.Sigmoid),
                     reads=[ps_b[1]], writes=[tmpf_b[4]])
                P.op("act", lambda e: e.activation(out=tmpf[:, 5, :], in_=ps[7][:, 0:512], func=AF.Sigmoid),
                     reads=[ps_b[7]], writes=[tmpf_b[5]])
                P.op("dve", lambda e: e.tensor_tensor(out=tmpf[:, 4, :], in0=ps[0][:, 0:512], in1=tmpf[:, 4, :], op=ALU.mult),
                     reads=[ps_b[0], tmpf_b[4]], writes=[tmpf_b[4]])
                P.op("dve", lambda e: e.tensor_tensor(out=tmpf[:, 5, :], in0=ps[2][:, 0:512], in1=tmpf[:, 5, :], op=ALU.mult),
                     reads=[ps_b[2], tmpf_b[5]], writes=[tmpf_b[5]])
                P.op("dve", lambda e: e.tensor_tensor(out=mg[:, m, :], in0=tmpf[:, 4, :], in1=tmpf[:, 5, :], op=ALU.add),
                     reads=[tmpf_b[4], tmpf_b[5]], writes=[mg_b[m]])
                if m % 2 == 1:
                    for hnd in r["mid"][m // 2]:
                        W.done(hnd)

            def outproj(t, early=()):
                r = p2req[t]
                b = lat_block(t)
                early = list(early)
                ek = (len(early) + 7) // 8
                for m in range(NCH):
                    for _ in range(ek):
                        if early:
                            early.pop(0)()
                    wo, wo_b = W.use(r["o"][m // 2])
                    wo3 = kv(wo, 256)
                    c0 = (m % 2) * 128
                    po = m % 2

                    def mmo(e, wo3=wo3, c0=c0, po=po):
                        for k in range(8):
                            ins = e.matmul(ps[po][:, 0:512], wo3[:, k, c0:c0 + 128], mg[:, k, :], start=(k == 0), stop=(k == 7))
                        return ins
                    P.op("pe", mmo, reads=[wo_b] + mg_b, writes=[ps_b[po]])
                    P.op("dve", lambda e, m=m, po=po: e.scalar_tensor_tensor(out=b["xsl"](m), in0=ps[po][:, 0:512],
                                                                              scalar=cols[:, 5, m:m + 1], in1=b["xsl"](m),
                                                                              op0=ALU.mult, op1=ALU.add),
                         reads=[ps_b[po], cols_b[5], b["xb"][m]], writes=[b["xb"][m]])
                    if m % 2 == 1:
                        W.done(r["o"][m // 2])

            for th in mh_steps(0):
                th()
            prev = None
            for h in range(8):
                parts = q_mm(0, h)
                if prev is not None:
                    prev[0](); prev[1]()
                prev = parts
            prev[0](); prev[1]()
            for t in range(4):
                nxt = t + 1 < 4
                attention(t, mh_steps(t + 1) if nxt else [])
                prev = None
                for m in range(NCH):
                    if nxt:
                        parts = q_mm(t + 1, m)
                    merge_step(t, m)
                    if nxt:
                        if prev is not None:
                            prev[0](); prev[1]()
                        prev = parts
                if nxt:
                    prev[0](); prev[1]()
                early = []
                if t == 3 and stage >= 4:
                    ffn_prepare(ffn2_blocks)
                    ffn2_pre.append(ffn_requests(w2i_d, w2o_d, 0))
                    P.wait("pool", [("pe", P.cnt["pe"])])
                    for bi in (0, 1):
                        early += ffn_mh_thunks(ffn2_blocks, bi)
                        ffn2_blocks[bi]["built"] = True
                outproj(t, early)

        st_b = [Buf("store%d" % i) for i in range(8)]

        def emit_output(t):
            stiles = [6, 7, 0, 1, 2, 3] if (t < 3 and stage >= 4) else [6, 7, 0, 1, 2, 3, 4, 5]
            ths = []
            k = 0
            for i in range(t * 4, t * 4 + 4):
                for half in range(2):
                    pb = 6 + half
                    sti = stiles[k % len(stiles)]
                    k += 1

                    def th(i=i, half=half, pb=pb, sti=sti):
                        def tr(e):
                            for j in range(4):
                                m = half * 4 + j
                                ins = e.transpose(ps[pb][:, j * 128:(j + 1) * 128], xT[:, m, i * 128:(i + 1) * 128], ident[:])
                            return ins
                        P.op("pe", tr, reads=[xT_b[m][t] for m in range(half * 4, half * 4 + 4)] + [ident_b], writes=[ps_b[pb]])
                        if half == 0:
                            P.op("act", lambda e: e.copy(out=tmpf[:, sti, :], in_=ps[pb][:]), reads=[ps_b[pb]],
                                 writes=[tmpf_b[sti]])
                        else:
                            P.op("dve", lambda e: e.tensor_copy(out=tmpf[:, sti, :], in_=ps[pb][:]), reads=[ps_b[pb]],
                                 writes=[tmpf_b[sti]])
                        P.dma("sp", lambda e: [e.dma_start(
                            out=out_d[i * 128:(i + 1) * 128, half * 512:(half + 1) * 512], in_=tmpf[:, sti, :])],
                              st_b[sti], reads=[tmpf_b[sti]], writes=[])
                    ths.append(th)
            return ths

        if stage >= 4:
            pre = ffn2_pre[0] if ffn2_pre else ffn_requests(w2i_d, w2o_d, 0)
            P.barrier()
            ffn(w2i_d, w2o_d, ffn2_blocks, pre, after_out=emit_output)
        else:
            P.barrier()
            for t in range(4):
                for th in emit_output(t):
                    th()
        P.wait("sp", [(b.dsem, b.dcnt) for b in st_b if b.dsem is not None])
        P.build(st)
    return nc


_CACHE = {}


def _consts():
    if "c" in _CACHE:
        return _CACHE["c"]
    bf = ml_dtypes.bfloat16
    ident = np.eye(128, dtype=np.float32)
    cb = np.zeros((128, C_END), dtype=np.float32)
    cb[:, C_ONES:C_ONES + 128] = 1.0
    R = np.zeros((128, 128), dtype=np.float32)
    for m in range(64):
        R[m + 64, m] = -1.0
        R[m, m + 64] = 1.0
    cb[:, C_RPERM:C_RPERM + 128] = R
    cc = np.arange(128, dtype=np.float64)
    ang = 2.0 * np.pi * np.outer(cc, cc) / 128.0
    cb[:, C_CSC:C_CSC + 128] = np.cos(ang)
    cb[:, C_CSC + 128:C_CSC + 256] = np.sin(ang)
    rows = SEQ // 64
    row_ids = np.repeat(np.arange(rows, dtype=np.float32), 64)
    col_ids = np.tile(np.arange(64, dtype=np.float32), rows)
    inv_freq = (np.float32(10000.0) ** (-np.arange(0, 64, 2, dtype=np.float32) / np.float32(64))).astype(np.float32)
    angr = np.concatenate([row_ids[:, None] * inv_freq, col_ids[:, None] * inv_freq], axis=-1).astype(np.float32)
    cosT = np.cos(angr).T
    sinT = np.sin(angr).T
    cb[0:64, C_COS:C_COS + 2048] = cosT
    cb[64:128, C_COS:C_COS + 2048] = cosT
    cb[0:64, C_SIN:C_SIN + 2048] = sinT
    cb[64:128, C_SIN:C_SIN + 2048] = sinT
    cbf = cb.astype(bf)
    tabs = np.zeros((32, 128, 1024), dtype=np.float32)
    j = np.arange(128, dtype=np.int64)
    for mb in range(2):
        m = np.arange(mb * 512, (mb + 1) * 512, dtype=np.int64)
        for ti in range(16):
            n = 2 * (128 * (ti % 8) + j) + (1 if ti >= 8 else 0)
            ang = (np.outer(n, m) % SEQ).astype(np.float64) * (2.0 * np.pi / SEQ)
            tabs[mb * 16 + ti, :, 0:512] = np.cos(ang)
            tabs[mb * 16 + ti, :, 512:1024] = -np.sin(ang)
    _CACHE["c"] = (ident, cbf, tabs.astype(bf))
    return _CACHE["c"]


def _colz(v):
    v = np.asarray(v, dtype=np.float32).reshape(-1, 128)
    return np.ascontiguousarray(v.T)


def _pack_k(w, ncols=256):
    K, N = w.shape
    kc = K // 128
    t = w.reshape(kc, 128, N // ncols, ncols)
    return np.ascontiguousarray(t.transpose(2, 1, 0, 3)).reshape(N // ncols, 128, kc * ncols)


def _pack_o(w):
    t = w.reshape(11, 2, 128, 1024)
    return np.ascontiguousarray(t.transpose(0, 2, 1, 3)).reshape(11, 128, 2048)


def kernel(x, c, ctx, c_ctx, w_ada, b_ada, norm_ffn1, w_ffn1_in, w_ffn1_out, norm_mix, w_in, q_norm, k_norm,
           w_attn_branch, w_fourier_branch, w_out, norm_ffn2, w_ffn2_in, w_ffn2_out, _stage=9, _ncores=8):
    f = lambda a: np.ascontiguousarray(np.asarray(a, dtype=np.float32))
    ident, cbf, tabs = _consts()
    key = ("nc", _stage)
    if key not in _CACHE:
        _CACHE[key] = build_program(_stage)
    nc = _CACHE[key]
    x = f(x); ctx = f(ctx); c = f(c)
    shared = dict(ident=ident, cbf=cbf, tabs=tabs, w_ada=_pack_k(f(w_ada)[0]), w_ffn1_in=_pack_k(f(w_ffn1_in)[0]),
                  w_ffn1_out=_pack_o(f(w_ffn1_out)[0]), w_in=_pack_k(f(w_in)[0]), w_ab=_pack_k(f(w_attn_branch)[0]),
                  w_fb=_pack_k(f(w_fourier_branch)[0]), w_o=_pack_k(f(w_out)[0]),
                  w_ffn2_in=_pack_k(f(w_ffn2_in)[0]), w_ffn2_out=_pack_o(f(w_ffn2_out)[0]))
    in_maps = []
    for b in range(_ncores):
        vec = np.zeros((128, NV), dtype=np.float32)
        vec[:, V_C:V_C + 8] = _colz(c[b])
        vec[:, V_CC:V_CC + 8] = _colz(f(c_ctx))
        vec[:, V_BADA:V_BADA + 72] = _colz(f(b_ada)[0])
        vec[:, V_NF1:V_NF1 + 8] = _colz(f(norm_ffn1)[0])
        vec[:, V_NMIX:V_NMIX + 8] = _colz(f(norm_mix)[0])
        vec[:, V_NF2:V_NF2 + 8] = _colz(f(norm_ffn2)[0])
        vec[:, V_QN] = f(q_norm)[0]
        vec[:, V_KN] = f(k_norm)[0]
        m = dict(shared)
        m.update(x=x[b], ctx=ctx[b], vecs=vec)
        in_maps.append(m)
    res = run_bass_kernel_spmd(nc, in_maps, core_ids=list(range(_ncores)))
    return np.stack([np.asarray(r["out"], dtype=np.float32) for r in res.results], axis=0)
```
